# Optimizing a Trainium2 kernel written in Bass

```python
import math
import jax, jax.numpy as jnp
from jax import lax
import numpy as np

D_MODEL = 1024
BATCH = 8
SEQ = 2048
DEPTH = 2
DEC_BATCH = 128
DEC_SEQ = 1
PAST_LEN = 16384
PAGE_SIZE = 128

EPS = 1e-6
D_MIX = D_MODEL
SSD_HEADS = 6
SSD_HEAD_DIM = 64
SSD_WIDTH = SSD_HEADS * SSD_HEAD_DIM
SSD_GROUPS = 2
SSD_HPG = SSD_HEADS // SSD_GROUPS
SSD_STATE = 128
SSD_CONV = 4
SSD_CHUNK = 128
SSD_CONV_CH = SSD_WIDTH + 2 * SSD_GROUPS * SSD_STATE
MLA_HEADS = 6
MLA_NOPE = 64
MLA_ROPE = 32
MLA_QK = MLA_NOPE + MLA_ROPE
MLA_V = 64
MLA_WIDTH = MLA_HEADS * MLA_V
Q_LORA = 256
KV_LORA = 128
ROPE_THETA = 10000.0
ATTN_SCALE = MLA_QK ** -0.5
Q_BLOCK = 128
POOL_WINDOWS = (2, 4, 8, 16)
POOL_GROUPS = 4
POOL_GROUP_DIM = 64
POOL_WIDTH = POOL_GROUPS * POOL_GROUP_DIM
POOL_BUF = 15
D_FF = -(-8 * D_MODEL // (3 * 256)) * 256
IN_SIZES = (SSD_WIDTH, SSD_CONV_CH, SSD_HEADS, Q_LORA, KV_LORA, MLA_ROPE, POOL_WIDTH)
IN_COLS = SSD_WIDTH + SSD_CONV_CH + SSD_HEADS + Q_LORA + KV_LORA + MLA_ROPE + POOL_WIDTH

kernel_name = 'hymba_ssd_mla_pool_decoder_step'


def rmsnorm(x, g):
    xf = x.astype(jnp.float32)
    y = xf * lax.rsqrt(jnp.mean(xf * xf, axis=-1, keepdims=True) + EPS)
    return (y * g.astype(jnp.float32)).astype(x.dtype)


def rope(x, pos):
    half = MLA_ROPE // 2
    inv = 1.0 / (ROPE_THETA ** (jnp.arange(half, dtype=jnp.float32) / half))
    ang = pos.astype(jnp.float32)[:, None] * inv[None, :]
    cos = jnp.cos(ang)[:, None, :]
    sin = jnp.sin(ang)[:, None, :]
    xf = x.astype(jnp.float32)
    x1, x2 = xf[..., :half], xf[..., half:]
    return jnp.concatenate([x1 * cos - x2 * sin, x2 * cos + x1 * sin], -1).astype(x.dtype)


def segsum(a):
    T = a.shape[-1]
    x = jnp.broadcast_to(a[..., :, None], a.shape + (T,))
    x = jnp.where(jnp.tril(jnp.ones((T, T), bool), -1), x, 0.0)
    s = jnp.cumsum(x, axis=-2)
    return jnp.where(jnp.tril(jnp.ones((T, T), bool)), s, -jnp.inf)


def ssd_scan(x, a, B, C, h0):
    b_, L, H, P = x.shape
    N = B.shape[-1]
    T = SSD_CHUNK if L % SSD_CHUNK == 0 else L
    nc = L // T
    xc = x.reshape(b_, nc, T, H, P)
    Bc = B.reshape(b_, nc, T, H, N)
    Cc = C.reshape(b_, nc, T, H, N)
    ac = a.reshape(b_, nc, T, H).transpose(0, 3, 1, 2)
    acum = jnp.cumsum(ac, axis=-1)
    G = jnp.einsum('bclhn,bcshn->bhcls', Cc, Bc) * jnp.exp(segsum(ac))
    y_diag = jnp.einsum('bhcls,bcshp->bclhp', G, xc)
    decay_states = jnp.exp(acum[..., -1:] - acum)
    states = jnp.einsum('bclhn,bhcl,bclhp->bchpn', Bc, decay_states, xc)
    states = jnp.concatenate([h0[:, None], states], axis=1)
    chunk_tot = jnp.pad(acum[..., -1], ((0, 0), (0, 0), (1, 0)))
    states = jnp.einsum('bhzc,bchpn->bzhpn', jnp.exp(segsum(chunk_tot)), states)
    prev, final = states[:, :-1], states[:, -1]
    y_off = jnp.einsum('bclhn,bchpn,bhcl->bclhp', Cc, prev, jnp.exp(acum))
    return (y_diag + y_off).reshape(b_, L, H, P), final


def causal_conv(xbc, buf, w, b):
    L = xbc.shape[1]
    xp = jnp.concatenate([buf, xbc], axis=1)
    y = b
    for k in range(SSD_CONV):
        y = y + xp[:, k:k + L] * w[k]
    return jax.nn.silu(y), xp[:, -(SSD_CONV - 1):]


def ssd_mixer(z, xbc, dt_raw, conv_buf, h0, conv_w, conv_b, dt_bias, a_log, d_skip, g_norm):
    b_, L, _ = z.shape
    f32 = jnp.float32
    xbc, conv_new = causal_conv(xbc, conv_buf, conv_w, conv_b)
    gn = SSD_GROUPS * SSD_STATE
    xs = xbc[..., :SSD_WIDTH].reshape(b_, L, SSD_HEADS, SSD_HEAD_DIM).astype(f32)
    Bm = xbc[..., SSD_WIDTH:SSD_WIDTH + gn].reshape(b_, L, SSD_GROUPS, SSD_STATE)
    Cm = xbc[..., SSD_WIDTH + gn:].reshape(b_, L, SSD_GROUPS, SSD_STATE)
    Bh = jnp.repeat(Bm, SSD_HPG, axis=2).astype(f32)
    Ch = jnp.repeat(Cm, SSD_HPG, axis=2).astype(f32)
    dt = jax.nn.softplus((dt_raw + dt_bias).astype(f32))
    A = -jnp.exp(a_log.astype(f32))
    y, hT = ssd_scan(xs * dt[..., None], dt * A, Bh, Ch, h0.astype(f32))
    y = y + d_skip.astype(f32)[:, None] * xs
    y = y.reshape(b_, L, SSD_GROUPS, SSD_WIDTH // SSD_GROUPS)
    zg = jax.nn.silu(z.astype(f32)).reshape(b_, L, SSD_GROUPS, SSD_WIDTH // SSD_GROUPS)
    y = rmsnorm(y * zg, g_norm.reshape(SSD_GROUPS, SSD_WIDTH // SSD_GROUPS))
    return y.reshape(b_, L, SSD_WIDTH).astype(z.dtype), hT.astype(h0.dtype), conv_new


def pool_mixer(u, buf, pos, w_pool, scale):
    b_, L, C = u.shape
    f32 = jnp.float32
    up = jnp.concatenate([buf, u], axis=1).astype(f32)
    S = jnp.concatenate([jnp.zeros((b_, 1, C), f32), jnp.cumsum(up, axis=1)], axis=1)
    end = S[:, POOL_BUF + 1:]
    means = []
    for g, w in enumerate(POOL_WINDOWS):
        sl = slice(g * POOL_GROUP_DIM, (g + 1) * POOL_GROUP_DIM)
        start = S[:, POOL_BUF + 1 - w:POOL_BUF + 1 - w + L, sl]
        cnt = jnp.minimum(pos + 1, w).astype(f32)[None, :, None]
        means.append((end[..., sl] - start) / cnt)
    pooled = jnp.concatenate(means, axis=-1) - u.astype(f32)
    out = jnp.einsum('blgc,gcd->blgd', pooled.reshape(b_, L, POOL_GROUPS, POOL_GROUP_DIM), w_pool.astype(f32))
    out = out.reshape(b_, L, C) * scale.astype(f32)
    return out.astype(u.dtype), up[:, -POOL_BUF:].astype(u.dtype)


def mla_keys(lat, kpe_r, w_k_up, g_k):
    k_nope = jnp.einsum('...r,rhd->...hd', lat, w_k_up)
    k_pe = jnp.broadcast_to(kpe_r[..., None, :], k_nope.shape[:-1] + (MLA_ROPE,))
    return rmsnorm(jnp.concatenate([k_nope, k_pe], axis=-1), g_k)


def mla_attend_prompt(q, lat, kpe_r, w_k_up, w_v_up, g_k):
    b_, L = q.shape[0], q.shape[1]
    k = mla_keys(lat, kpe_r, w_k_up, g_k)
    v = jnp.einsum('blr,rhd->blhd', lat, w_v_up)
    nb = L // Q_BLOCK
    qb = q.reshape(b_, nb, Q_BLOCK, MLA_HEADS, MLA_QK).transpose(1, 0, 2, 3, 4)
    kpos = jnp.arange(L)

    def block(args):
        qi, i = args
        s = jnp.einsum('bqhd,bkhd->bhqk', qi, k, preferred_element_type=jnp.float32) * ATTN_SCALE
        qpos = i * Q_BLOCK + jnp.arange(Q_BLOCK)
        s = jnp.where(kpos[None, :] <= qpos[:, None], s, -jnp.inf)
        p = jax.nn.softmax(s, axis=-1).astype(v.dtype)
        return jnp.einsum('bhqk,bkhd->bqhd', p, v)

    o = lax.map(block, (qb, jnp.arange(nb)))
    return o.transpose(1, 0, 2, 3, 4).reshape(b_, L, MLA_WIDTH)


def mla_attend_sample(q, lat_new, kpe_new, cache_lat, cache_kpe, page_table, li, w_k_up, w_v_up, g_k):
    b_, T = q.shape[0], q.shape[1]
    past = page_table.shape[1] * PAGE_SIZE
    kpos = jnp.arange(past + T)
    qpos = past + jnp.arange(T)
    mask = kpos[None, :] <= qpos[:, None]

    def one(args):
        pt, qs, ln, kn = args
        lat = jnp.concatenate([cache_lat[li, pt].reshape(past, KV_LORA), ln], axis=0)
        kp = jnp.concatenate([cache_kpe[li, pt].reshape(past, MLA_ROPE), kn], axis=0)
        k = mla_keys(lat, kp, w_k_up, g_k)
        s = jnp.einsum('thd,shd->hts', qs, k, preferred_element_type=jnp.float32) * ATTN_SCALE
        s = jnp.where(mask[None], s, -jnp.inf)
        p = jax.nn.softmax(s, axis=-1).astype(lat.dtype)
        o_lat = jnp.einsum('hts,sr->thr', p, lat)
        return jnp.einsum('thr,rhd->thd', o_lat, w_v_up)

    o = lax.map(one, (page_table, q, lat_new, kpe_new))
    return o.reshape(b_, T, MLA_WIDTH)


def trunk_layer(x, c, pos0, lp, h0, conv_buf, pool_buf, attend):
    b_, L, _ = x.shape
    pos = pos0 + jnp.arange(L, dtype=jnp.int32)
    mod = (jax.nn.silu(c) @ lp['w_ada'] + lp['b_ada']).reshape(b_, 6, D_MODEL)[:, :, None, :]
    shift1, scale1, gate1, shift2, scale2, gate2 = (mod[:, i] for i in range(6))
    h = rmsnorm(x, lp['g_norm1']) * (1.0 + scale1) + shift1
    proj = h @ lp['w_in']
    offs = np.cumsum(IN_SIZES)[:-1].tolist()
    z, xbc, dt_raw, cq, ckv, kpe, u = jnp.split(proj, offs, axis=-1)
    ssd_out, hT, conv_new = ssd_mixer(z, xbc, dt_raw, conv_buf, h0, lp['conv_w'], lp['conv_b'],
                                      lp['dt_bias'], lp['a_log'], lp['d_skip'], lp['g_ssd_norm'])
    q = jnp.einsum('blr,rhd->blhd', rmsnorm(cq, lp['g_q_lora']), lp['w_q_up'])
    q = jnp.concatenate([q[..., :MLA_NOPE], rope(q[..., MLA_NOPE:], pos)], axis=-1)
    q = rmsnorm(q, lp['g_qk_q'])
    lat = rmsnorm(ckv, lp['g_kv_lora'])
    kpe_r = rope(kpe[:, :, None, :], pos)[:, :, 0, :]
    mla_out = rmsnorm(attend(q, lat, kpe_r), lp['g_mla_out'])
    pool_out, pool_new = pool_mixer(u, pool_buf, pos, lp['w_pool'], lp['pool_scale'])
    mix = jnp.concatenate([ssd_out, mla_out, pool_out], axis=-1) @ lp['w_out']
    x = x + gate1 * mix
    h2 = rmsnorm(x, lp['g_norm2']) * (1.0 + scale2) + shift2
    ffn = (jax.nn.silu(h2 @ lp['w_gate']) * (h2 @ lp['w_up'])) @ lp['w_down']
    x = x + gate2 * ffn
    return x, lat, kpe_r, hT, conv_new, pool_new


def setup_inputs(seed: int = 0) -> dict:
    key = jax.random.key(seed)
    keys = iter(jax.random.split(key, 48))
    f32 = jnp.float32

    def normal(shape, scale=1.0):
        return jax.random.normal(next(keys), shape, f32) * scale

    def gain(shape):
        return 1.0 + 0.02 * normal(shape)

    n_pages = PAST_LEN // PAGE_SIZE
    n_pool = (DEC_BATCH * n_pages * 5) // 4
    perm = jax.random.permutation(next(keys), n_pool)
    page_table = perm[:DEC_BATCH * n_pages].reshape(DEC_BATCH, n_pages).astype(jnp.int32)
    dt0 = jnp.exp(jax.random.uniform(next(keys), (DEPTH, SSD_HEADS), f32, math.log(1e-3), math.log(1e-1)))
    dt_bias = dt0 + jnp.log(-jnp.expm1(-dt0))
    a_log = jnp.log(jax.random.uniform(next(keys), (DEPTH, SSD_HEADS), f32, 1.0, 16.0))
    return {
        'x_prompt': normal((BATCH, SEQ, D_MODEL)),
        'x_sample': normal((DEC_BATCH, DEC_SEQ, D_MODEL)),
        'cache_kv_latent': normal((DEPTH, n_pool, PAGE_SIZE, KV_LORA)),
        'cache_k_rope': normal((DEPTH, n_pool, PAGE_SIZE, MLA_ROPE)),
        'state_ssm': normal((DEPTH, DEC_BATCH, SSD_HEADS, SSD_HEAD_DIM, SSD_STATE), 0.5),
        'state_conv': normal((DEPTH, DEC_BATCH, SSD_CONV - 1, SSD_CONV_CH)),
        'state_pool': normal((DEPTH, DEC_BATCH, POOL_BUF, POOL_WIDTH)),
        'page_table': page_table,
        'c_prompt': normal((BATCH, D_MODEL)),
        'c_sample': normal((DEC_BATCH, D_MODEL)),
        'w_ada': normal((DEPTH, D_MODEL, 6 * D_MODEL), 0.5 * D_MODEL ** -0.5),
        'b_ada': normal((DEPTH, 6 * D_MODEL), 0.01),
        'g_norm1': gain((DEPTH, D_MODEL)),
        'w_in': normal((DEPTH, D_MODEL, IN_COLS), D_MODEL ** -0.5),
        'conv_w': normal((DEPTH, SSD_CONV, SSD_CONV_CH), SSD_CONV ** -0.5),
        'conv_b': normal((DEPTH, SSD_CONV_CH), 0.01),
        'dt_bias': dt_bias,
        'a_log': a_log,
        'd_skip': gain((DEPTH, SSD_HEADS)),
        'g_ssd_norm': gain((DEPTH, SSD_WIDTH)),
        'g_q_lora': gain((DEPTH, Q_LORA)),
        'w_q_up': normal((DEPTH, Q_LORA, MLA_HEADS, MLA_QK), Q_LORA ** -0.5),
        'g_kv_lora': gain((DEPTH, KV_LORA)),
        'w_k_up': normal((DEPTH, KV_LORA, MLA_HEADS, MLA_NOPE), KV_LORA ** -0.5),
        'w_v_up': normal((DEPTH, KV_LORA, MLA_HEADS, MLA_V), KV_LORA ** -0.5),
        'g_qk_q': gain((DEPTH, MLA_QK)),
        'g_qk_k': gain((DEPTH, MLA_QK)),
        'g_mla_out': gain((DEPTH, MLA_WIDTH)),
        'w_pool': normal((DEPTH, POOL_GROUPS, POOL_GROUP_DIM, POOL_GROUP_DIM), POOL_GROUP_DIM ** -0.5),
        'pool_scale': gain((DEPTH, POOL_WIDTH)),
        'w_out': normal((DEPTH, D_MIX, D_MODEL), D_MIX ** -0.5),
        'g_norm2': gain((DEPTH, D_MODEL)),
        'w_gate': normal((DEPTH, D_MODEL, D_FF), D_MODEL ** -0.5),
        'w_up': normal((DEPTH, D_MODEL, D_FF), D_MODEL ** -0.5),
        'w_down': normal((DEPTH, D_FF, D_MODEL), D_FF ** -0.5),
    }


def reference(x_prompt, x_sample, cache_kv_latent, cache_k_rope, state_ssm, state_conv, state_pool, page_table,
              c_prompt, c_sample, w_ada, b_ada, g_norm1, w_in, conv_w, conv_b, dt_bias, a_log, d_skip, g_ssd_norm,
              g_q_lora, w_q_up, g_kv_lora, w_k_up, w_v_up, g_qk_q, g_qk_k, g_mla_out, w_pool, pool_scale,
              w_out, g_norm2, w_gate, w_up, w_down):
    yp, ys = x_prompt, x_sample
    bp = x_prompt.shape[0]
    dtype = x_prompt.dtype
    p_new = [[] for _ in range(5)]
    s_new = [[] for _ in range(5)]
    for li in range(DEPTH):
        lp = dict(w_ada=w_ada[li], b_ada=b_ada[li], g_norm1=g_norm1[li], w_in=w_in[li], conv_w=conv_w[li],
                  conv_b=conv_b[li], dt_bias=dt_bias[li], a_log=a_log[li], d_skip=d_skip[li],
                  g_ssd_norm=g_ssd_norm[li], g_q_lora=g_q_lora[li], w_q_up=w_q_up[li], g_kv_lora=g_kv_lora[li],
                  g_qk_q=g_qk_q[li], g_mla_out=g_mla_out[li], w_pool=w_pool[li], pool_scale=pool_scale[li],
                  w_out=w_out[li], g_norm2=g_norm2[li], w_gate=w_gate[li], w_up=w_up[li], w_down=w_down[li])
        wk, wv, gk = w_k_up[li], w_v_up[li], g_qk_k[li]

        def attend_prompt(q, lat, kp, wk=wk, wv=wv, gk=gk):
            return mla_attend_prompt(q, lat, kp, wk, wv, gk)

        def attend_sample(q, lat, kp, wk=wk, wv=wv, gk=gk, li=li):
            return mla_attend_sample(q, lat, kp, cache_kv_latent, cache_k_rope, page_table, li, wk, wv, gk)

        h0 = jnp.zeros((bp, SSD_HEADS, SSD_HEAD_DIM, SSD_STATE), dtype)
        cb0 = jnp.zeros((bp, SSD_CONV - 1, SSD_CONV_CH), dtype)
        pb0 = jnp.zeros((bp, POOL_BUF, POOL_WIDTH), dtype)
        yp, *pst = trunk_layer(yp, c_prompt, 0, lp, h0, cb0, pb0, attend_prompt)
        ys, *sst = trunk_layer(ys, c_sample, PAST_LEN, lp, state_ssm[li], state_conv[li], state_pool[li],
                               attend_sample)
        for lst, v in zip(p_new, pst):
            lst.append(v)
        for lst, v in zip(s_new, sst):
            lst.append(v)
    p_lat, p_kpe, p_ssm, p_conv, p_pool = (jnp.stack(v, axis=0) for v in p_new)
    s_lat, s_kpe, s_ssm, s_conv, s_pool = (jnp.stack(v, axis=0) for v in s_new)
    return (yp, ys, p_lat, p_kpe, p_ssm, p_conv, p_pool, s_lat, s_kpe, s_ssm, s_conv, s_pool)
```

```python
import math
from contextlib import ExitStack

import numpy as np
import concourse.bass as bass
import concourse.mybir as mybir
from concourse.bass_utils import run_bass_kernel_spmd

F32 = mybir.dt.float32
BF16 = mybir.dt.bfloat16
I32 = mybir.dt.int32
AF = mybir.ActivationFunctionType
ALU = mybir.AluOpType
AX = mybir.AxisListType

NCORES = 8
D = 1024
SEQ = 2048
NBLK = 4
NB = 512
NS = 16
DEPTH = 2
EPS = 1e-6
DFF = 2816
NFC = DFF // 128
ATTN_SCALE = 96 ** -0.5
PAST = 16384
NPG = 128
C_Z, C_XS, C_B, C_C, C_DT, C_CQ, C_CKV, C_KPE, C_U = 0, 384, 768, 1024, 1280, 1286, 1542, 1670, 1702
IN_COLS = 1958

ENGS = ("pe", "act", "dve", "pool", "sp")


class Tok:
    __slots__ = ("name", "w", "rs")

    def __init__(self, name):
        self.name = name
        self.w = None
        self.rs = []


class Buf:
    def __init__(self, ap, name):
        self.ap = ap
        self.tok = Tok(name)

    def __getitem__(self, k):
        return self.ap[k]


class Sched:
    SAME = True

    def __init__(self):
        self.ins = {e: [] for e in ENGS}
        self.dma_slots = {}

    def op(self, eng, fn, R=(), W=(), slot=None):
        deps = []
        for b in R:
            t = b.tok
            if t.w is not None:
                deps.append(t.w)
        for b in W:
            t = b.tok
            if t.w is not None:
                deps.append(t.w)
            deps.extend(t.rs)
        idx = len(self.ins[eng])
        rec = dict(fn=fn, deps=deps, signal=False, dma=None)
        if slot is not None:
            s = self.dma_slots.setdefault(slot, [len(self.dma_slots), 0])
            if s[1] > 0:
                deps.append(("D", s[0], s[1]))
            s[1] += 16
            rec["dma"] = (s[0], s[1])
            ev = ("D", s[0], s[1])
        else:
            ev = ("E", eng, idx)
        self.ins[eng].append(rec)
        for b in R:
            b.tok.rs.append(ev)
        for b in W:
            b.tok.w = ev
            b.tok.rs = []
        return ev

    def finalize(self):
        for e in ENGS:
            for i, rec in enumerate(self.ins[e]):
                for d in rec["deps"]:
                    if d[0] == "E" and (d[1] != e or (self.SAME and d[2] < i)):
                        self.ins[d[1]][d[2]]["signal"] = True
        self.val = {}
        for e in ENGS:
            c = 0
            for i, rec in enumerate(self.ins[e]):
                if rec["signal"]:
                    c += 1
                    self.val[(e, i)] = c

    def run_engine(self, e, engobj, esems, dsems):
        seen = {}
        for i, rec in enumerate(self.ins[e]):
            need = {}
            for d in rec["deps"]:
                if d[0] == "E":
                    if d[1] == e and (not self.SAME or d[2] >= i):
                        continue
                    key = ("E", d[1])
                    v = self.val[(d[1], d[2])]
                else:
                    key = ("D", d[1])
                    v = d[2]
                if seen.get(key, 0) >= v:
                    continue
                if need.get(key, 0) < v:
                    need[key] = v
            for key, v in need.items():
                sem = esems[key[1]] if key[0] == "E" else dsems[key[1]]
                engobj.wait_ge(sem, v)
                seen[key] = v
            if rec["fn"] is None:
                continue
            r = rec["fn"](engobj)
            if rec["dma"] is not None:
                r.then_inc(dsems[rec["dma"][0]], 16)
            elif rec["signal"]:
                r.then_inc(esems[e], 1)


class Builder:
    def __init__(self, n_pool, dbg=False):
        self.n_pool = n_pool
        self.nc = bass.Bass("TRN2", target_bir_lowering=False)
        self.S = Sched()
        self.es = ExitStack()
        self.out_events = []
        self.nbuf = 0
        self.dbg_on = dbg
        self.dbg_list = []

    def dram_in(self, name, shape, dt=F32):
        return self.nc.dram_tensor(name, list(shape), dt, kind="ExternalInput").ap()

    def dram_out(self, name, shape, dt=F32):
        return self.nc.dram_tensor(name, list(shape), dt, kind="ExternalOutput").ap()

    def sb(self, name, shape, dt=F32):
        t = self.es.enter_context(self.nc.sbuf_tensor(name, list(shape), dt))
        return Buf(t[:], name)

    def arena_init(self, nbytes):
        self.ar_words = nbytes // 4
        t = self.es.enter_context(self.nc.sbuf_tensor("arena", [128, self.ar_words], F32))
        self.ar_t = t
        self.ar_ptr = 0
        self.ar_hist = []
        self.ar_peak = 0

    def amark(self):
        return self.ar_ptr

    def arelease(self, mark):
        self.ar_ptr = mark

    def aa(self, name, parts, fshape, dt=F32):
        n = 1
        for s in fshape:
            n *= s
        esz = 4 if dt in (F32, I32) else 2
        words = (n * esz + 3) // 4
        lo = self.ar_ptr
        hi = lo + words
        assert hi <= self.ar_words, "arena overflow %s need %d have %d" % (name, hi, self.ar_words)
        self.ar_ptr = hi
        self.ar_peak = max(self.ar_peak, hi)
        ap = self.ar_t[0:parts, lo:hi]
        if dt != F32:
            ap = ap.bitcast(dt)
            if esz == 2:
                ap = ap[:, 0:n]
        if len(fshape) == 2:
            ap = ap.rearrange("p (a b) -> p a b", a=fshape[0])
        elif len(fshape) == 3:
            ap = ap.rearrange("p (a b c) -> p a b c", a=fshape[0], b=fshape[1])
        self.nbuf += 1
        b = Buf(ap, "%s#%d" % (name, self.nbuf))
        keep = []
        for (l2, h2, ob) in self.ar_hist:
            if l2 < hi and lo < h2:
                if ob.tok.w is not None:
                    b.tok.rs.append(ob.tok.w)
                b.tok.rs.extend(ob.tok.rs)
                if not (lo <= l2 and h2 <= hi):
                    keep.append((l2, h2, ob))
            else:
                keep.append((l2, h2, ob))
        keep.append((lo, hi, b))
        self.ar_hist = keep
        return b

    def op(self, eng, fn, R=(), W=(), slot=None):
        Rn = [b for b in R if not getattr(b, "excl", False)]
        Wn = list(W) + [b for b in R if getattr(b, "excl", False) and b not in W]
        return self.S.op(eng, fn, Rn, Wn, slot)

    def mm(self, out, lhsT, rhs, R, W, start=True, stop=True, sgc=False):
        return self.op("pe", lambda e: e.matmul(out, lhsT=lhsT, rhs=rhs, start=start, stop=stop, skip_group_check=sgc), R, W)

    def tr(self, out, in_, ident, R, W):
        return self.op("pe", lambda e: e.transpose(out=out, in_=in_, identity=ident), R, W)

    def act(self, out, in_, func, R, W, scale=None, bias=None, accum=None):
        kw = {}
        if scale is not None:
            kw["scale"] = scale
        if bias is not None:
            kw["bias"] = bias
        if accum is not None:
            kw["accum_out"] = accum
        return self.op("act", lambda e: e.activation(out=out, in_=in_, func=func, **kw), R, W)

    def tt(self, eng, out, in0, in1, op, R, W):
        return self.op(eng, lambda e: e.tensor_tensor(out=out, in0=in0, in1=in1, op=op), R, W)

    def ts(self, eng, out, in0, s1, s2, op0, op1, R, W):
        if op1 is None:
            return self.op(eng, lambda e: e.tensor_scalar(out=out, in0=in0, scalar1=s1, scalar2=None, op0=op0), R, W)
        return self.op(eng, lambda e: e.tensor_scalar(out=out, in0=in0, scalar1=s1, scalar2=s2, op0=op0, op1=op1), R, W)

    def stt(self, out, in0, scalar, in1, op0, op1, R, W):
        return self.op("dve", lambda e: e.scalar_tensor_tensor(out=out, in0=in0, scalar=scalar, in1=in1, op0=op0, op1=op1), R, W)

    def red(self, out, in_, op, R, W):
        return self.op("dve", lambda e: e.tensor_reduce(out=out, in_=in_, axis=AX.X, op=op), R, W)

    def cp(self, eng, out, in_, R, W):
        if eng == "act":
            return self.op("act", lambda e: e.copy(out=out, in_=in_), R, W)
        return self.op(eng, lambda e: e.tensor_copy(out=out, in_=in_), R, W)

    def memset(self, eng, ap, val, W):
        return self.op(eng, lambda e: e.memset(ap, val), (), W)

    def recip(self, out, in_, R, W):
        return self.op("dve", lambda e: e.reciprocal(out=out, in_=in_), R, W)

    def dma(self, eng, out, in_, R, W, slot, slow=False):
        if slow:
            return self.op(eng, lambda e: e.dma_start(out=out, in_=in_, allow_slow_non_contiguous=True), R, W, slot)
        return self.op(eng, lambda e: e.dma_start(out=out, in_=in_), R, W, slot)

    def store(self, out, in_, R, slot, slow=False):
        ev = self.dma("sp", out, in_, R, (), slot, slow)
        self.out_events.append(ev)
        return ev

    def rsqrt(self, out, in_, scale, R, W):
        P = out.shape[0]
        self.act(out, in_, AF.Ln, list(R) + [self.eps], W, scale=scale, bias=self.eps.ap[0:P, 0:1])
        self.act(out, out, AF.Exp, W, W, scale=-0.5)

    def psum_init(self):
        t = self.es.enter_context(self.nc.psum_tensor("ps", [128, 4096], F32))
        self.ps_t = t
        self.PB = [Buf(t[:, b * 512:(b + 1) * 512], "psb%d" % b) for b in range(8)]
        for b in self.PB:
            b.excl = True
        self.rot = {"d": 0, "s": 0, "t": 0}

    def bank(self, kind):
        if kind == "d":
            b = self.rot["d"]
            self.rot["d"] = (b + 1) % 4
            return self.PB[b]
        if kind == "s":
            b = self.rot["s"]
            self.rot["s"] = (b + 1) % 2
            return self.PB[4 + b]
        b = self.rot["t"]
        self.rot["t"] = (b + 1) % 2
        return self.PB[6 + b]

    def wbuf_init(self, n=3):
        self.wb = [self.sb("wbuf%d" % i, [128, 4096], BF16) for i in range(n)]
        self.wi = 0

    def wload(self, dram_view, parts, kc, ncols):
        b = self.wb[self.wi]
        slot = "wbuf%d" % self.wi
        self.wi = (self.wi + 1) % len(self.wb)
        v = b.ap[0:parts, 0:kc * ncols].rearrange("p (k n) -> p k n", k=kc)
        self.dma("pool", v, dram_view, (), [b], slot)
        return b, v

    def declare(self):
        npool = self.n_pool
        I = {}
        I["xp"] = self.dram_in("xp", [SEQ, D])
        I["xs"] = self.dram_in("xs", [NS, D])
        I["cl"] = self.dram_in("cl", [DEPTH, npool, 128, 128])
        I["ck"] = self.dram_in("ck", [DEPTH, npool, 128, 32])
        I["sssm"] = self.dram_in("sssm", [DEPTH, NS, 6, 64, 128])
        I["sconv"] = self.dram_in("sconv", [DEPTH, NS, 3, 896])
        I["spool"] = self.dram_in("spool", [DEPTH, NS, 15, 256])
        I["pt"] = self.dram_in("pt", [NS, NPG], I32)
        I["cp"] = self.dram_in("cp", [1, D])
        I["cs"] = self.dram_in("cs", [NS, D])
        I["w_ada"] = self.dram_in("w_ada", [DEPTH, D, 6 * D])
        I["b_ada"] = self.dram_in("b_ada", [DEPTH, 6 * D])
        I["g_norm1"] = self.dram_in("g_norm1", [DEPTH, D])
        I["w_in"] = self.dram_in("w_in", [DEPTH, D, IN_COLS])
        I["conv_w"] = self.dram_in("conv_w", [DEPTH, 4, 896])
        I["conv_b"] = self.dram_in("conv_b", [DEPTH, 896])
        I["dt_bias"] = self.dram_in("dt_bias", [DEPTH, 6])
        I["a_log"] = self.dram_in("a_log", [DEPTH, 6])
        I["d_skip"] = self.dram_in("d_skip", [DEPTH, 6])
        I["g_ssd_norm"] = self.dram_in("g_ssd_norm", [DEPTH, 384])
        I["g_q_lora"] = self.dram_in("g_q_lora", [DEPTH, 256])
        I["w_q_up"] = self.dram_in("w_q_up", [DEPTH, 256, 6, 96])
        I["g_kv_lora"] = self.dram_in("g_kv_lora", [DEPTH, 128])
        I["w_k_up"] = self.dram_in("w_k_up", [DEPTH, 128, 6, 64])
        I["w_v_up"] = self.dram_in("w_v_up", [DEPTH, 128, 6, 64])
        I["g_qk_q"] = self.dram_in("g_qk_q", [DEPTH, 96])
        I["g_qk_k"] = self.dram_in("g_qk_k", [DEPTH, 96])
        I["g_mla_out"] = self.dram_in("g_mla_out", [DEPTH, 384])
        I["w_pool"] = self.dram_in("w_pool", [DEPTH, 4, 64, 64])
        I["pool_scale"] = self.dram_in("pool_scale", [DEPTH, 256])
        I["w_out"] = self.dram_in("w_out", [DEPTH, D, D])
        I["g_norm2"] = self.dram_in("g_norm2", [DEPTH, D])
        I["w_gate"] = self.dram_in("w_gate", [DEPTH, D, DFF])
        I["w_up"] = self.dram_in("w_up", [DEPTH, D, DFF])
        I["w_down"] = self.dram_in("w_down", [DEPTH, DFF, D])
        I["kconst"] = self.dram_in("kconst", [128, 8])
        self.I = I
        O = {}
        O["y_p"] = self.dram_out("y_p", [SEQ, D])
        O["y_s"] = self.dram_out("y_s", [NS, D])
        O["p_lat"] = self.dram_out("p_lat", [DEPTH, SEQ, 128])
        O["p_kpe"] = self.dram_out("p_kpe", [DEPTH, SEQ, 32])
        O["p_ssm"] = self.dram_out("p_ssm", [DEPTH, 6, 64, 128])
        O["p_conv"] = self.dram_out("p_conv", [DEPTH, 3, 896])
        O["p_pool"] = self.dram_out("p_pool", [DEPTH, 15, 256])
        O["s_lat"] = self.dram_out("s_lat", [DEPTH, NS, 128])
        O["s_kpe"] = self.dram_out("s_kpe", [DEPTH, NS, 32])
        O["s_ssm"] = self.dram_out("s_ssm", [DEPTH, NS, 6, 64, 128])
        O["s_conv"] = self.dram_out("s_conv", [DEPTH, NS, 3, 896])
        O["s_pool"] = self.dram_out("s_pool", [DEPTH, NS, 15, 256])
        self.O = O

    class _Stop(Exception):
        pass

    def ckpt(self, k):
        if getattr(self, "stop_at", None) == k:
            raise Builder._Stop()

    def setup_slot(self, eng):
        self._ss = getattr(self, "_ss", 0) + 1
        return "%s_ld%d" % (eng, self._ss % 4)

    def dbg(self, name, buf, ap, shape):
        if not self.dbg_on:
            return
        o = self.dram_out("dbg_" + name, list(shape))
        self.dbg_list.append("dbg_" + name)
        self.store(o, ap, [buf], "dbgst_" + name)

    def setup(self):
        I = self.I
        self.psum_init()
        self.wbuf_init(3)
        sb = self.sb
        self.eps = sb("eps", [128, 1])
        self.memset("pool", self.eps.ap, EPS, [self.eps])
        self.identf = sb("identf", [128, 128])
        self.identb = sb("identb", [128, 128], BF16)
        self.memset("pool", self.identf.ap, 0.0, [self.identf])
        self.op("pool", lambda e: e.affine_select(out=self.identf.ap, in_=self.identf.ap, pattern=[[-1, 128]], compare_op=ALU.not_equal,
                                                  fill=1.0, base=0, channel_multiplier=1), [self.identf], [self.identf])
        self.cp("pool", self.identb.ap, self.identf.ap, [self.identf], [self.identb])
        self.trif = sb("trif", [128, 128])
        self.trib = sb("trib", [128, 128], BF16)
        self.memset("pool", self.trif.ap, 1.0, [self.trif])
        self.op("pool", lambda e: e.affine_select(out=self.trif.ap, in_=self.trif.ap, pattern=[[1, 128]], compare_op=ALU.is_ge,
                                                  fill=0.0, base=0, channel_multiplier=-1), [self.trif], [self.trif])
        self.cp("pool", self.trib.ap, self.trif.ap, [self.trif], [self.trib])
        self.onesf = sb("onesf", [128, 128])
        self.onesb = sb("onesb", [128, 128], BF16)
        self.memset("pool", self.onesf.ap, 1.0, [self.onesf])
        self.memset("pool", self.onesb.ap, 1.0, [self.onesb])
        self.ckpt(1)
        self.bmask = sb("bmask", [128, 4])
        self.memset("pool", self.bmask.ap, 0.0, [self.bmask])
        for i in range(4):
            self.memset("pool", self.bmask.ap[32 * i:32 * i + 32, i:i + 1], 1.0, [self.bmask])
        self.ckpt(2)
        self.kc = sb("kc", [128, 8])
        self.dma("sp", self.kc.ap, I["kconst"], (), [self.kc], "kc")
        self.rct = sb("rct", [128, 2, 16])
        tmpi = sb("tmpi", [128, 16])
        self.op("pool", lambda e: e.iota(tmpi.ap, pattern=[[1, 16]], base=1, channel_multiplier=0, allow_small_or_imprecise_dtypes=True), (), [tmpi])
        for c in range(2):
            self.ts("dve", self.rct.ap[:, c, :], tmpi.ap, self.kc.ap[:, 2 + c:3 + c], None, ALU.min, None, [tmpi, self.kc], [self.rct])
        self.recip(self.rct.ap, self.rct.ap, [self.rct], [self.rct])
        self.ckpt(3)
        ptT = sb("ptT", [128, NS], I32)
        self.dma("sp", ptT.ap, I["pt"].rearrange("s j -> j s"), (), [ptT], "ptT", slow=True)
        ptf = sb("ptf", [128, NS])
        self.cp("dve", ptf.ap, ptT.ap, [ptT], [ptf])
        self.idx8 = sb("idx8", [128, DEPTH, NS, 8], I32)
        for l in range(DEPTH):
            for c in range(8):
                self.ts("dve", self.idx8.ap[:, l, :, c], ptf.ap, 8.0, float(c + 8 * l * self.n_pool), ALU.mult, ALU.add, [ptf], [self.idx8])

        self.ckpt(4)
        L = []
        for l in range(DEPTH):
            P = {}

            def ld(name, shape, src, slow=True, eng="sp", dt=F32):
                b = sb("%s_%d" % (name, l), shape, dt)
                self.dma(eng, b.ap, src, (), [b], self.setup_slot(eng), slow=slow)
                return b
            P["g1"] = ld("g1", [128, 8], I["g_norm1"][l].rearrange("(c p) -> p c", p=128))
            P["g2"] = ld("g2", [128, 8], I["g_norm2"][l].rearrange("(c p) -> p c", p=128))
            P["bada"] = ld("bada", [128, 48], I["b_ada"][l].rearrange("(c p) -> p c", p=128))
            P["cwx"] = sb("cwx_%d" % l, [64, 6, 4])
            P["cwbc"] = sb("cwbc_%d" % l, [128, 4, 4])
            for k in range(4):
                self.dma("sp", P["cwx"].ap[:, :, k], I["conv_w"][l][k, 0:384].rearrange("(h p) -> p h", p=64), (), [P["cwx"]], self.setup_slot("sp"), slow=True)
                self.dma("sp", P["cwbc"].ap[:, :, k], I["conv_w"][l][k, 384:896].rearrange("(c p) -> p c", p=128), (), [P["cwbc"]], self.setup_slot("sp"), slow=True)
            P["cbx"] = ld("cbx", [64, 6], I["conv_b"][l][0:384].rearrange("(h p) -> p h", p=64))
            P["cbbc"] = ld("cbbc", [128, 4], I["conv_b"][l][384:896].rearrange("(c p) -> p c", p=128))
            P["gssd"] = ld("gssd", [64, 6], I["g_ssd_norm"][l].rearrange("(h p) -> p h", p=64))
            P["dskip"] = ld("dskip", [64, 6], I["d_skip"][l:l + 1, :].partition_broadcast(64), slow=False)
            P["dtb"] = ld("dtb", [128, 6], I["dt_bias"][l:l + 1, :].partition_broadcast(128), slow=False)
            alog = ld("alog", [128, 6], I["a_log"][l:l + 1, :].partition_broadcast(128), slow=False)
            P["negA"] = sb("negA_%d" % l, [128, 6])
            self.act(P["negA"].ap, alog.ap, AF.Exp, [alog], [P["negA"]])
            self.ts("dve", P["negA"].ap, P["negA"].ap, -1.0, None, ALU.mult, None, [P["negA"]], [P["negA"]])
            self.ckpt(5)
            P["gql"] = ld("gql", [128, 2], I["g_q_lora"][l].rearrange("(c p) -> p c", p=128))
            P["gkv"] = ld("gkv", [128, 1], I["g_kv_lora"][l].rearrange("(c p) -> p c", p=128))
            gqn = ld("gqn", [64, 1], I["g_qk_q"][l][0:64].rearrange("(c p) -> p c", p=64))
            gkn = ld("gkn", [64, 1], I["g_qk_k"][l][0:64].rearrange("(c p) -> p c", p=64))
            gqp = sb("gqp_%d" % l, [128, 1])
            gkp = sb("gkp_%d" % l, [128, 1])
            for i in range(4):
                self.dma("sp", gqp.ap[32 * i:32 * i + 32, :], I["g_qk_q"][l][64:96].rearrange("(c p) -> p c", p=32), (), [gqp], self.setup_slot("sp"), slow=True)
                self.dma("sp", gkp.ap[32 * i:32 * i + 32, :], I["g_qk_k"][l][64:96].rearrange("(c p) -> p c", p=32), (), [gkp], self.setup_slot("sp"), slow=True)
            P["gqkn"] = sb("gqkn_%d" % l, [64, 1])
            P["gqkp"] = sb("gqkp_%d" % l, [128, 1])
            self.tt("dve", P["gqkn"].ap, gqn.ap, gkn.ap, ALU.mult, [gqn, gkn], [P["gqkn"]])
            self.tt("dve", P["gqkp"].ap, gqp.ap, gkp.ap, ALU.mult, [gqp, gkp], [P["gqkp"]])
            self.ckpt(6)
            gq1 = ld("gq1", [1, 96], I["g_qk_q"][l:l + 1, :], slow=False)
            gk1 = ld("gk1", [1, 96], I["g_qk_k"][l:l + 1, :], slow=False)
            self.tt("dve", gq1.ap, gq1.ap, gk1.ap, ALU.mult, [gq1, gk1], [gq1])
            m1 = sb("m1_%d" % l, [1, 1])
            self.op("dve", lambda e, m1=m1, gq1=gq1: e.tensor_reduce(out=m1.ap, in_=gq1.ap, axis=AX.X, op=ALU.max, apply_absolute_value=True), [gq1], [m1])
            pb = self.bank("s")
            self.mm(pb.ap[:, 0:1], self.onesf.ap[0:1, 0:128], m1.ap, [self.onesf, m1], [pb])
            P["negM"] = sb("negM_%d" % l, [128, 1])
            self.ts("dve", P["negM"].ap, pb.ap[:, 0:1], -ATTN_SCALE * 96.0, None, ALU.mult, None, [pb], [P["negM"]])
            self.ckpt(7)
            P["gmla_bc"] = ld("gmla_bc", [128, 384], I["g_mla_out"][l:l + 1, :].partition_broadcast(128), slow=False)
            P["gmla_fm"] = ld("gmla_fm", [64, 6], I["g_mla_out"][l].rearrange("(h p) -> p h", p=64))
            P["pscale"] = ld("pscale", [128, 2], I["pool_scale"][l].rearrange("(c p) -> p c", p=128))
            wq = I["w_q_up"][l].rearrange("(c p) h d -> p c h d", p=128)
            P["wqn"] = sb("wqn_%d" % l, [128, 2, 6, 64], BF16)
            P["wqp4"] = sb("wqp4_%d" % l, [128, 2, 6, 128], BF16)
            P["wqr4"] = sb("wqr4_%d" % l, [128, 2, 6, 128], BF16)
            for c in range(2):
                self.dma("pool", P["wqn"].ap[:, c, :, :], wq[:, c, :, 0:64], (), [P["wqn"]], self.setup_slot("pool"))
                for i in range(4):
                    self.dma("pool", P["wqp4"].ap[:, c, :, 32 * i:32 * i + 32], wq[:, c, :, 64:96], (), [P["wqp4"]], self.setup_slot("pool"))
                    self.dma("pool", P["wqr4"].ap[:, c, :, 32 * i:32 * i + 16], wq[:, c, :, 80:96], (), [P["wqr4"]], self.setup_slot("pool"))
                    self.dma("pool", P["wqr4"].ap[:, c, :, 32 * i + 16:32 * i + 32], wq[:, c, :, 64:80], (), [P["wqr4"]], self.setup_slot("pool"))
            self.ckpt(8)
            P["wk"] = ld("wk", [128, 384], I["w_k_up"][l].rearrange("r h d -> r (h d)"), slow=False, eng="pool", dt=BF16)
            P["wv"] = ld("wv", [128, 384], I["w_v_up"][l].rearrange("r h d -> r (h d)"), slow=False, eng="pool", dt=BF16)
            P["wkT"] = sb("wkT_%d" % l, [64, 6, 128], BF16)
            for h in range(6):
                tb = self.bank("t")
                tv = tb.ap.bitcast(BF16)
                self.tr(tv[0:64, 0:128], P["wk"].ap[:, h * 64:(h + 1) * 64], self.identb.ap, [P["wk"], self.identb], [tb])
                self.cp("dve", P["wkT"].ap[:, h, :], tv[0:64, 0:128], [tb], [P["wkT"]])
            self.ckpt(9)
            win = I["w_in"][l].rearrange("(c p) m -> p c m", p=128)
            P["wkr"] = sb("wkr_%d" % l, [128, 8, 32], BF16)
            self.dma("pool", P["wkr"].ap[:, :, 0:16], win[:, :, C_KPE + 16:C_KPE + 32], (), [P["wkr"]], self.setup_slot("pool"))
            self.dma("pool", P["wkr"].ap[:, :, 16:32], win[:, :, C_KPE:C_KPE + 16], (), [P["wkr"]], self.setup_slot("pool"))
            P["wpool"] = sb("wpool_%d" % l, [128, 2, 128], BF16)
            self.memset("pool", P["wpool"].ap, 0.0, [P["wpool"]])
            for g in range(4):
                o = 64 * (g % 2)
                self.dma("pool", P["wpool"].ap[o:o + 64, g // 2, o:o + 64], I["w_pool"][l, g], (), [P["wpool"]], self.setup_slot("pool"))
            self.ckpt(10)
            P["ST"] = sb("ST_%d" % l, [128, 6, 64])
            P["STb"] = sb("STb_%d" % l, [128, 6, 64], BF16)
            self.memset("pool", P["ST"].ap, 0.0, [P["ST"]])
            self.memset("pool", P["STb"].ap, 0.0, [P["STb"]])
            P["hx"] = sb("hx_%d" % l, [64, 6, 3])
            P["hbc"] = sb("hbc_%d" % l, [128, 4, 3])
            self.memset("pool", P["hx"].ap, 0.0, [P["hx"]])
            self.memset("pool", P["hbc"].ap, 0.0, [P["hbc"]])
            P["hu"] = sb("hu_%d" % l, [128, 2, 15])
            self.memset("pool", P["hu"].ap, 0.0, [P["hu"]])
            P["latT"] = sb("latT_%d" % l, [128, SEQ], BF16)
            P["kpeT"] = sb("kpeT_%d" % l, [32, SEQ], BF16)
            P["vaug"] = sb("vaug_%d" % l, [128, 16, 6, 66], BF16)
            self.memset("pool", P["vaug"].ap, 1.0, [P["vaug"]])
            P["rks"] = sb("rks_%d" % l, [128, 16, 6])
            P["mod"] = sb("mod_%d" % l, [128, 48, 17])
            P["A1"] = sb("A1_%d" % l, [128, 8, 17])
            P["A2"] = sb("A2_%d" % l, [128, 8, 17])
            L.append(P)
        self.L = L

        self.ckpt(11)
        call = sb("call", [17, D])
        self.dma("sp", call.ap[0:1, :], I["cp"], (), [call], "call")
        self.dma("sp", call.ap[1:17, :], I["cs"], (), [call], "call")
        self.act(call.ap, call.ap, AF.Silu, [call], [call])
        cT = sb("cT", [128, 8, 18], BF16)
        self.memset("pool", cT.ap, 0.0, [cT])
        for c in range(8):
            tb = self.bank("t")
            self.tr(tb.ap[:, 0:17], call.ap[:, c * 128:(c + 1) * 128], self.identf.ap[0:17, 0:17], [call, self.identf], [tb])
            self.cp("dve", cT.ap[:, c, 0:17], tb.ap[:, 0:17], [tb], [cT])
        self.ckpt(12)
        for l in range(DEPTH):
            P = L[l]
            wa = I["w_ada"][l].rearrange("(c p) m -> p c m", p=128)
            for half in range(2):
                pb = self.bank("d")
                for t in range(6):
                    wbuf, wv_ = self.wload(wa[:, :, (half * 6 + t) * 512:(half * 6 + t + 1) * 512], 128, 8, 512)
                    for mi in range(4):
                        mloc = t * 4 + mi
                        for kc in range(8):
                            self.mm(pb.ap[:, mloc * 18:(mloc + 1) * 18], wv_[:, kc, mi * 128:(mi + 1) * 128], cT.ap[:, kc, :], [wbuf, cT], [pb],
                                    start=(kc == 0), stop=(kc == 7))
                self.tt("dve", P["mod"].ap[:, half * 24:(half + 1) * 24, :], pb.ap[:, 0:24 * 18].rearrange("p (a b) -> p a b", a=24)[:, :, 0:17],
                        P["bada"].ap[:, half * 24:(half + 1) * 24].unsqueeze(2).to_broadcast([128, 24, 17]), ALU.add, [pb, P["bada"]], [P["mod"]])
            for (A, g, so) in ((P["A1"], P["g1"], 8), (P["A2"], P["g2"], 32)):
                self.ts("dve", A.ap, P["mod"].ap[:, so:so + 8, :], 1.0, None, ALU.add, None, [P["mod"]], [A])
                self.tt("dve", A.ap, A.ap, g.ap.unsqueeze(2).to_broadcast([128, 8, 17]), ALU.mult, [A, g], [A])

    def rope_tables(self, base, step, nb):
        cos = self.aa("cos", 128, [nb])
        sin = self.aa("sin", 128, [nb])
        m = self.amark()
        pos = self.aa("pos", 128, [nb])
        ki = self.aa("ki", 128, [nb], I32)
        kf = self.aa("kf", 128, [nb])
        r = self.aa("r", 128, [nb])
        msk = self.aa("msk", 128, [nb])
        self.op("pool", lambda e: e.iota(pos.ap, pattern=[[step, nb]], base=base, channel_multiplier=0, allow_small_or_imprecise_dtypes=True), (), [pos])
        self.ts("dve", pos.ap, pos.ap, self.kc.ap[:, 0:1], None, ALU.mult, None, [pos, self.kc], [pos])
        TWO_PI = 2.0 * math.pi
        C1 = 6.28125
        C2 = TWO_PI - C1
        for (dst, shift) in ((sin, 0.0), (cos, math.pi / 2.0)):
            self.ts("dve", kf.ap, pos.ap, 1.0 / TWO_PI, shift / TWO_PI, ALU.mult, ALU.add, [pos], [kf])
            self.cp("dve", ki.ap, kf.ap, [kf], [ki])
            self.cp("dve", kf.ap, ki.ap, [ki], [kf])
            self.stt(r.ap, kf.ap, -C1, pos.ap, ALU.mult, ALU.add, [kf, pos], [r])
            self.stt(r.ap, kf.ap, -C2, r.ap, ALU.mult, ALU.add, [kf, r], [r])
            if shift != 0.0:
                self.ts("dve", r.ap, r.ap, shift, None, ALU.add, None, [r], [r])
            self.ts("dve", msk.ap, r.ap, math.pi, -TWO_PI, ALU.is_gt, ALU.mult, [r], [msk])
            self.tt("dve", r.ap, r.ap, msk.ap, ALU.add, [r, msk], [r])
            self.ts("dve", msk.ap, r.ap, -math.pi, TWO_PI, ALU.is_lt, ALU.mult, [r], [msk])
            self.tt("dve", r.ap, r.ap, msk.ap, ALU.add, [r, msk], [r])
            self.ts("dve", r.ap, r.ap, math.pi, -math.pi, ALU.min, ALU.max, [r], [r])
            self.act(dst.ap, r.ap, AF.Sin, [r], [dst])
        self.ts("dve", sin.ap, sin.ap, self.kc.ap[:, 1:2], None, ALU.mult, None, [sin, self.kc], [sin])
        self.arelease(m)
        return cos, sin

    def norm_mod(self, xT, hT, A, Bm, bo, nb, sample):
        m = self.amark()
        pb = self.bank("s")
        sqs = [self.aa("nsq", 128, [nb], BF16) for _ in range(2)]
        for c in range(8):
            sq = sqs[c % 2]
            self.act(sq.ap, xT.ap[:, c, :], AF.Square, [xT], [sq])
            self.mm(pb.ap[:, 0:nb], self.onesb.ap, sq.ap, [self.onesb, sq], [pb], start=(c == 0), stop=(c == 7))
        rstd = self.aa("nrstd", 128, [nb])
        self.rsqrt(rstd.ap, pb.ap[:, 0:nb], 1.0 / D, [pb], [rstd])
        if not sample:
            tmps = [self.aa("ntmp", 128, [nb]) for _ in range(2)]
            for c in range(8):
                t = tmps[c % 2]
                self.tt("dve", t.ap, xT.ap[:, c, :], rstd.ap, ALU.mult, [xT, rstd], [t])
                self.act(hT.ap[:, c, :], t.ap, AF.Identity, [t, A, Bm], [hT], scale=A.ap[:, c, 0:1], bias=Bm.ap[:, bo + c, 0:1])
        else:
            t = self.aa("ntmp", 128, [8, nb])
            self.tt("dve", t.ap, xT.ap, rstd.ap.unsqueeze(1).to_broadcast([128, 8, nb]), ALU.mult, [xT, rstd], [t])
            self.tt("dve", t.ap, t.ap, A.ap[:, :, 1:17], ALU.mult, [t, A], [t])
            self.tt("dve", hT.ap, t.ap, Bm.ap[:, bo:bo + 8, 1:17], ALU.add, [t, Bm], [hT])
        self.arelease(m)

    def fm2tm(self, dst_ap, dst_buf, src_ap, src_buf, K, n):
        tb = self.bank("t")
        self.tr(tb.ap[0:n, 0:K], src_ap, self.identf.ap[0:K, 0:K], [src_buf, self.identf], [tb])
        self.cp("act", dst_ap, tb.ap[0:n, 0:K], [tb], [dst_buf])

    def conv_silu(self, P_, raw, nb, w4, wbuf, bcol, bbuf, out_ap, out_buf, acc):
        self.ts("dve", acc.ap, raw.ap[:, 3:3 + nb], w4[:, 3:4], bcol, ALU.mult, ALU.add, [raw, wbuf, bbuf], [acc])
        for k in (2, 1, 0):
            self.stt(acc.ap, raw.ap[:, k:k + nb], w4[:, k:k + 1], acc.ap, ALU.mult, ALU.add, [raw, wbuf, acc], [acc])
        self.act(out_ap, acc.ap, AF.Silu, [acc], [out_buf])

    def ssd_prompt(self, l, hT, mixs, last):
        P = self.L[l]
        I = self.I
        nb = NB
        win = I["w_in"][l].rearrange("(c p) m -> p c m", p=128)
        m0 = self.amark()
        zs = self.aa("zs", 64, [6, nb], BF16)
        xsc = self.aa("xsc", 64, [6, nb], BF16)
        Bc = self.aa("Bc", 128, [2, nb], BF16)
        Cc = self.aa("Cc", 128, [2, nb], BF16)
        dttm = self.aa("dttm", 128, [4, 6])
        atm = self.aa("atm", 128, [4, 6])
        wb, wv = self.wload(win[:, :, C_Z:C_Z + 384], 128, 8, 384)
        for h in range(6):
            pb = self.bank("d")
            for kc in range(8):
                self.mm(pb.ap[0:64, 0:nb], wv[:, kc, h * 64:(h + 1) * 64], hT.ap[:, kc, :], [wb, hT], [pb], start=(kc == 0), stop=(kc == 7))
            self.act(zs.ap[:, h, :], pb.ap[0:64, 0:nb], AF.Silu, [pb], [zs])
        self.ckpt(30)
        m1 = self.amark()
        raws = [self.aa("raw", 128, [3 + nb]) for _ in range(2)]
        accs = [self.aa("cacc", 128, [nb]) for _ in range(2)]
        ri = 0
        wb, wv = self.wload(win[:, :, C_XS:C_XS + 512], 128, 8, 512)
        for h in range(6):
            pb = self.bank("d")
            for kc in range(8):
                self.mm(pb.ap[0:64, 0:nb], wv[:, kc, h * 64:(h + 1) * 64], hT.ap[:, kc, :], [wb, hT], [pb], start=(kc == 0), stop=(kc == 7))
            raw, acc = raws[ri % 2], accs[ri % 2]
            ri += 1
            self.cp("dve", raw.ap[0:64, 0:3], P["hx"].ap[:, h, :], [P["hx"]], [raw])
            self.cp("act", raw.ap[0:64, 3:3 + nb], pb.ap[0:64, 0:nb], [pb], [raw])
            self.cp("dve", P["hx"].ap[:, h, :], raw.ap[0:64, nb:nb + 3], [raw], [P["hx"]])
            rawv = Buf(raw.ap[0:64, :], "x"); rawv.tok = raw.tok
            accv = Buf(acc.ap[0:64, :], "x"); accv.tok = acc.tok
            self.conv_silu(64, rawv, nb, P["cwx"].ap[:, h, :], P["cwx"], P["cbx"].ap[:, h:h + 1], P["cbx"], xsc.ap[:, h, :], xsc, accv)

        def bc_chunk(ci, wb, wv, col0):
            pb = self.bank("d")
            for kc in range(8):
                self.mm(pb.ap[:, 0:nb], wv[:, kc, col0:col0 + 128], hT.ap[:, kc, :], [wb, hT], [pb], start=(kc == 0), stop=(kc == 7))
            nonlocal ri
            raw, acc = raws[ri % 2], accs[ri % 2]
            ri += 1
            self.cp("dve", raw.ap[:, 0:3], P["hbc"].ap[:, ci, :], [P["hbc"]], [raw])
            self.cp("act", raw.ap[:, 3:3 + nb], pb.ap[:, 0:nb], [pb], [raw])
            self.cp("dve", P["hbc"].ap[:, ci, :], raw.ap[:, nb:nb + 3], [raw], [P["hbc"]])
            dst = Bc if ci < 2 else Cc
            self.conv_silu(128, raw, nb, P["cwbc"].ap[:, ci, :], P["cwbc"], P["cbbc"].ap[:, ci:ci + 1], P["cbbc"], dst.ap[:, ci % 2, :], dst, acc)
        bc_chunk(0, wb, wv, 384)
        self.ckpt(31)
        wb, wv = self.wload(win[:, :, 896:1286], 128, 8, 390)
        bc_chunk(1, wb, wv, 0)
        bc_chunk(2, wb, wv, 128)
        bc_chunk(3, wb, wv, 256)
        self.ckpt(32)
        pbd = self.bank("s")
        for t in range(4):
            for kc in range(8):
                self.mm(pbd.ap[:, t * 6:(t + 1) * 6], hT.ap[:, kc, t * 128:(t + 1) * 128], wv[:, kc, 384:390], [wb, hT], [pbd], start=(kc == 0), stop=(kc == 7))
        self.tt("dve", dttm.ap, pbd.ap[:, 0:24].rearrange("p (a b) -> p a b", a=4), P["dtb"].ap.unsqueeze(1).to_broadcast([128, 4, 6]), ALU.add, [pbd, P["dtb"]], [dttm])
        self.act(dttm.ap, dttm.ap, AF.Exp, [dttm], [dttm])
        self.act(dttm.ap, dttm.ap, AF.Ln, [dttm], [dttm], bias=self.onesf.ap[:, 0:1])
        self.tt("dve", atm.ap, dttm.ap, P["negA"].ap.unsqueeze(1).to_broadcast([128, 4, 6]), ALU.mult, [dttm, P["negA"]], [atm])
        self.ckpt(33)
        self.arelease(m1)
        for t in range(4):
            self.ssd_chunk(l, t, t * 128, 128, zs, xsc, Bc, Cc, dttm, atm, mixs)
        if last:
            self.ssd_final_outputs(l)
        self.arelease(m0)

    def ssd_chunk(self, l, t, c0, T, zs, xsc, Bc, Cc, dttm, atm, mixs):
        P = self.L[l]
        m = self.amark()
        cs = slice(c0, c0 + T)
        tb = self.bank("t")
        tv = tb.ap.bitcast(BF16)
        for h in range(6):
            self.tr(tv[:, h * 64:(h + 1) * 64], xsc.ap[:, h, cs], self.identb.ap[0:64, 0:64], [xsc, self.identb], [tb])
        xdt = self.aa("xdt", 128, [6, 64], BF16)
        self.tt("dve", xdt.ap, tv[:, 0:384].rearrange("p (h q) -> p h q", h=6), dttm.ap[:, t, :].unsqueeze(2).to_broadcast([128, 6, 64]), ALU.mult, [tb, dttm], [xdt])
        tb2 = self.bank("t")
        tv2 = tb2.ap.bitcast(BF16)
        for g in range(2):
            self.tr(tv2[:, g * 128:(g + 1) * 128], Bc.ap[:, g, cs], self.identb.ap, [Bc, self.identb], [tb2])
        Btm = self.aa("Btm", 128, [2, 128], BF16)
        self.cp("act", Btm.ap, tv2[:, 0:256].rearrange("p (g n) -> p g n", g=2), [tb2], [Btm])
        self.ckpt(34)
        pa = self.bank("s")
        self.mm(pa.ap[:, 0:6], self.trif.ap, atm.ap[:, t, :], [self.trif, atm], [pa])
        acum = self.aa("acum", 128, [6])
        self.cp("dve", acum.ap, pa.ap[:, 0:6], [pa], [acum])
        self.ckpt(35)
        atri = self.aa("atri", 128, [6, 128])
        self.tt("pool", atri.ap, self.trif.ap.unsqueeze(1).to_broadcast([128, 6, 128]), atm.ap[:, t, :].unsqueeze(2).to_broadcast([128, 6, 128]), ALU.mult,
                [self.trif, atm], [atri])
        self.ckpt(41)
        pbc0, pbc1 = self.PB[4], self.PB[5]
        af = atri.ap.rearrange("p h l -> p (h l)")
        self.mm(pbc0.ap[:, 0:512], self.onesf.ap, af[:, 0:512], [self.onesf, atri], [pbc0])
        self.mm(pbc1.ap[:, 0:256], self.onesf.ap, af[:, 512:768], [self.onesf, atri], [pbc1])
        self.ckpt(42)
        abc = self.ps_t[:, 4 * 512:4 * 512 + 768].rearrange("p (h l) -> p h l", h=6)
        EA = self.aa("EA", 128, [6, 128])
        self.act(EA.ap, abc, AF.Exp, [pbc0, pbc1], [EA])
        self.ckpt(43)
        Dm = self.aa("Dm", 128, [6, 128])
        self.tt("dve", Dm.ap[:, 0:4, :], pbc0.ap[:, 0:512].rearrange("p (h l) -> p h l", h=4), acum.ap[:, 0:4].unsqueeze(2).to_broadcast([128, 4, 128]),
                ALU.subtract, [pbc0, acum], [Dm])
        self.tt("dve", Dm.ap[:, 4:6, :], pbc1.ap[:, 0:256].rearrange("p (h l) -> p h l", h=2), acum.ap[:, 4:6].unsqueeze(2).to_broadcast([128, 2, 128]),
                ALU.subtract, [pbc1, acum], [Dm])
        self.ckpt(44)
        dec = self.aa("dec", 128, [6])
        self.ts("dve", Dm.ap, Dm.ap, 0.0, None, ALU.min, None, [Dm], [Dm])
        self.ckpt(45)
        self.act(dec.ap, Dm.ap[:, :, T - 1], AF.Exp, [Dm], [dec])
        self.ckpt(46)
        self.act(Dm.ap, Dm.ap, AF.Exp, [Dm], [Dm])
        self.ckpt(36)
        pg = self.bank("d")
        for g in range(2):
            self.mm(pg.ap[:, g * 128:(g + 1) * 128], Bc.ap[:, g, cs], Cc.ap[:, g, cs], [Bc, Cc], [pg])
        GmT = self.aa("GmT", 128, [2, 128])
        self.tt("dve", GmT.ap, pg.ap[:, 0:256].rearrange("p (g l) -> p g l", g=2), self.trif.ap.unsqueeze(1).to_broadcast([128, 2, 128]), ALU.mult, [pg, self.trif], [GmT])
        Mm = self.aa("Mm", 128, [6, 128], BF16)
        self.tt("dve", Mm.ap.rearrange("p (g j) l -> p g j l", g=2), Dm.ap.rearrange("p (g j) l -> p g j l", g=2),
                GmT.ap.unsqueeze(2).to_broadcast([128, 2, 3, 128]), ALU.mult, [Dm, GmT], [Mm])
        Cs = self.aa("Cs", 128, [6, 128], BF16)
        self.tt("pool", Cs.ap.rearrange("p (g j) l -> p g j l", g=2), EA.ap.rearrange("p (g j) l -> p g j l", g=2),
                Cc.ap[:, :, cs].unsqueeze(2).to_broadcast([128, 2, 3, 128]), ALU.mult, [EA, Cc], [Cs])
        xdtd = self.aa("xdtd", 128, [6, 64], BF16)
        self.tt("pool", xdtd.ap, xdt.ap, dec.ap.unsqueeze(2).to_broadcast([128, 6, 64]), ALU.mult, [xdt, dec], [xdtd])
        self.ckpt(37)
        py0, py1 = self.PB[0], self.PB[1]
        for h in range(6):
            pb = py0 if h < 4 else py1
            o = (h % 4) * 128
            self.mm(pb.ap[0:64, o:o + 128], xdt.ap[:, h, :], Mm.ap[:, h, :], [xdt, Mm], [pb], start=True, stop=False)
            self.mm(pb.ap[0:64, o:o + 128], P["STb"].ap[:, h, :], Cs.ap[:, h, :], [P["STb"], Cs], [pb], start=False, stop=True)
        yv = self.ps_t[0:64, 0:768].rearrange("p (h l) -> p h l", h=6)
        yz = self.aa("yz", 64, [6, 128])
        self.tt("pool", yz.ap, xsc.ap[:, :, cs], P["dskip"].ap.unsqueeze(2).to_broadcast([64, 6, 128]), ALU.mult, [xsc, P["dskip"]], [yz])
        self.tt("dve", yz.ap[:, 0:4, :], yz.ap[:, 0:4, :], py0.ap[0:64, 0:512].rearrange("p (h l) -> p h l", h=4), ALU.add, [yz, py0], [yz])
        self.tt("dve", yz.ap[:, 4:6, :], yz.ap[:, 4:6, :], py1.ap[0:64, 0:256].rearrange("p (h l) -> p h l", h=2), ALU.add, [yz, py1], [yz])
        self.tt("dve", yz.ap, yz.ap, zs.ap[:, :, cs], ALU.mult, [yz, zs], [yz])
        self.ckpt(38)
        self.ssd_gnorm(l, yz, mixs, cs, T)
        self.ckpt(39)
        pst = self.PB[2]
        for h in range(6):
            self.mm(pst.ap[:, h * 64:(h + 1) * 64], Btm.ap[:, h // 3, :], xdtd.ap[:, h, :], [Btm, xdtd], [pst])
        self.tt("dve", P["ST"].ap, P["ST"].ap, EA.ap[:, :, T - 1].unsqueeze(2).to_broadcast([128, 6, 64]), ALU.mult, [P["ST"], EA], [P["ST"]])
        self.tt("dve", P["ST"].ap, P["ST"].ap, pst.ap[:, 0:384].rearrange("p (h q) -> p h q", h=6), ALU.add, [P["ST"], pst], [P["ST"]])
        self.cp("act", P["STb"].ap, P["ST"].ap, [P["ST"]], [P["STb"]])
        self.arelease(m)

    def ssd_gnorm(self, l, yz, mixs, cs, T):
        P = self.L[l]
        m = self.amark()
        sq = self.aa("gsq", 64, [6, T], BF16)
        self.act(sq.ap, yz.ap, AF.Square, [yz], [sq])
        pb = self.bank("s")
        for g in range(2):
            for j in range(3):
                self.mm(pb.ap[0:64, g * T:(g + 1) * T], self.onesb.ap[0:64, 0:64], sq.ap[:, g * 3 + j, :], [self.onesb, sq], [pb], start=(j == 0), stop=(j == 2))
        rs = self.aa("grs", 64, [2, T])
        self.rsqrt(rs.ap, pb.ap[0:64, 0:2 * T].rearrange("p (g l) -> p g l", g=2), 1.0 / 192.0, [pb], [rs])
        self.tt("dve", yz.ap.rearrange("p (g j) l -> p g j l", g=2), yz.ap.rearrange("p (g j) l -> p g j l", g=2),
                rs.ap.unsqueeze(2).to_broadcast([64, 2, 3, T]), ALU.mult, [yz, rs], [yz])
        self.tt("dve", mixs.ap[:, :, cs], yz.ap, P["gssd"].ap.unsqueeze(2).to_broadcast([64, 6, T]), ALU.mult, [yz, P["gssd"]], [mixs])
        self.arelease(m)

    def ssd_final_outputs(self, l):
        P = self.L[l]
        O = self.O
        m = self.amark()
        stg = self.aa("stg", 128, [3, 128])
        for i in range(3):
            tb = self.bank("t")
            self.tr(tb.ap[:, 0:128], P["ST"].ap[:, 2 * i:2 * i + 2, :].rearrange("p a b -> p (a b)"), self.identf.ap, [P["ST"], self.identf], [tb])
            self.cp("act", stg.ap[:, i, :], tb.ap[:, 0:128], [tb], [stg])
        self.store(O["p_ssm"][l].rearrange("(i a) p n -> (a p) i n", a=2), stg.ap, [stg], "st_pssm%d" % l)
        for k in range(3):
            self.store(O["p_conv"][l][k, 0:384].rearrange("(h p) -> p h", p=64), P["hx"].ap[:, :, k], [P["hx"]], "st_pconv%d" % l, slow=True)
            self.store(O["p_conv"][l][k, 384:896].rearrange("(c p) -> p c", p=128), P["hbc"].ap[:, :, k], [P["hbc"]], "st_pconv%d" % l, slow=True)
        self.arelease(m)

    def mla_front(self, l, hT, nb, cos, sin):
        P = self.L[l]
        win = self.I["w_in"][l].rearrange("(c p) m -> p c m", p=128)
        cqn = self.aa("cqn", 128, [2, nb], BF16)
        lat = self.aa("lat", 128, [nb])
        kper = self.aa("kper", 32, [nb])
        m = self.amark()
        wb, wv = self.wload(win[:, :, C_CQ:C_CQ + 416], 128, 8, 416)
        self.ckpt(61)
        cqr = self.aa("cqr", 128, [2, nb])
        sq = self.aa("msq", 128, [nb], BF16)
        ps = self.bank("s")
        for c in range(2):
            pb = self.bank("d")
            for kc in range(8):
                self.mm(pb.ap[:, 0:nb], wv[:, kc, c * 128:(c + 1) * 128], hT.ap[:, kc, :], [wb, hT], [pb], start=(kc == 0), stop=(kc == 7))
            self.cp("dve", cqr.ap[:, c, :], pb.ap[:, 0:nb], [pb], [cqr])
            self.ckpt(62)
            self.act(sq.ap, cqr.ap[:, c, :], AF.Square, [cqr], [sq])
            self.mm(ps.ap[:, 0:nb], self.onesb.ap, sq.ap, [self.onesb, sq], [ps], start=(c == 0), stop=(c == 1))
            self.ckpt(63)
        rs = self.aa("mrs", 128, [nb])
        self.rsqrt(rs.ap, ps.ap[:, 0:nb], 1.0 / 256.0, [ps], [rs])
        self.ckpt(64)
        for c in range(2):
            self.stt(cqn.ap[:, c, :], cqr.ap[:, c, :], P["gql"].ap[:, c:c + 1], rs.ap, ALU.mult, ALU.mult, [cqr, P["gql"], rs], [cqn])
        self.ckpt(58)
        pb = self.bank("d")
        for kc in range(8):
            self.mm(pb.ap[:, 0:nb], wv[:, kc, 256:384], hT.ap[:, kc, :], [wb, hT], [pb], start=(kc == 0), stop=(kc == 7))
        self.act(sq.ap, pb.ap[:, 0:nb], AF.Square, [pb], [sq])
        ps = self.bank("s")
        self.mm(ps.ap[:, 0:nb], self.onesb.ap, sq.ap, [self.onesb, sq], [ps])
        rs2 = self.aa("mrs2", 128, [nb])
        self.rsqrt(rs2.ap, ps.ap[:, 0:nb], 1.0 / 128.0, [ps], [rs2])
        self.stt(lat.ap, pb.ap[:, 0:nb], P["gkv"].ap[:, 0:1], rs2.ap, ALU.mult, ALU.mult, [pb, P["gkv"], rs2], [lat])
        self.ckpt(59)
        pa = self.bank("d")
        for kc in range(8):
            self.mm(pa.ap[0:32, 0:nb], wv[:, kc, 384:416], hT.ap[:, kc, :], [wb, hT], [pa], start=(kc == 0), stop=(kc == 7))
        pr = self.bank("d")
        for kc in range(8):
            self.mm(pr.ap[0:32, 0:nb], P["wkr"].ap[:, kc, :], hT.ap[:, kc, :], [P["wkr"], hT], [pr], start=(kc == 0), stop=(kc == 7))
        t1 = self.aa("kt1", 32, [nb])
        self.tt("dve", t1.ap, pa.ap[0:32, 0:nb], cos.ap[0:32, :], ALU.mult, [pa, cos], [t1])
        self.tt("dve", kper.ap, pr.ap[0:32, 0:nb], sin.ap[0:32, :], ALU.mult, [pr, sin], [kper])
        self.tt("dve", kper.ap, kper.ap, t1.ap, ALU.add, [kper, t1], [kper])
        self.ckpt(60)
        self.arelease(m)
        return cqn, lat, kper

    def mla_q_head(self, l, h, cqn, nb, cos, sin, PP, qn_ap, qn_buf, qp_ap, qp_buf):
        P = self.L[l]
        m = self.amark()
        pn = self.bank("d")
        for c in range(2):
            self.mm(pn.ap[0:64, 0:nb], P["wqn"].ap[:, c, h, :], cqn.ap[:, c, :], [P["wqn"], cqn], [pn], start=(c == 0), stop=(c == 1))
        pa = self.bank("d")
        for c in range(2):
            self.mm(pa.ap[0:PP, 0:nb], P["wqp4"].ap[:, c, h, 0:PP], cqn.ap[:, c, :], [P["wqp4"], cqn], [pa], start=(c == 0), stop=(c == 1))
        pr = self.bank("d")
        for c in range(2):
            self.mm(pr.ap[0:PP, 0:nb], P["wqr4"].ap[:, c, h, 0:PP], cqn.ap[:, c, :], [P["wqr4"], cqn], [pr], start=(c == 0), stop=(c == 1))
        qpf = self.aa("qpf", PP, [nb])
        t1 = self.aa("qt1", PP, [nb])
        self.tt("dve", t1.ap, pa.ap[0:PP, 0:nb], cos.ap[0:PP, :], ALU.mult, [pa, cos], [t1])
        self.tt("dve", qpf.ap, pr.ap[0:PP, 0:nb], sin.ap[0:PP, :], ALU.mult, [pr, sin], [qpf])
        self.tt("dve", qpf.ap, qpf.ap, t1.ap, ALU.add, [qpf, t1], [qpf])
        sqn = self.aa("sqn", 64, [nb], BF16)
        sqp = self.aa("sqp", 32, [nb], BF16)
        self.act(sqn.ap, pn.ap[0:64, 0:nb], AF.Square, [pn], [sqn])
        self.act(sqp.ap, qpf.ap[0:32, :], AF.Square, [qpf], [sqp])
        MP = max(PP, 64)
        ps = self.bank("s")
        self.mm(ps.ap[0:MP, 0:nb], self.onesb.ap[0:64, 0:MP], sqn.ap, [self.onesb, sqn], [ps], start=True, stop=False)
        self.mm(ps.ap[0:MP, 0:nb], self.onesb.ap[0:32, 0:MP], sqp.ap, [self.onesb, sqp], [ps], start=False, stop=True)
        rs = self.aa("qrs", MP, [nb])
        self.rsqrt(rs.ap, ps.ap[0:MP, 0:nb], 1.0 / 96.0, [ps], [rs])
        self.stt(qn_ap, pn.ap[0:64, 0:nb], P["gqkn"].ap[:, 0:1], rs.ap[0:64, :], ALU.mult, ALU.mult, [pn, P["gqkn"], rs], [qn_buf])
        self.stt(qp_ap, qpf.ap, P["gqkp"].ap[0:PP, 0:1], rs.ap[0:PP, :], ALU.mult, ALU.mult, [qpf, P["gqkp"], rs], [qp_buf])
        self.arelease(m)

    def mla_prompt(self, l, blk, hT, cos, sin, mixm):
        P = self.L[l]
        O = self.O
        nb = NB
        m0 = self.amark()
        cqn, lat, kper = self.mla_front(l, hT, nb, cos, sin)
        c0 = blk * NB
        self.cp("act", P["latT"].ap[:, c0:c0 + nb], lat.ap, [lat], [P["latT"]])
        self.cp("act", P["kpeT"].ap[:, c0:c0 + nb], kper.ap, [kper], [P["kpeT"]])
        self.ckpt(51)
        for t in range(4):
            j = blk * 4 + t
            cs = slice(t * 128, (t + 1) * 128)
            m1 = self.amark()
            lt = self.aa("lt", 128, [128])
            self.fm2tm(lt.ap, lt, lat.ap[:, cs], lat, 128, 128)
            self.store(O["p_lat"][l, c0 + t * 128:c0 + (t + 1) * 128, :], lt.ap, [lt], "st_lt", )
            kt = self.aa("kt", 128, [32])
            self.fm2tm(kt.ap, kt, kper.ap[:, cs], kper, 32, 128)
            self.store(O["p_kpe"][l, c0 + t * 128:c0 + (t + 1) * 128, :], kt.ap, [kt], "st_kt")
            self.ckpt(52)
            pk = self.bank("d")
            self.mm(pk.ap[:, 0:384], P["latT"].ap[:, c0 + t * 128:c0 + (t + 1) * 128], P["wk"].ap, [P["latT"], P["wk"]], [pk])
            ksq = self.aa("ksq", 128, [384], BF16)
            self.act(ksq.ap, pk.ap[:, 0:384], AF.Square, [pk], [ksq])
            ss = self.aa("kss", 128, [8])
            self.red(ss.ap[:, 0:6], ksq.ap.rearrange("p (h d) -> p h d", h=6), ALU.add, [ksq], [ss])
            kq = self.aa("kq", 128, [32])
            self.tt("dve", kq.ap, kt.ap, kt.ap, ALU.mult, [kt], [kq])
            self.red(ss.ap[:, 6:7], kq.ap, ALU.add, [kq], [ss])
            self.ts("dve", ss.ap[:, 0:6], ss.ap[:, 0:6], ss.ap[:, 6:7], None, ALU.add, None, [ss], [ss])
            self.rsqrt(P["rks"].ap[:, j, :], ss.ap[:, 0:6], 1.0 / 96.0, [ss], [P["rks"]])
            self.ts("dve", P["rks"].ap[:, j, :], P["rks"].ap[:, j, :], ATTN_SCALE, None, ALU.mult, None, [P["rks"]], [P["rks"]])
            self.ckpt(53)
            pv = self.bank("d")
            self.mm(pv.ap[:, 0:384], P["latT"].ap[:, c0 + t * 128:c0 + (t + 1) * 128], P["wv"].ap, [P["latT"], P["wv"]], [pv])
            self.cp("act", P["vaug"].ap[:, j, :, 0:64], pv.ap[:, 0:384].rearrange("p (h d) -> p h d", h=6), [pv], [P["vaug"]])
            self.ckpt(54)
            self.arelease(m1)
        qabs = self.aa("qabs", 128, [6, nb], BF16)
        qpa = self.aa("qpa", 32, [6, nb], BF16)
        for h in range(6):
            m1 = self.amark()
            qn = self.aa("qn", 64, [nb], BF16)
            self.mla_q_head(l, h, cqn, nb, cos, sin, 32, qn.ap, qn, qpa.ap[:, h, :], qpa)
            pq = self.bank("s")
            self.mm(pq.ap[:, 0:nb], P["wkT"].ap[:, h, :], qn.ap, [P["wkT"], qn], [pq])
            self.cp("act", qabs.ap[:, h, :], pq.ap[:, 0:nb], [pq], [qabs])
            self.arelease(m1)
        self.ckpt(55)
        OB = [self.PB[0], self.PB[1], self.PB[2], self.PB[3]]
        first = [True] * 4
        njt = blk * 4 + 4
        PTs = [self.aa("PT", 128, [nb], BF16) for _ in range(3)]
        pi = 0
        for h in range(6):
            for j in range(njt):
                tq = max(0, j - blk * 4)
                q0 = tq * 128
                w = nb - q0
                ps = self.PB[4 + (pi % 4)]
                self.mm(ps.ap[:, 0:w], P["latT"].ap[:, j * 128:(j + 1) * 128], qabs.ap[:, h, q0:nb], [P["latT"], qabs], [ps], start=True, stop=False)
                self.mm(ps.ap[:, 0:w], P["kpeT"].ap[:, j * 128:(j + 1) * 128], qpa.ap[:, h, q0:nb], [P["kpeT"], qpa], [ps], start=False, stop=True)
                PT = PTs[pi % 3]
                pi += 1
                self.act(PT.ap[:, 0:w], ps.ap[:, 0:w], AF.Exp, [ps, P["rks"], P["negM"]], [PT], scale=P["rks"].ap[:, j, h:h + 1], bias=P["negM"].ap[:, 0:1])
                if j >= blk * 4:
                    self.tt("pool", PT.ap[:, 0:128], PT.ap[:, 0:128], self.trib.ap, ALU.mult, [PT, self.trib], [PT])
                for t in range(tq, 4):
                    o = (t - tq) * 128
                    self.mm(OB[t].ap[:, h * 66:(h + 1) * 66], PT.ap[:, o:o + 128], P["vaug"].ap[:, j, h, :], [PT, P["vaug"]], [OB[t]],
                            start=first[t], stop=(h == 5 and j == blk * 4 + t), sgc=True)
                    first[t] = False
        self.ckpt(56)
        for t in range(4):
            m1 = self.amark()
            ov = OB[t].ap[:, 0:396].rearrange("p (h d) -> p h d", h=6)
            rinv = self.aa("rinv", 128, [6, 1])
            self.recip(rinv.ap, ov[:, :, 64:65], [OB[t]], [rinv])
            on = self.aa("on", 128, [6, 64])
            self.tt("dve", on.ap, ov[:, :, 0:64], rinv.ap.to_broadcast([128, 6, 64]), ALU.mult, [OB[t], rinv], [on])
            junk = self.aa("junk", 128, [384], BF16)
            ssq = self.aa("ssq", 128, [1])
            self.act(junk.ap, on.ap.rearrange("p h d -> p (h d)"), AF.Square, [on], [junk, ssq], accum=ssq.ap)
            self.rsqrt(ssq.ap, ssq.ap, 1.0 / 384.0, [ssq], [ssq])
            mn = self.aa("mn", 128, [384], BF16)
            self.stt(mn.ap, on.ap.rearrange("p h d -> p (h d)"), ssq.ap[:, 0:1], P["gmla_bc"].ap, ALU.mult, ALU.mult, [on, ssq, P["gmla_bc"]], [mn])
            tb = self.bank("t")
            tv = tb.ap.bitcast(BF16)
            for c in range(3):
                self.tr(tv[:, c * 128:(c + 1) * 128], mn.ap[:, c * 128:(c + 1) * 128], self.identb.ap, [mn, self.identb], [tb])
            self.cp("act", mixm.ap[:, :, t * 128:(t + 1) * 128], tv[:, 0:384].rearrange("p (c q) -> p c q", c=3), [tb], [mixm])
            self.arelease(m1)
        self.arelease(m0)

    def pool_prompt(self, l, blk, hT, mixp, last):
        P = self.L[l]
        nb = NB
        win = self.I["w_in"][l].rearrange("(c p) m -> p c m", p=128)
        m0 = self.amark()
        LW = 15 + nb
        up = self.aa("up", 128, [2, LW])
        wa = self.aa("pwa", 128, [2, LW])
        wb_ = self.aa("pwb", 128, [2, LW])
        pooled = self.aa("pooled", 128, [2, nb], BF16)
        self.memset("pool", wa.ap, 0.0, [wa])
        self.memset("pool", wb_.ap, 0.0, [wb_])
        wb, wv = self.wload(win[:, :, C_U:C_U + 256], 128, 8, 256)
        self.cp("dve", up.ap[:, :, 0:15], P["hu"].ap, [P["hu"]], [up])
        for c in range(2):
            pb = self.bank("d")
            for kc in range(8):
                self.mm(pb.ap[:, 0:nb], wv[:, kc, c * 128:(c + 1) * 128], hT.ap[:, kc, :], [wb, hT], [pb], start=(kc == 0), stop=(kc == 7))
            self.cp("act", up.ap[:, c, 15:LW], pb.ap[:, 0:nb], [pb], [up])
        self.cp("dve", P["hu"].ap, up.ap[:, :, nb:LW], [up], [P["hu"]])
        src = up
        dsts = [wa, wb_, wa, wb_]
        grp = [(0, 0), (0, 64), (1, 0), (1, 64)]
        for g in range(4):
            sh = 1 << g
            dst = dsts[g]
            self.tt("pool", dst.ap[:, :, sh:LW], src.ap[:, :, sh:LW], src.ap[:, :, 0:LW - sh], ALU.add, [src], [dst])
            c, po = grp[g]
            self.stt(pooled.ap[po:po + 64, c, :], dst.ap[po:po + 64, c, 15:LW], self.kc.ap[po:po + 64, 4 + c:5 + c], up.ap[po:po + 64, c, 15:LW],
                     ALU.mult, ALU.subtract, [dst, self.kc, up], [pooled])
            if blk == 0:
                tmp = self.aa("ptmp", 128, [16])
                self.tt("dve", tmp.ap[po:po + 64, :], dst.ap[po:po + 64, c, 15:31], self.rct.ap[po:po + 64, c, :], ALU.mult, [dst, self.rct], [tmp])
                self.tt("dve", pooled.ap[po:po + 64, c, 0:16], tmp.ap[po:po + 64, :], up.ap[po:po + 64, c, 15:31], ALU.subtract, [tmp, up], [pooled])
            src = dst
        for c in range(2):
            pb = self.bank("d")
            self.mm(pb.ap[:, 0:nb], P["wpool"].ap[:, c, :], pooled.ap[:, c, :], [P["wpool"], pooled], [pb])
            self.ts("dve", mixp.ap[:, c, :], pb.ap[:, 0:nb], P["pscale"].ap[:, c:c + 1], None, ALU.mult, None, [pb, P["pscale"]], [mixp])
        if last:
            for c in range(2):
                self.store(self.O["p_pool"][l][:, c * 128:(c + 1) * 128].rearrange("t p -> p t"), P["hu"].ap[:, c, :], [P["hu"]], "st_ppool%d" % l, slow=True)
        self.arelease(m0)

    def out_proj(self, l, xT, mixs, mixm, mixp, nb, sample):
        P = self.L[l]
        wo = self.I["w_out"][l]
        gate = P["mod"]
        for half in range(2):
            cs_ = slice(half * 512, (half + 1) * 512)
            wbA, wvA = self.wload(wo[0:384, cs_].rearrange("(h p) m -> p h m", p=64), 64, 6, 512)
            if sample:
                wbA2, wvA2 = self.wload(wo[384:768, cs_].rearrange("(h p) m -> p h m", p=64), 64, 6, 512)
                wbB, wvB = self.wload(wo[768:D, cs_].rearrange("(c p) m -> p c m", p=128), 128, 2, 512)
            else:
                wbB, wvB = self.wload(wo[384:D, cs_].rearrange("(c p) m -> p c m", p=128), 128, 5, 512)
            for mi in range(4):
                mc = half * 4 + mi
                ms = slice(mi * 128, (mi + 1) * 128)
                pb = self.bank("d")
                ops = []
                for h in range(6):
                    ops.append((wvA[:, h, ms], wbA, mixs.ap[:, h, :], mixs))
                if sample:
                    for h in range(6):
                        ops.append((wvA2[:, h, ms], wbA2, mixm.ap[:, h, :], mixm))
                    for c in range(2):
                        ops.append((wvB[:, c, ms], wbB, mixp.ap[:, c, :], mixp))
                else:
                    for c in range(3):
                        ops.append((wvB[:, c, ms], wbB, mixm.ap[:, c, :], mixm))
                    for c in range(2):
                        ops.append((wvB[:, 3 + c, ms], wbB, mixp.ap[:, c, :], mixp))
                for i, (lh, lb, rh, rb) in enumerate(ops):
                    self.mm(pb.ap[:, 0:nb], lh, rh, [lb, rb], [pb], start=(i == 0), stop=(i == len(ops) - 1))
                self.resid(xT, mc, pb, gate, 16 + mc, nb, sample)

    def resid(self, xT, mc, pb, gate, gi, nb, sample):
        if not sample:
            self.stt(xT.ap[:, mc, :], pb.ap[:, 0:nb], gate.ap[:, gi, 0:1], xT.ap[:, mc, :], ALU.mult, ALU.add, [pb, gate, xT], [xT])
        else:
            m = self.amark()
            t = self.aa("rtmp", 128, [nb])
            self.tt("dve", t.ap, pb.ap[:, 0:nb], gate.ap[:, gi, 1:17], ALU.mult, [pb, gate], [t])
            self.tt("dve", xT.ap[:, mc, :], xT.ap[:, mc, :], t.ap, ALU.add, [xT, t], [xT])
            self.arelease(m)

    def ffn(self, l, xT, hT, nb, sample):
        P = self.L[l]
        I = self.I
        m0 = self.amark()
        hid = self.aa("hid", 128, [NFC, nb], BF16)
        sgs = [self.aa("sg", 128, [nb]) for _ in range(2)]
        wg = I["w_gate"][l].rearrange("(c p) m -> p c m", p=128)
        wu = I["w_up"][l].rearrange("(c p) m -> p c m", p=128)
        fi = 0
        for t in range(6):
            ncol = 512 if t < 5 else 256
            wbg, wvg = self.wload(wg[:, :, t * 512:t * 512 + ncol], 128, 8, ncol)
            wbu, wvu = self.wload(wu[:, :, t * 512:t * 512 + ncol], 128, 8, ncol)
            for i in range(ncol // 128):
                fc = t * 4 + i
                pg = self.bank("d")
                for kc in range(8):
                    self.mm(pg.ap[:, 0:nb], wvg[:, kc, i * 128:(i + 1) * 128], hT.ap[:, kc, :], [wbg, hT], [pg], start=(kc == 0), stop=(kc == 7))
                pu = self.bank("d")
                for kc in range(8):
                    self.mm(pu.ap[:, 0:nb], wvu[:, kc, i * 128:(i + 1) * 128], hT.ap[:, kc, :], [wbu, hT], [pu], start=(kc == 0), stop=(kc == 7))
                sg = sgs[fi % 2]
                fi += 1
                self.act(sg.ap, pg.ap[:, 0:nb], AF.Silu, [pg], [sg])
                self.tt("dve", hid.ap[:, fc, :], sg.ap, pu.ap[:, 0:nb], ALU.mult, [sg, pu], [hid])
        wd = I["w_down"][l].rearrange("(c p) m -> p c m", p=128)
        for mg in range(4):
            pbs = [self.bank("d"), self.bank("d")]
            for kh in range(2):
                wb, wv = self.wload(wd[:, kh * 11:(kh + 1) * 11, mg * 256:(mg + 1) * 256], 128, 11, 256)
                for mi in range(2):
                    for k in range(11):
                        self.mm(pbs[mi].ap[:, 0:nb], wv[:, k, mi * 128:(mi + 1) * 128], hid.ap[:, kh * 11 + k, :], [wb, hid], [pbs[mi]],
                                start=(kh == 0 and k == 0), stop=(kh == 1 and k == 10))
            for mi in range(2):
                mc = mg * 2 + mi
                self.resid(xT, mc, pbs[mi], P["mod"], 40 + mc, nb, sample)
        self.arelease(m0)

    def prompt_block(self, blk):
        I, O = self.I, self.O
        nb = NB
        c0 = blk * NB
        last = (blk == NBLK - 1)
        m0 = self.amark()
        xT = self.aa("xT", 128, [8, nb])
        hT = self.aa("hT", 128, [8, nb], BF16)
        mx = self.amark()
        xins = [self.aa("xin", 128, [D]) for _ in range(2)]
        for t in range(4):
            xin = xins[t % 2]
            self.dma("sp", xin.ap, I["xp"][c0 + t * 128:c0 + (t + 1) * 128, :], (), [xin], "xin%d" % (t % 2))
            for half in range(2):
                tb = self.bank("t")
                for i in range(4):
                    c = half * 4 + i
                    self.tr(tb.ap[:, i * 128:(i + 1) * 128], xin.ap[:, c * 128:(c + 1) * 128], self.identf.ap, [xin, self.identf], [tb])
                self.cp("act" if half else "dve", xT.ap[:, half * 4:half * 4 + 4, t * 128:(t + 1) * 128], tb.ap.rearrange("p (c q) -> p c q", c=4), [tb], [xT])
        self.arelease(mx)
        self.ckpt(20)
        cos, sin = self.rope_tables(c0, 1, nb)
        self.ckpt(21)
        for l in range(DEPTH):
            P = self.L[l]
            m1 = self.amark()
            mixs = self.aa("mixs", 64, [6, nb], BF16)
            mixm = self.aa("mixm", 128, [3, nb], BF16)
            mixp = self.aa("mixp", 128, [2, nb], BF16)
            self.norm_mod(xT, hT, P["A1"], P["mod"], 0, nb, False)
            self.ckpt(22)
            self.ssd_prompt(l, hT, mixs, last)
            self.ckpt(23)
            self.mla_prompt(l, blk, hT, cos, sin, mixm)
            self.ckpt(24)
            self.pool_prompt(l, blk, hT, mixp, last)
            self.ckpt(25)
            self.out_proj(l, xT, mixs, mixm, mixp, nb, False)
            self.ckpt(26)
            self.norm_mod(xT, hT, P["A2"], P["mod"], 24, nb, False)
            self.ckpt(27)
            self.ffn(l, xT, hT, nb, False)
            self.ckpt(28)
            self.arelease(m1)
        youts = [self.aa("yout", 128, [D]) for _ in range(2)]
        for t in range(4):
            yo = youts[t % 2]
            for half in range(2):
                tb = self.bank("t")
                for i in range(4):
                    c = half * 4 + i
                    self.tr(tb.ap[:, i * 128:(i + 1) * 128], xT.ap[:, c, t * 128:(t + 1) * 128], self.identf.ap, [xT, self.identf], [tb])
                self.cp("act" if half else "dve", yo.ap[:, half * 512:(half + 1) * 512], tb.ap, [tb], [yo])
            self.store(O["y_p"][c0 + t * 128:c0 + (t + 1) * 128, :], yo.ap, [yo], "st_y%d" % (t % 2))
        self.arelease(m0)

    def ssd_sample(self, l, hT, mixs):
        P = self.L[l]
        I, O = self.I, self.O
        nb = NS
        win = I["w_in"][l].rearrange("(c p) m -> p c m", p=128)
        m0 = self.amark()
        zs = self.aa("zs", 64, [6, nb], BF16)
        xsc = self.aa("xsc", 64, [6, nb])
        BCc = self.aa("BCc", 128, [4, nb])
        xnx = self.aa("xnx", 64, [6, nb])
        xnbc = self.aa("xnbc", 128, [4, nb])
        cvx = self.aa("cvx", 64, [6, 48])
        cvbc = self.aa("cvbc", 128, [4, 48])
        m1 = self.amark()
        cvtm = self.aa("cvtm", 48, [896])
        self.dma("sp", cvtm.ap, I["sconv"][l].rearrange("s k c -> (s k) c"), (), [cvtm], "cvtm")
        for h in range(6):
            tb = self.bank("t")
            self.tr(tb.ap[0:64, 0:48], cvtm.ap[:, h * 64:(h + 1) * 64], self.identf.ap[0:48, 0:48], [cvtm, self.identf], [tb])
            self.cp("act", cvx.ap[:, h, :], tb.ap[0:64, 0:48], [tb], [cvx])
        for ci in range(4):
            tb = self.bank("t")
            self.tr(tb.ap[:, 0:48], cvtm.ap[:, 384 + ci * 128:384 + (ci + 1) * 128], self.identf.ap[0:48, 0:48], [cvtm, self.identf], [tb])
            self.cp("act", cvbc.ap[:, ci, :], tb.ap[:, 0:48], [tb], [cvbc])
        self.arelease(m1)
        wb, wv = self.wload(win[:, :, C_Z:C_Z + 384], 128, 8, 384)
        for h in range(6):
            pb = self.bank("d")
            for kc in range(8):
                self.mm(pb.ap[0:64, 0:nb], wv[:, kc, h * 64:(h + 1) * 64], hT.ap[:, kc, :], [wb, hT], [pb], start=(kc == 0), stop=(kc == 7))
            self.act(zs.ap[:, h, :], pb.ap[0:64, 0:nb], AF.Silu, [pb], [zs])
        acc = self.aa("sacc", 128, [nb])

        def conv1(PP, pb, new_ap, new_buf, cv_ap, cv_buf, w4, wbuf, bcol, bbuf, out_ap, out_buf):
            self.cp("act", new_ap, pb.ap[0:PP, 0:nb], [pb], [new_buf])
            self.ts("dve", acc.ap[0:PP, :], new_ap, w4[:, 3:4], bcol, ALU.mult, ALU.add, [new_buf, wbuf, bbuf], [acc])
            cvv = cv_ap.rearrange("p (s k) -> p s k", k=3)
            for k in range(3):
                self.stt(acc.ap[0:PP, :], cvv[:, :, k], w4[:, k:k + 1], acc.ap[0:PP, :], ALU.mult, ALU.add, [cv_buf, wbuf, acc], [acc])
            self.act(out_ap, acc.ap[0:PP, :], AF.Silu, [acc], [out_buf])
        wb, wv = self.wload(win[:, :, C_XS:C_XS + 512], 128, 8, 512)
        for h in range(6):
            pb = self.bank("d")
            for kc in range(8):
                self.mm(pb.ap[0:64, 0:nb], wv[:, kc, h * 64:(h + 1) * 64], hT.ap[:, kc, :], [wb, hT], [pb], start=(kc == 0), stop=(kc == 7))
            conv1(64, pb, xnx.ap[:, h, :], xnx, cvx.ap[:, h, :], cvx, P["cwx"].ap[:, h, :], P["cwx"], P["cbx"].ap[:, h:h + 1], P["cbx"], xsc.ap[:, h, :], xsc)

        def bc1(ci, wb, wv, col0):
            pb = self.bank("d")
            for kc in range(8):
                self.mm(pb.ap[:, 0:nb], wv[:, kc, col0:col0 + 128], hT.ap[:, kc, :], [wb, hT], [pb], start=(kc == 0), stop=(kc == 7))
            conv1(128, pb, xnbc.ap[:, ci, :], xnbc, cvbc.ap[:, ci, :], cvbc, P["cwbc"].ap[:, ci, :], P["cwbc"], P["cbbc"].ap[:, ci:ci + 1], P["cbbc"], BCc.ap[:, ci, :], BCc)
        bc1(0, wb, wv, 384)
        wb, wv = self.wload(win[:, :, 896:1286], 128, 8, 390)
        bc1(1, wb, wv, 0)
        bc1(2, wb, wv, 128)
        bc1(3, wb, wv, 256)
        pbd = self.bank("s")
        for kc in range(8):
            self.mm(pbd.ap[0:16, 0:6], hT.ap[:, kc, :], wv[:, kc, 384:390], [wb, hT], [pbd], start=(kc == 0), stop=(kc == 7))
        dtea = self.aa("dtea", 16, [12])
        self.tt("dve", dtea.ap[:, 0:6], pbd.ap[0:16, 0:6], P["dtb"].ap[0:16, :], ALU.add, [pbd, P["dtb"]], [dtea])
        self.act(dtea.ap[:, 0:6], dtea.ap[:, 0:6], AF.Exp, [dtea], [dtea])
        self.act(dtea.ap[:, 0:6], dtea.ap[:, 0:6], AF.Ln, [dtea], [dtea], bias=self.onesf.ap[0:16, 0:1])
        self.tt("dve", dtea.ap[:, 6:12], dtea.ap[:, 0:6], P["negA"].ap[0:16, :], ALU.mult, [dtea, P["negA"]], [dtea])
        self.act(dtea.ap[:, 6:12], dtea.ap[:, 6:12], AF.Exp, [dtea], [dtea])
        ev = self.dma("sp", O["s_conv"][l][:, 0:2, :], I["sconv"][l][:, 1:3, :], (), (), "st_sconv_cp%d" % l)
        self.out_events.append(ev)
        cvnew = self.aa("cvnew", 16, [896])
        for h in range(6):
            self.fm2tm(cvnew.ap[:, h * 64:(h + 1) * 64], cvnew, xnx.ap[:, h, :], xnx, 64, 16)
        for ci in range(4):
            self.fm2tm(cvnew.ap[:, 384 + ci * 128:384 + (ci + 1) * 128], cvnew, xnbc.ap[:, ci, :], xnbc, 128, 16)
        self.store(O["s_conv"][l][:, 2, :], cvnew.ap, [cvnew], "st_sconv%d" % l)
        R = self.aa("Rdt", 16, [16, 12])
        self.tt("dve", R.ap, dtea.ap.unsqueeze(1).to_broadcast([16, 16, 12]), self.identf.ap[0:16, 0:16].unsqueeze(2).to_broadcast([16, 16, 12]), ALU.mult,
                [dtea, self.identf], [R])
        pr = self.bank("s")
        self.mm(pr.ap[0:64, 0:192], self.onesf.ap[0:16, 0:64], R.ap.rearrange("p s j -> p (s j)"), [self.onesf, R], [pr])
        dtb64 = self.aa("dtb64", 64, [16, 12])
        self.cp("dve", dtb64.ap, pr.ap[0:64, 0:192].rearrange("p (s j) -> p s j", s=16), [pr], [dtb64])
        xdt = self.aa("xdts", 64, [16, 6])
        self.tt("dve", xdt.ap, xsc.ap.rearrange("p h s -> p s h"), dtb64.ap[:, :, 0:6], ALU.mult, [xsc, dtb64], [xdt])
        BCtm = self.aa("BCtm", 16, [4, 128])
        for ci in range(4):
            self.fm2tm(BCtm.ap[:, ci, :], BCtm, BCc.ap[:, ci, :], BCc, 128, 16)
        ysm = self.aa("ysm", 64, [16, 6])
        hss = [self.aa("hs", 64, [6, 128]) for _ in range(2)]
        hns = [self.aa("hn", 64, [6, 128]) for _ in range(2)]
        tmp = self.aa("stmp", 64, [6, 128])
        BCd = self.aa("BCd", 16, [512])
        for s in range(NS):
            hs, hn = hss[s % 2], hns[s % 2]
            self.dma("sp", hs.ap, I["sssm"][l, s].rearrange("h p n -> p h n"), (), [hs], "hs%d" % (s % 2))
            self.ts("dve", BCd.ap, BCtm.ap.rearrange("p c n -> p (c n)"), self.identf.ap[0:16, s:s + 1], None, ALU.mult, None, [BCtm, self.identf], [BCd])
            pbc = self.bank("d")
            self.mm(pbc.ap[0:64, 0:512], self.onesf.ap[0:16, 0:64], BCd.ap, [self.onesf, BCd], [pbc])
            bv = pbc.ap[0:64, 0:256].rearrange("p (g n) -> p g n", g=2)
            cv = pbc.ap[0:64, 256:512].rearrange("p (g n) -> p g n", g=2)
            self.tt("dve", hn.ap, hs.ap, dtb64.ap[:, s, 6:12].unsqueeze(2).to_broadcast([64, 6, 128]), ALU.mult, [hs, dtb64], [hn])
            self.tt("dve", tmp.ap.rearrange("p (g j) n -> p g j n", g=2), bv.unsqueeze(2).to_broadcast([64, 2, 3, 128]),
                    xdt.ap[:, s, :].rearrange("p (g j) -> p g j", g=2).unsqueeze(3).to_broadcast([64, 2, 3, 128]), ALU.mult, [pbc, xdt], [tmp])
            self.tt("dve", hn.ap, hn.ap, tmp.ap, ALU.add, [hn, tmp], [hn])
            self.store(O["s_ssm"][l, s].rearrange("h p n -> p h n"), hn.ap, [hn], "st_hn%d" % (s % 2))
            self.tt("dve", tmp.ap.rearrange("p (g j) n -> p g j n", g=2), hn.ap.rearrange("p (g j) n -> p g j n", g=2),
                    cv.unsqueeze(2).to_broadcast([64, 2, 3, 128]), ALU.mult, [hn, pbc], [tmp])
            self.red(ysm.ap[:, s, :], tmp.ap, ALU.add, [tmp], [ysm])
        yz = self.aa("yzs", 64, [6, nb])
        self.tt("dve", yz.ap, xsc.ap, P["dskip"].ap.unsqueeze(2).to_broadcast([64, 6, nb]), ALU.mult, [xsc, P["dskip"]], [yz])
        self.tt("dve", yz.ap, yz.ap, ysm.ap.rearrange("p s h -> p h s"), ALU.add, [yz, ysm], [yz])
        self.tt("dve", yz.ap, yz.ap, zs.ap, ALU.mult, [yz, zs], [yz])
        self.ssd_gnorm(l, yz, mixs, slice(0, nb), nb)
        self.arelease(m0)

    def pool_sample(self, l, hT, mixp):
        P = self.L[l]
        I, O = self.I, self.O
        nb = NS
        win = I["w_in"][l].rearrange("(c p) m -> p c m", p=128)
        m0 = self.amark()
        upad = self.aa("upads", 128, [2, 16, 16])
        unew = self.aa("unew", 128, [2, nb])
        pooled = self.aa("pooleds", 128, [2, nb], BF16)
        for i in range(2):
            ptm = self.aa("ptm%d" % i, 120, [256])
            self.dma("sp", ptm.ap, I["spool"][l, 8 * i:8 * i + 8].rearrange("s t c -> (s t) c"), (), [ptm], "ptm%d" % i)
            for c in range(2):
                tb = self.bank("t")
                self.tr(tb.ap[:, 0:120], ptm.ap[:, c * 128:(c + 1) * 128], self.identf.ap[0:120, 0:120], [ptm, self.identf], [tb])
                self.cp("act", upad.ap[:, c, 8 * i:8 * i + 8, 0:15], tb.ap[:, 0:120].rearrange("p (s t) -> p s t", s=8), [tb], [upad])
        wb, wv = self.wload(win[:, :, C_U:C_U + 256], 128, 8, 256)
        for c in range(2):
            pb = self.bank("d")
            for kc in range(8):
                self.mm(pb.ap[:, 0:nb], wv[:, kc, c * 128:(c + 1) * 128], hT.ap[:, kc, :], [wb, hT], [pb], start=(kc == 0), stop=(kc == 7))
            self.cp("act", unew.ap[:, c, :], pb.ap[:, 0:nb], [pb], [unew])
            self.cp("dve", upad.ap[:, c, :, 15], unew.ap[:, c, :], [unew], [upad])
        W = self.aa("Wsum", 128, [nb])
        grp = [(0, 0), (0, 64), (1, 0), (1, 64)]
        for g in range(4):
            w = 2 << g
            c, po = grp[g]
            self.red(W.ap[po:po + 64, :], upad.ap[po:po + 64, c, :, 16 - w:16], ALU.add, [upad], [W])
            self.stt(pooled.ap[po:po + 64, c, :], W.ap[po:po + 64, :], self.kc.ap[po:po + 64, 4 + c:5 + c], unew.ap[po:po + 64, c, :],
                     ALU.mult, ALU.subtract, [W, self.kc, unew], [pooled])
        for c in range(2):
            pb = self.bank("d")
            self.mm(pb.ap[:, 0:nb], P["wpool"].ap[:, c, :], pooled.ap[:, c, :], [P["wpool"], pooled], [pb])
            self.ts("dve", mixp.ap[:, c, :], pb.ap[:, 0:nb], P["pscale"].ap[:, c:c + 1], None, ALU.mult, None, [pb, P["pscale"]], [mixp])
        ev = self.dma("sp", O["s_pool"][l][:, 0:14, :], I["spool"][l][:, 1:15, :], (), (), "st_spool_cp%d" % l)
        self.out_events.append(ev)
        ut = self.aa("ut", 16, [256])
        for c in range(2):
            self.fm2tm(ut.ap[:, c * 128:(c + 1) * 128], ut, unew.ap[:, c, :], unew, 128, 16)
        self.store(O["s_pool"][l][:, 14, :], ut.ap, [ut], "st_spool%d" % l)
        self.arelease(m0)

    def mla_sample(self, l, hT, cos, sin, mixm):
        P = self.L[l]
        I, O = self.I, self.O
        nb = NS
        m0 = self.amark()
        cqn, lat, kper = self.mla_front(l, hT, nb, cos, sin)
        latnewT = self.aa("latnewT", 128, [128], BF16)
        latnew_tm = self.aa("latnew_tm", 128, [128], BF16)
        kpenewT4 = self.aa("kpenewT4", 128, [128], BF16)
        sspnew = self.aa("sspnew", 128, [1])
        for b in (latnewT, latnew_tm, kpenewT4, sspnew):
            self.memset("pool", b.ap, 0.0, [b])
        self.cp("act", latnewT.ap[:, 0:nb], lat.ap, [lat], [latnewT])
        self.cp("act", kpenewT4.ap[0:32, 0:nb], kper.ap, [kper], [kpenewT4])
        lnt = self.aa("lnt", 16, [128])
        self.fm2tm(lnt.ap, lnt, lat.ap, lat, 128, 16)
        self.store(O["s_lat"][l], lnt.ap, [lnt], "st_slat%d" % l)
        self.cp("dve", latnew_tm.ap[0:16, :], lnt.ap, [lnt], [latnew_tm])
        knt = self.aa("knt", 16, [32])
        self.fm2tm(knt.ap, knt, kper.ap, kper, 32, 16)
        self.store(O["s_kpe"][l], knt.ap, [knt], "st_skpe%d" % l)
        ksq0 = self.aa("ksq0", 16, [32])
        self.tt("dve", ksq0.ap, knt.ap, knt.ap, ALU.mult, [knt], [ksq0])
        self.red(sspnew.ap[0:16, :], ksq0.ap, ALU.add, [ksq0], [sspnew])
        qp4 = self.aa("qp4", 128, [6, nb], BF16)
        qabs = self.aa("qabss", 128, [nb, 6], BF16)
        for h in range(6):
            m1 = self.amark()
            qn = self.aa("qn", 64, [nb], BF16)
            self.mla_q_head(l, h, cqn, nb, cos, sin, 128, qn.ap, qn, qp4.ap[:, h, :], qp4)
            pq = self.bank("s")
            self.mm(pq.ap[:, 0:nb], P["wkT"].ap[:, h, :], qn.ap, [P["wkT"], qn], [pq])
            self.cp("act", qabs.ap[:, :, h], pq.ap[:, 0:nb], [pq], [qabs])
            self.arelease(m1)
        qpad = self.aa("qpad", 128, [nb, 4, 6], BF16)
        for i in range(4):
            self.ts("dve", qpad.ap[:, :, i, :], qp4.ap.rearrange("p h s -> p s h"), self.bmask.ap[:, i:i + 1], None, ALU.mult, None, [qp4, self.bmask], [qpad])
        rhsb = [self.aa("rhsb", 128, [390], BF16) for _ in range(2)]
        for b in rhsb:
            self.cp("dve", b.ap[:, 0:384], P["wk"].ap, [P["wk"]], [b])
        cl8 = I["cl"].rearrange("l n (c a) r -> (l n c) (a r)", c=8)
        ck8 = I["ck"].rearrange("l n (c a) r -> (l n c) (a r)", c=8)
        latcs = [self.aa("latc", 128, [16, 128], BF16) for _ in range(2)]
        kpecs = [self.aa("kpec", 128, [16, 32], BF16) for _ in range(2)]
        latT4s = [self.aa("latT4", 128, [512], BF16) for _ in range(2)]
        kpeT4s = [self.aa("kpeT4", 128, [128], BF16) for _ in range(2)]
        ksqb = self.aa("ksqb", 128, [4, 384], BF16)
        ssn = self.aa("ssn", 128, [4, 6])
        sc = self.aa("scs", 128, [4, 6])
        pbf = [self.aa("pbf", 128, [4, 6], BF16) for _ in range(2)]
        ssp = self.aa("ssp", 128, [16])
        kq = self.aa("kqs", 128, [16, 32])
        oacc = self.PB[0]
        po = self.PB[1]
        PK = [self.PB[2], self.PB[3], self.PB[4], self.PB[5]]
        state = {"g": 0, "first": True}

        def group(s, nt, lat_lhsT, kpeT4_ap, kpeT4_buf, kidx, ssp_ap, ssp_buf, pv_rhs, pv_buf, lat_bufs, onehot):
            gi = state["g"]
            state["g"] += 1
            rb = rhsb[s % 2]
            for i in range(nt):
                pk = PK[i]
                self.mm(pk.ap[:, 384:390], kpeT4_ap, qpad.ap[:, s, kidx(i), :], [kpeT4_buf, qpad], [pk], start=True, stop=False, sgc=True)
                self.mm(pk.ap[:, 0:390], lat_lhsT(i), rb.ap, lat_bufs + [rb], [pk], start=False, stop=True, sgc=True)
            kv = self.ps_t[:, 2 * 512:(2 + nt) * 512].rearrange("p (a b) -> p a b", a=nt)
            self.act(ksqb.ap[:, 0:nt, :], kv[:, :, 0:384], AF.Square, PK[0:nt], [ksqb])
            self.red(ssn.ap[:, 0:nt, :].rearrange("p a h -> p (a h)"), ksqb.ap[:, 0:nt, :].rearrange("p a (h d) -> p (a h) d", h=6), ALU.add, [ksqb], [ssn])
            self.tt("dve", ssn.ap[:, 0:nt, :], ssn.ap[:, 0:nt, :], ssp_ap.unsqueeze(2).to_broadcast([128, nt, 6]), ALU.add, [ssn, ssp_buf], [ssn])
            self.rsqrt(ssn.ap[:, 0:nt, :], ssn.ap[:, 0:nt, :], 1.0 / 96.0, [ssn], [ssn])
            self.cp("act", sc.ap[:, 0:nt, :], kv[:, :, 384:390], PK[0:nt], [sc])
            self.tt("dve", sc.ap[:, 0:nt, :], sc.ap[:, 0:nt, :], ssn.ap[:, 0:nt, :], ALU.mult, [sc, ssn], [sc])
            pb_ = pbf[gi % 2]
            self.act(pb_.ap[:, 0:nt, :], sc.ap[:, 0:nt, :], AF.Exp, [sc, P["negM"]], [pb_], scale=ATTN_SCALE, bias=P["negM"].ap[:, 0:1])
            if onehot is not None:
                self.ts("dve", pb_.ap[:, 0:nt, :], pb_.ap[:, 0:nt, :], onehot, None, ALU.mult, None, [pb_, self.identf], [pb_])
            for i in range(nt):
                self.mm(oacc.ap[0:6, 0:128], pb_.ap[:, i, :], pv_rhs(i), [pb_, pv_buf], [oacc], start=state["first"], stop=False, sgc=True)
                state["first"] = False
                self.mm(oacc.ap[0:6, 128:130], pb_.ap[:, i, :], self.onesb.ap[:, 0:2], [pb_, self.onesb], [oacc], start=False,
                        stop=(onehot is not None and i == nt - 1), sgc=True)

        ci_ = 0
        for s in range(NS):
            rb = rhsb[s % 2]
            self.cp("dve", rb.ap[:, 384:390], qabs.ap[:, s, :], [qabs], [rb])
            state["first"] = True
            for c in range(8):
                latc, kpec = latcs[ci_ % 2], kpecs[ci_ % 2]
                sl = ci_ % 2
                ci_ += 1
                self.op("pool", lambda e, latc=latc, s=s, c=c: e.indirect_dma_start(
                    out=latc.ap.rearrange("p a r -> p (a r)"), out_offset=None, in_=cl8[:, :],
                    in_offset=bass.IndirectOffsetOnAxis(ap=self.idx8.ap[:, l, s, c:c + 1], axis=0)), [self.idx8], [latc], "latc%d" % sl)
                self.op("pool", lambda e, kpec=kpec, s=s, c=c: e.indirect_dma_start(
                    out=kpec.ap.rearrange("p a r -> p (a r)"), out_offset=None, in_=ck8[:, :],
                    in_offset=bass.IndirectOffsetOnAxis(ap=self.idx8.ap[:, l, s, c:c + 1], axis=0)), [self.idx8], [kpec], "kpec%d" % sl)
                self.tt("pool", kq.ap, kpec.ap, kpec.ap, ALU.mult, [kpec], [kq])
                self.red(ssp.ap, kq.ap, ALU.add, [kq], [ssp])
                for g in range(4):
                    gi = state["g"]
                    latT4, kpeT4 = latT4s[gi % 2], kpeT4s[gi % 2]
                    tb = self.PB[6]
                    tv = tb.ap.bitcast(BF16)
                    for i in range(4):
                        self.tr(tv[:, i * 128:(i + 1) * 128], latc.ap[:, 4 * g + i, :], self.identb.ap, [latc, self.identb], [tb])
                    self.cp("act", latT4.ap, tv[:, 0:512], [tb], [latT4])
                    tb2 = self.PB[7]
                    tv2 = tb2.ap.bitcast(BF16)
                    self.tr(tv2[:, 0:128], kpec.ap[:, 4 * g:4 * g + 4, :].rearrange("p a r -> p (a r)"), self.identb.ap, [kpec, self.identb], [tb2])
                    self.cp("dve", kpeT4.ap, tv2[:, 0:128], [tb2], [kpeT4])
                    group(s, 4, lambda i, latT4=latT4: latT4.ap[:, i * 128:(i + 1) * 128], kpeT4.ap, kpeT4, lambda i: i,
                          ssp.ap[:, 4 * g:4 * g + 4], ssp, lambda i, latc=latc, g=g: latc.ap[:, 4 * g + i, :], latc, [latT4], None)
            group(s, 1, lambda i: latnewT.ap, kpenewT4.ap, kpenewT4, lambda i: 0, sspnew.ap[:, 0:1], sspnew,
                  lambda i: latnew_tm.ap, latnew_tm, [latnewT], self.identf.ap[:, s:s + 1])
            m1 = self.amark()
            oa = self.aa("oa", 6, [129])
            self.cp("act", oa.ap, oacc.ap[0:6, 0:129], [oacc], [oa])
            rinv = self.aa("orinv", 6, [1])
            self.recip(rinv.ap, oa.ap[:, 128:129], [oa], [rinv])
            ol = self.aa("ol", 6, [128])
            self.ts("dve", ol.ap, oa.ap[:, 0:128], rinv.ap[:, 0:1], None, ALU.mult, None, [oa, rinv], [ol])
            tb = self.PB[6]
            self.tr(tb.ap[:, 0:6], ol.ap, self.identf.ap[0:6, 0:6], [ol, self.identf], [tb])
            olT = self.aa("olT", 128, [8], BF16)
            self.memset("pool", olT.ap, 0.0, [olT])
            self.cp("act", olT.ap[:, 0:6], tb.ap[:, 0:6], [tb], [olT])
            for h in range(6):
                self.mm(po.ap[0:64, h * 18 + s:h * 18 + s + 2], P["wv"].ap[:, h * 64:(h + 1) * 64], olT.ap[:, h:h + 2], [P["wv"], olT], [po], start=True, stop=True, sgc=True)
            self.arelease(m1)
        of = self.aa("of", 64, [6, nb])
        self.cp("dve", of.ap, po.ap[0:64, 0:108].rearrange("p (h s) -> p h s", h=6)[:, :, 0:16], [po], [of])
        sq = self.aa("osq", 64, [6, nb], BF16)
        self.act(sq.ap, of.ap, AF.Square, [of], [sq])
        ps = self.bank("s")
        for h in range(6):
            self.mm(ps.ap[0:64, 0:nb], self.onesb.ap[0:64, 0:64], sq.ap[:, h, :], [self.onesb, sq], [ps], start=(h == 0), stop=(h == 5))
        rs = self.aa("ors", 64, [nb])
        self.rsqrt(rs.ap, ps.ap[0:64, 0:nb], 1.0 / 384.0, [ps], [rs])
        self.tt("dve", of.ap, of.ap, rs.ap.unsqueeze(1).to_broadcast([64, 6, nb]), ALU.mult, [of, rs], [of])
        self.tt("dve", mixm.ap, of.ap, P["gmla_fm"].ap.unsqueeze(2).to_broadcast([64, 6, nb]), ALU.mult, [of, P["gmla_fm"]], [mixm])
        self.arelease(m0)

    def sample_block(self):
        I, O = self.I, self.O
        nb = NS
        m0 = self.amark()
        xT = self.aa("xTs", 128, [8, nb])
        hT = self.aa("hTs", 128, [8, nb], BF16)
        xin = self.aa("xins", 16, [D])
        self.dma("sp", xin.ap, I["xs"], (), [xin], "xins")
        for c in range(8):
            tb = self.bank("t")
            self.tr(tb.ap[:, 0:nb], xin.ap[:, c * 128:(c + 1) * 128], self.identf.ap[0:16, 0:16], [xin, self.identf], [tb])
            self.cp("act", xT.ap[:, c, :], tb.ap[:, 0:nb], [tb], [xT])
        cos, sin = self.rope_tables(PAST, 0, nb)
        for l in range(DEPTH):
            P = self.L[l]
            m1 = self.amark()
            mixs = self.aa("mixs", 64, [6, nb], BF16)
            mixm = self.aa("mixm", 64, [6, nb], BF16)
            mixp = self.aa("mixp", 128, [2, nb], BF16)
            self.norm_mod(xT, hT, P["A1"], P["mod"], 0, nb, True)
            self.ssd_sample(l, hT, mixs)
            self.mla_sample(l, hT, cos, sin, mixm)
            self.pool_sample(l, hT, mixp)
            self.out_proj(l, xT, mixs, mixm, mixp, nb, True)
            self.norm_mod(xT, hT, P["A2"], P["mod"], 24, nb, True)
            self.ffn(l, xT, hT, nb, True)
            self.arelease(m1)
        yo = self.aa("youts", 16, [D])
        for c in range(8):
            self.fm2tm(yo.ap[:, c * 128:(c + 1) * 128], yo, xT.ap[:, c, :], xT, 128, 16)
        self.store(O["y_s"], yo.ap, [yo], "st_ys")
        self.arelease(m0)

    def build(self, do_prompt=True, do_sample=True, nblk=NBLK, arena_bytes=88 * 1024):
        self.declare()
        self.arena_init(arena_bytes)
        try:
            self.setup()
            if do_prompt:
                for blk in range(nblk):
                    self.prompt_block(blk)
            if do_sample:
                self.sample_block()
        except Builder._Stop:
            pass
        self.S.ins["sp"].append(dict(fn=None, deps=list(self.out_events), signal=False, dma=None))
        self.S.finalize()
        nc = self.nc
        esems = {e: self.es.enter_context(nc.semaphore("sem_" + e)) for e in ENGS}
        dsems = {v[0]: self.es.enter_context(nc.semaphore("dsem_%d" % v[0])) for k, v in self.S.dma_slots.items()}
        S = self.S
        with nc.Block() as block:
            @block.sync
            def _(eng):
                S.run_engine("sp", eng, esems, dsems)

            @block.scalar
            def _(eng):
                S.run_engine("act", eng, esems, dsems)

            @block.vector
            def _(eng):
                S.run_engine("dve", eng, esems, dsems)

            @block.gpsimd
            def _(eng):
                S.run_engine("pool", eng, esems, dsems)

            @block.tensor
            def _(eng):
                S.run_engine("pe", eng, esems, dsems)
        self.es.close()
        return nc


def host_consts():
    kc = np.zeros((128, 8), np.float32)
    half = 16
    inv = (1.0 / (np.float32(10000.0) ** (np.arange(half, dtype=np.float32) / np.float32(half)))).astype(np.float32)
    p = np.arange(128)
    kc[:, 0] = inv[p % 16]
    kc[:, 1] = np.where((p % 32) < 16, -1.0, 1.0)
    kc[:, 2] = np.where(p < 64, 2.0, 4.0)
    kc[:, 3] = np.where(p < 64, 8.0, 16.0)
    kc[:, 4] = 1.0 / kc[:, 2]
    kc[:, 5] = 1.0 / kc[:, 3]
    return kc


WEIGHTS = ["w_ada", "b_ada", "g_norm1", "w_in", "conv_w", "conv_b", "dt_bias", "a_log", "d_skip", "g_ssd_norm", "g_q_lora", "w_q_up",
           "g_kv_lora", "w_k_up", "w_v_up", "g_qk_q", "g_qk_k", "g_mla_out", "w_pool", "pool_scale", "w_out", "g_norm2", "w_gate", "w_up", "w_down"]


def make_in_map(inp, c):
    f = lambda a: np.ascontiguousarray(np.asarray(a))
    m = {
        "xp": f(inp["x_prompt"][c]),
        "xs": f(inp["x_sample"][NS * c:NS * (c + 1), 0]),
        "cl": inp["cache_kv_latent"],
        "ck": inp["cache_k_rope"],
        "sssm": f(inp["state_ssm"][:, NS * c:NS * (c + 1)]),
        "sconv": f(inp["state_conv"][:, NS * c:NS * (c + 1)]),
        "spool": f(inp["state_pool"][:, NS * c:NS * (c + 1)]),
        "pt": f(inp["page_table"][NS * c:NS * (c + 1)]).astype(np.int32, copy=False),
        "cp": f(inp["c_prompt"][c:c + 1]),
        "cs": f(inp["c_sample"][NS * c:NS * (c + 1)]),
        "kconst": host_consts(),
    }
    for w in WEIGHTS:
        m[w] = inp[w]
    return m


_CACHE = {}


def kernel(**inp):
    inp = {k: np.asarray(v) for k, v in inp.items()}
    n_pool = inp["cache_kv_latent"].shape[1]
    ncores = inp["x_prompt"].shape[0]
    if n_pool not in _CACHE:
        _CACHE[n_pool] = Builder(n_pool).build()
    nc = _CACHE[n_pool]
    in_maps = [make_in_map(inp, c) for c in range(ncores)]
    res = run_bass_kernel_spmd(nc, in_maps, core_ids=list(range(ncores)))
    R = res.results
    cat = lambda k, ax: np.concatenate([np.asarray(r[k]) for r in R], axis=ax)
    y_p = np.stack([np.asarray(r["y_p"]) for r in R], 0)
    y_s = cat("y_s", 0)[:, None, :]
    p_lat = np.stack([np.asarray(r["p_lat"]) for r in R], 1)
    p_kpe = np.stack([np.asarray(r["p_kpe"]) for r in R], 1)
    p_ssm = np.stack([np.asarray(r["p_ssm"]) for r in R], 1)
    p_conv = np.stack([np.asarray(r["p_conv"]) for r in R], 1)
    p_pool = np.stack([np.asarray(r["p_pool"]) for r in R], 1)
    s_lat = cat("s_lat", 1)[:, :, None, :]
    s_kpe = cat("s_kpe", 1)[:, :, None, :]
    s_ssm = cat("s_ssm", 1)
    s_conv = cat("s_conv", 1)
    s_pool = cat("s_pool", 1)
    return (y_p, y_s, p_lat, p_kpe, p_ssm, p_conv, p_pool, s_lat, s_kpe, s_ssm, s_conv, s_pool)
```

```python
import math
from contextlib import ExitStack

import numpy as np
import concourse.bass as bass
import concourse.mybir as mybir
from concourse.bass_utils import run_bass_kernel_spmd

F32 = mybir.dt.float32
BF16 = mybir.dt.bfloat16
I32 = mybir.dt.int32
AF = mybir.ActivationFunctionType
ALU = mybir.AluOpType
AX = mybir.AxisListType

NCORES = 8
D = 1024
SEQ = 2048
NBLK = 4
NB = 512
NS = 16
DEPTH = 2
EPS = 1e-6
DFF = 2816
NFC = DFF // 128
ATTN_SCALE = 96 ** -0.5
PAST = 16384
NPG = 128
C_Z, C_XS, C_B, C_C, C_DT, C_CQ, C_CKV, C_KPE, C_U = 0, 384, 768, 1024, 1280, 1286, 1542, 1670, 1702
IN_COLS = 1958

ENGS = ("pe", "act", "dve", "pool", "sp")


class Tok:
    __slots__ = ("name", "w", "rs")

    def __init__(self, name):
        self.name = name
        self.w = None
        self.rs = []


class Buf:
    def __init__(self, ap, name):
        self.ap = ap
        self.tok = Tok(name)

    def __getitem__(self, k):
        return self.ap[k]


class Sched:
    SAME = True
    SAME_EXEMPT = ("pe",)

    def __init__(self):
        self.ins = {e: [] for e in ENGS}
        self.dma_slots = {}

    def op(self, eng, fn, R=(), W=(), slot=None):
        deps = []
        for b in R:
            t = b.tok
            if t.w is not None:
                deps.append(t.w)
        for b in W:
            t = b.tok
            if t.w is not None:
                deps.append(t.w)
            deps.extend(t.rs)
        idx = len(self.ins[eng])
        rec = dict(fn=fn, deps=deps, signal=False, dma=None)
        if slot is not None:
            s = self.dma_slots.setdefault(slot, [len(self.dma_slots), 0])
            if s[1] > 0:
                deps.append(("D", s[0], s[1]))
            s[1] += 16
            rec["dma"] = (s[0], s[1])
            ev = ("D", s[0], s[1])
        else:
            ev = ("E", eng, idx)
        self.ins[eng].append(rec)
        for b in R:
            b.tok.rs.append(ev)
        for b in W:
            b.tok.w = ev
            b.tok.rs = []
        return ev

    def finalize(self):
        for e in ENGS:
            for i, rec in enumerate(self.ins[e]):
                for d in rec["deps"]:
                    if d[0] == "E" and (d[1] != e or (self.SAME and e not in self.SAME_EXEMPT and d[2] < i)):
                        self.ins[d[1]][d[2]]["signal"] = True
        self.val = {}
        for e in ENGS:
            c = 0
            for i, rec in enumerate(self.ins[e]):
                if rec["signal"]:
                    c += 1
                    self.val[(e, i)] = c

    def run_engine(self, e, engobj, esems, dsems):
        seen = {}
        for i, rec in enumerate(self.ins[e]):
            need = {}
            for d in rec["deps"]:
                if d[0] == "E":
                    if d[1] == e and (not self.SAME or e in self.SAME_EXEMPT or d[2] >= i):
                        continue
                    key = ("E", d[1])
                    v = self.val[(d[1], d[2])]
                else:
                    key = ("D", d[1])
                    v = d[2]
                if seen.get(key, 0) >= v:
                    continue
                if need.get(key, 0) < v:
                    need[key] = v
            for key, v in need.items():
                sem = esems[key[1]] if key[0] == "E" else dsems[key[1]]
                engobj.wait_ge(sem, v)
                seen[key] = v
            if rec["fn"] is None:
                continue
            r = rec["fn"](engobj)
            if rec["dma"] is not None:
                r.then_inc(dsems[rec["dma"][0]], 16)
            elif rec["signal"]:
                r.then_inc(esems[e], 1)


class Builder:
    def __init__(self, n_pool, dbg=False):
        self.n_pool = n_pool
        self.nc = bass.Bass("TRN2", target_bir_lowering=False)
        self.S = Sched()
        self.es = ExitStack()
        self.out_events = []
        self.nbuf = 0
        self.dbg_on = dbg
        self.dbg_list = []

    def dram_in(self, name, shape, dt=F32):
        return self.nc.dram_tensor(name, list(shape), dt, kind="ExternalInput").ap()

    def dram_out(self, name, shape, dt=F32):
        return self.nc.dram_tensor(name, list(shape), dt, kind="ExternalOutput").ap()

    def sb(self, name, shape, dt=F32):
        t = self.es.enter_context(self.nc.sbuf_tensor(name, list(shape), dt))
        return Buf(t[:], name)

    def arena_init(self, nbytes):
        self.ar_words = nbytes // 4
        t = self.es.enter_context(self.nc.sbuf_tensor("arena", [128, self.ar_words], F32))
        self.ar_t = t
        self.ar_ptr = 0
        self.ar_hist = []
        self.ar_peak = 0

    def amark(self):
        return self.ar_ptr

    def arelease(self, mark):
        self.ar_ptr = mark

    def aa(self, name, parts, fshape, dt=F32):
        n = 1
        for s in fshape:
            n *= s
        esz = 4 if dt in (F32, I32) else 2
        words = (n * esz + 3) // 4
        lo = self.ar_ptr
        hi = lo + words
        assert hi <= self.ar_words, "arena overflow %s need %d have %d" % (name, hi, self.ar_words)
        self.ar_ptr = hi
        self.ar_peak = max(self.ar_peak, hi)
        ap = self.ar_t[0:parts, lo:hi]
        if dt != F32:
            ap = ap.bitcast(dt)
            if esz == 2:
                ap = ap[:, 0:n]
        if len(fshape) == 2:
            ap = ap.rearrange("p (a b) -> p a b", a=fshape[0])
        elif len(fshape) == 3:
            ap = ap.rearrange("p (a b c) -> p a b c", a=fshape[0], b=fshape[1])
        self.nbuf += 1
        b = Buf(ap, "%s#%d" % (name, self.nbuf))
        keep = []
        for (l2, h2, ob) in self.ar_hist:
            if l2 < hi and lo < h2:
                if ob.tok.w is not None:
                    b.tok.rs.append(ob.tok.w)
                b.tok.rs.extend(ob.tok.rs)
                if not (lo <= l2 and h2 <= hi):
                    keep.append((l2, h2, ob))
            else:
                keep.append((l2, h2, ob))
        keep.append((lo, hi, b))
        self.ar_hist = keep
        return b

    def op(self, eng, fn, R=(), W=(), slot=None):
        Rn = [b for b in R if not getattr(b, "excl", False)]
        Wn = list(W) + [b for b in R if getattr(b, "excl", False) and b not in W]
        return self.S.op(eng, fn, Rn, Wn, slot)

    def mm(self, out, lhsT, rhs, R, W, start=True, stop=True, sgc=False):
        return self.op("pe", lambda e: e.matmul(out, lhsT=lhsT, rhs=rhs, start=start, stop=stop, skip_group_check=sgc), R, W)

    def tr(self, out, in_, ident, R, W):
        return self.op("pe", lambda e: e.transpose(out=out, in_=in_, identity=ident), R, W)

    def act(self, out, in_, func, R, W, scale=None, bias=None, accum=None):
        kw = {}
        if scale is not None:
            kw["scale"] = scale
        if bias is not None:
            kw["bias"] = bias
        if accum is not None:
            kw["accum_out"] = accum
        return self.op("act", lambda e: e.activation(out=out, in_=in_, func=func, **kw), R, W)

    def tt(self, eng, out, in0, in1, op, R, W):
        return self.op(eng, lambda e: e.tensor_tensor(out=out, in0=in0, in1=in1, op=op), R, W)

    def ts(self, eng, out, in0, s1, s2, op0, op1, R, W):
        if op1 is None:
            return self.op(eng, lambda e: e.tensor_scalar(out=out, in0=in0, scalar1=s1, scalar2=None, op0=op0), R, W)
        return self.op(eng, lambda e: e.tensor_scalar(out=out, in0=in0, scalar1=s1, scalar2=s2, op0=op0, op1=op1), R, W)

    def stt(self, out, in0, scalar, in1, op0, op1, R, W):
        return self.op("dve", lambda e: e.scalar_tensor_tensor(out=out, in0=in0, scalar=scalar, in1=in1, op0=op0, op1=op1), R, W)

    def red(self, out, in_, op, R, W):
        return self.op("dve", lambda e: e.tensor_reduce(out=out, in_=in_, axis=AX.X, op=op), R, W)

    def cp(self, eng, out, in_, R, W):
        if eng == "act":
            return self.op("act", lambda e: e.copy(out=out, in_=in_), R, W)
        return self.op(eng, lambda e: e.tensor_copy(out=out, in_=in_), R, W)

    def memset(self, eng, ap, val, W):
        return self.op(eng, lambda e: e.memset(ap, val), (), W)

    def recip(self, out, in_, R, W):
        return self.op("dve", lambda e: e.reciprocal(out=out, in_=in_), R, W)

    def dma(self, eng, out, in_, R, W, slot, slow=False):
        if slow:
            return self.op(eng, lambda e: e.dma_start(out=out, in_=in_, allow_slow_non_contiguous=True), R, W, slot)
        return self.op(eng, lambda e: e.dma_start(out=out, in_=in_), R, W, slot)

    def store(self, out, in_, R, slot, slow=False):
        ev = self.dma("sp", out, in_, R, (), slot, slow)
        self.out_events.append(ev)
        return ev

    def rsqrt(self, out, in_, scale, R, W):
        P = out.shape[0]
        self.act(out, in_, AF.Ln, list(R) + [self.eps], W, scale=scale, bias=self.eps.ap[0:P, 0:1])
        self.act(out, out, AF.Exp, W, W, scale=-0.5)

    def psum_init(self):
        t = self.es.enter_context(self.nc.psum_tensor("ps", [128, 4096], F32))
        self.ps_t = t
        self.PB = [Buf(t[:, b * 512:(b + 1) * 512], "psb%d" % b) for b in range(8)]
        for b in self.PB:
            b.excl = True
        self.rot = {"d": 0, "s": 0, "t": 0}

    def bank(self, kind):
        if kind == "d":
            b = self.rot["d"]
            self.rot["d"] = (b + 1) % 4
            return self.PB[b]
        if kind == "s":
            b = self.rot["s"]
            self.rot["s"] = (b + 1) % 2
            return self.PB[4 + b]
        b = self.rot["t"]
        self.rot["t"] = (b + 1) % 2
        return self.PB[6 + b]

    def wbuf_init(self, n=3):
        self.wb = [self.sb("wbuf%d" % i, [128, 4096], BF16) for i in range(n)]
        self.wi = 0

    def wload(self, dram_view, parts, kc, ncols):
        b = self.wb[self.wi]
        slot = "wbuf%d" % self.wi
        self.wi = (self.wi + 1) % len(self.wb)
        v = b.ap[0:parts, 0:kc * ncols].rearrange("p (k n) -> p k n", k=kc)
        self.dma("pool", v, dram_view, (), [b], slot)
        return b, v

    def declare(self):
        npool = self.n_pool
        I = {}
        I["xp"] = self.dram_in("xp", [SEQ, D])
        I["xs"] = self.dram_in("xs", [NS, D])
        I["cl"] = self.dram_in("cl", [DEPTH, npool, 128, 128])
        I["ck"] = self.dram_in("ck", [DEPTH, npool, 128, 32])
        I["sssm"] = self.dram_in("sssm", [DEPTH, NS, 6, 64, 128])
        I["sconv"] = self.dram_in("sconv", [DEPTH, NS, 3, 896])
        I["spool"] = self.dram_in("spool", [DEPTH, NS, 15, 256])
        I["pt"] = self.dram_in("pt", [NS, NPG], I32)
        I["cp"] = self.dram_in("cp", [1, D])
        I["cs"] = self.dram_in("cs", [NS, D])
        I["w_ada"] = self.dram_in("w_ada", [DEPTH, D, 6 * D])
        I["b_ada"] = self.dram_in("b_ada", [DEPTH, 6 * D])
        I["g_norm1"] = self.dram_in("g_norm1", [DEPTH, D])
        I["w_in"] = self.dram_in("w_in", [DEPTH, D, IN_COLS])
        I["conv_w"] = self.dram_in("conv_w", [DEPTH, 4, 896])
        I["conv_b"] = self.dram_in("conv_b", [DEPTH, 896])
        I["dt_bias"] = self.dram_in("dt_bias", [DEPTH, 6])
        I["a_log"] = self.dram_in("a_log", [DEPTH, 6])
        I["d_skip"] = self.dram_in("d_skip", [DEPTH, 6])
        I["g_ssd_norm"] = self.dram_in("g_ssd_norm", [DEPTH, 384])
        I["g_q_lora"] = self.dram_in("g_q_lora", [DEPTH, 256])
        I["w_q_up"] = self.dram_in("w_q_up", [DEPTH, 256, 6, 96])
        I["g_kv_lora"] = self.dram_in("g_kv_lora", [DEPTH, 128])
        I["w_k_up"] = self.dram_in("w_k_up", [DEPTH, 128, 6, 64])
        I["w_v_up"] = self.dram_in("w_v_up", [DEPTH, 128, 6, 64])
        I["g_qk_q"] = self.dram_in("g_qk_q", [DEPTH, 96])
        I["g_qk_k"] = self.dram_in("g_qk_k", [DEPTH, 96])
        I["g_mla_out"] = self.dram_in("g_mla_out", [DEPTH, 384])
        I["w_pool"] = self.dram_in("w_pool", [DEPTH, 4, 64, 64])
        I["pool_scale"] = self.dram_in("pool_scale", [DEPTH, 256])
        I["w_out"] = self.dram_in("w_out", [DEPTH, D, D])
        I["g_norm2"] = self.dram_in("g_norm2", [DEPTH, D])
        I["w_gate"] = self.dram_in("w_gate", [DEPTH, D, DFF])
        I["w_up"] = self.dram_in("w_up", [DEPTH, D, DFF])
        I["w_down"] = self.dram_in("w_down", [DEPTH, DFF, D])
        I["kconst"] = self.dram_in("kconst", [128, 8])
        self.I = I
        O = {}
        O["y_p"] = self.dram_out("y_p", [SEQ, D])
        O["y_s"] = self.dram_out("y_s", [NS, D])
        O["p_lat"] = self.dram_out("p_lat", [DEPTH, SEQ, 128])
        O["p_kpe"] = self.dram_out("p_kpe", [DEPTH, SEQ, 32])
        O["p_ssm"] = self.dram_out("p_ssm", [DEPTH, 6, 64, 128])
        O["p_conv"] = self.dram_out("p_conv", [DEPTH, 3, 896])
        O["p_pool"] = self.dram_out("p_pool", [DEPTH, 15, 256])
        O["s_lat"] = self.dram_out("s_lat", [DEPTH, NS, 128])
        O["s_kpe"] = self.dram_out("s_kpe", [DEPTH, NS, 32])
        O["s_ssm"] = self.dram_out("s_ssm", [DEPTH, NS, 6, 64, 128])
        O["s_conv"] = self.dram_out("s_conv", [DEPTH, NS, 3, 896])
        O["s_pool"] = self.dram_out("s_pool", [DEPTH, NS, 15, 256])
        self.O = O

    class _Stop(Exception):
        pass

    def ckpt(self, k):
        if getattr(self, "stop_at", None) == k:
            raise Builder._Stop()

    def setup_slot(self, eng):
        self._ss = getattr(self, "_ss", 0) + 1
        return "%s_ld%d" % (eng, self._ss % 4)

    def dbg(self, name, buf, ap, shape):
        if not self.dbg_on:
            return
        o = self.dram_out("dbg_" + name, list(shape))
        self.dbg_list.append("dbg_" + name)
        self.store(o, ap, [buf], "dbgst_" + name)

    def setup(self):
        I = self.I
        self.psum_init()
        self.wbuf_init(3)
        sb = self.sb
        self.eps = sb("eps", [128, 1])
        self.memset("pool", self.eps.ap, EPS, [self.eps])
        self.identf = sb("identf", [128, 128])
        self.identb = sb("identb", [128, 128], BF16)
        self.memset("pool", self.identf.ap, 0.0, [self.identf])
        self.op("pool", lambda e: e.affine_select(out=self.identf.ap, in_=self.identf.ap, pattern=[[-1, 128]], compare_op=ALU.not_equal,
                                                  fill=1.0, base=0, channel_multiplier=1), [self.identf], [self.identf])
        self.cp("pool", self.identb.ap, self.identf.ap, [self.identf], [self.identb])
        self.trif = sb("trif", [128, 128])
        self.trib = sb("trib", [128, 128], BF16)
        self.memset("pool", self.trif.ap, 1.0, [self.trif])
        self.op("pool", lambda e: e.affine_select(out=self.trif.ap, in_=self.trif.ap, pattern=[[1, 128]], compare_op=ALU.is_ge,
                                                  fill=0.0, base=0, channel_multiplier=-1), [self.trif], [self.trif])
        self.cp("pool", self.trib.ap, self.trif.ap, [self.trif], [self.trib])
        self.onesf = sb("onesf", [128, 128])
        self.onesb = sb("onesb", [128, 128], BF16)
        self.memset("pool", self.onesf.ap, 1.0, [self.onesf])
        self.memset("pool", self.onesb.ap, 1.0, [self.onesb])
        self.ckpt(1)
        self.bmask = sb("bmask", [128, 4])
        self.memset("pool", self.bmask.ap, 0.0, [self.bmask])
        for i in range(4):
            self.memset("pool", self.bmask.ap[32 * i:32 * i + 32, i:i + 1], 1.0, [self.bmask])
        self.ckpt(2)
        self.kc = sb("kc", [128, 8])
        self.dma("sp", self.kc.ap, I["kconst"], (), [self.kc], "kc")
        self.rct = sb("rct", [128, 2, 16])
        tmpi = sb("tmpi", [128, 16])
        self.op("pool", lambda e: e.iota(tmpi.ap, pattern=[[1, 16]], base=1, channel_multiplier=0, allow_small_or_imprecise_dtypes=True), (), [tmpi])
        for c in range(2):
            self.ts("dve", self.rct.ap[:, c, :], tmpi.ap, self.kc.ap[:, 2 + c:3 + c], None, ALU.min, None, [tmpi, self.kc], [self.rct])
        self.recip(self.rct.ap, self.rct.ap, [self.rct], [self.rct])
        self.ckpt(3)
        ptT = sb("ptT", [128, NS], I32)
        self.dma("sp", ptT.ap, I["pt"].rearrange("s j -> j s"), (), [ptT], "ptT", slow=True)
        ptf = sb("ptf", [128, NS])
        self.cp("dve", ptf.ap, ptT.ap, [ptT], [ptf])
        self.idx8 = sb("idx8", [128, DEPTH, NS, 8], I32)
        for l in range(DEPTH):
            for c in range(8):
                self.ts("dve", self.idx8.ap[:, l, :, c], ptf.ap, 8.0, float(c + 8 * l * self.n_pool), ALU.mult, ALU.add, [ptf], [self.idx8])

        self.ckpt(4)
        L = []
        for l in range(DEPTH):
            P = {}

            def ld(name, shape, src, slow=True, eng="sp", dt=F32):
                b = sb("%s_%d" % (name, l), shape, dt)
                self.dma(eng, b.ap, src, (), [b], self.setup_slot(eng), slow=slow)
                return b
            P["g1"] = ld("g1", [128, 8], I["g_norm1"][l].rearrange("(c p) -> p c", p=128))
            P["g2"] = ld("g2", [128, 8], I["g_norm2"][l].rearrange("(c p) -> p c", p=128))
            P["bada"] = ld("bada", [128, 48], I["b_ada"][l].rearrange("(c p) -> p c", p=128))
            P["cwx"] = sb("cwx_%d" % l, [64, 6, 4])
            P["cwbc"] = sb("cwbc_%d" % l, [128, 4, 4])
            for k in range(4):
                self.dma("sp", P["cwx"].ap[:, :, k], I["conv_w"][l][k, 0:384].rearrange("(h p) -> p h", p=64), (), [P["cwx"]], self.setup_slot("sp"), slow=True)
                self.dma("sp", P["cwbc"].ap[:, :, k], I["conv_w"][l][k, 384:896].rearrange("(c p) -> p c", p=128), (), [P["cwbc"]], self.setup_slot("sp"), slow=True)
            P["cbx"] = ld("cbx", [64, 6], I["conv_b"][l][0:384].rearrange("(h p) -> p h", p=64))
            P["cbbc"] = ld("cbbc", [128, 4], I["conv_b"][l][384:896].rearrange("(c p) -> p c", p=128))
            P["gssd"] = ld("gssd", [64, 6], I["g_ssd_norm"][l].rearrange("(h p) -> p h", p=64))
            P["dskip"] = ld("dskip", [64, 6], I["d_skip"][l:l + 1, :].partition_broadcast(64), slow=False)
            P["dtb"] = ld("dtb", [128, 6], I["dt_bias"][l:l + 1, :].partition_broadcast(128), slow=False)
            alog = ld("alog", [128, 6], I["a_log"][l:l + 1, :].partition_broadcast(128), slow=False)
            P["negA"] = sb("negA_%d" % l, [128, 6])
            self.act(P["negA"].ap, alog.ap, AF.Exp, [alog], [P["negA"]])
            self.ts("dve", P["negA"].ap, P["negA"].ap, -1.0, None, ALU.mult, None, [P["negA"]], [P["negA"]])
            self.ckpt(5)
            P["gql"] = ld("gql", [128, 2], I["g_q_lora"][l].rearrange("(c p) -> p c", p=128))
            P["gkv"] = ld("gkv", [128, 1], I["g_kv_lora"][l].rearrange("(c p) -> p c", p=128))
            gqn = ld("gqn", [64, 1], I["g_qk_q"][l][0:64].rearrange("(c p) -> p c", p=64))
            gkn = ld("gkn", [64, 1], I["g_qk_k"][l][0:64].rearrange("(c p) -> p c", p=64))
            gqp = sb("gqp_%d" % l, [128, 1])
            gkp = sb("gkp_%d" % l, [128, 1])
            for i in range(4):
                self.dma("sp", gqp.ap[32 * i:32 * i + 32, :], I["g_qk_q"][l][64:96].rearrange("(c p) -> p c", p=32), (), [gqp], self.setup_slot("sp"), slow=True)
                self.dma("sp", gkp.ap[32 * i:32 * i + 32, :], I["g_qk_k"][l][64:96].rearrange("(c p) -> p c", p=32), (), [gkp], self.setup_slot("sp"), slow=True)
            P["gqkn"] = sb("gqkn_%d" % l, [64, 1])
            P["gqkp"] = sb("gqkp_%d" % l, [128, 1])
            self.tt("dve", P["gqkn"].ap, gqn.ap, gkn.ap, ALU.mult, [gqn, gkn], [P["gqkn"]])
            self.tt("dve", P["gqkp"].ap, gqp.ap, gkp.ap, ALU.mult, [gqp, gkp], [P["gqkp"]])
            self.ckpt(6)
            gq1 = ld("gq1", [1, 96], I["g_qk_q"][l:l + 1, :], slow=False)
            gk1 = ld("gk1", [1, 96], I["g_qk_k"][l:l + 1, :], slow=False)
            self.tt("dve", gq1.ap, gq1.ap, gk1.ap, ALU.mult, [gq1, gk1], [gq1])
            m1 = sb("m1_%d" % l, [1, 1])
            self.op("dve", lambda e, m1=m1, gq1=gq1: e.tensor_reduce(out=m1.ap, in_=gq1.ap, axis=AX.X, op=ALU.max, apply_absolute_value=True), [gq1], [m1])
            pb = self.bank("s")
            self.mm(pb.ap[:, 0:1], self.onesf.ap[0:1, 0:128], m1.ap, [self.onesf, m1], [pb])
            P["negM"] = sb("negM_%d" % l, [128, 1])
            self.ts("dve", P["negM"].ap, pb.ap[:, 0:1], -ATTN_SCALE * 96.0, None, ALU.mult, None, [pb], [P["negM"]])
            self.ckpt(7)
            P["gmla_bc"] = ld("gmla_bc", [128, 384], I["g_mla_out"][l:l + 1, :].partition_broadcast(128), slow=False)
            P["gmla_fm"] = ld("gmla_fm", [64, 6], I["g_mla_out"][l].rearrange("(h p) -> p h", p=64))
            P["pscale"] = ld("pscale", [128, 2], I["pool_scale"][l].rearrange("(c p) -> p c", p=128))
            wq = I["w_q_up"][l].rearrange("(c p) h d -> p c h d", p=128)
            P["wqn"] = sb("wqn_%d" % l, [128, 2, 6, 64], BF16)
            P["wqp4"] = sb("wqp4_%d" % l, [128, 2, 6, 128], BF16)
            P["wqr4"] = sb("wqr4_%d" % l, [128, 2, 6, 128], BF16)
            for c in range(2):
                self.dma("pool", P["wqn"].ap[:, c, :, :], wq[:, c, :, 0:64], (), [P["wqn"]], self.setup_slot("pool"))
                for i in range(4):
                    self.dma("pool", P["wqp4"].ap[:, c, :, 32 * i:32 * i + 32], wq[:, c, :, 64:96], (), [P["wqp4"]], self.setup_slot("pool"))
                    self.dma("pool", P["wqr4"].ap[:, c, :, 32 * i:32 * i + 16], wq[:, c, :, 80:96], (), [P["wqr4"]], self.setup_slot("pool"))
                    self.dma("pool", P["wqr4"].ap[:, c, :, 32 * i + 16:32 * i + 32], wq[:, c, :, 64:80], (), [P["wqr4"]], self.setup_slot("pool"))
            self.ckpt(8)
            P["wk"] = ld("wk", [128, 384], I["w_k_up"][l].rearrange("r h d -> r (h d)"), slow=False, eng="pool", dt=BF16)
            P["wv"] = ld("wv", [128, 384], I["w_v_up"][l].rearrange("r h d -> r (h d)"), slow=False, eng="pool", dt=BF16)
            P["wkT"] = sb("wkT_%d" % l, [64, 6, 128], BF16)
            for h in range(6):
                tb = self.bank("t")
                tv = tb.ap.bitcast(BF16)
                self.tr(tv[0:64, 0:128], P["wk"].ap[:, h * 64:(h + 1) * 64], self.identb.ap, [P["wk"], self.identb], [tb])
                self.cp("dve", P["wkT"].ap[:, h, :], tv[0:64, 0:128], [tb], [P["wkT"]])
            self.ckpt(9)
            win = I["w_in"][l].rearrange("(c p) m -> p c m", p=128)
            P["wkr"] = sb("wkr_%d" % l, [128, 8, 32], BF16)
            self.dma("pool", P["wkr"].ap[:, :, 0:16], win[:, :, C_KPE + 16:C_KPE + 32], (), [P["wkr"]], self.setup_slot("pool"))
            self.dma("pool", P["wkr"].ap[:, :, 16:32], win[:, :, C_KPE:C_KPE + 16], (), [P["wkr"]], self.setup_slot("pool"))
            P["wpool"] = sb("wpool_%d" % l, [128, 2, 128], BF16)
            self.memset("pool", P["wpool"].ap, 0.0, [P["wpool"]])
            for g in range(4):
                o = 64 * (g % 2)
                self.dma("pool", P["wpool"].ap[o:o + 64, g // 2, o:o + 64], I["w_pool"][l, g], (), [P["wpool"]], self.setup_slot("pool"))
            self.ckpt(10)
            P["ST"] = sb("ST_%d" % l, [128, 6, 64])
            P["STb"] = sb("STb_%d" % l, [128, 6, 64], BF16)
            self.memset("pool", P["ST"].ap, 0.0, [P["ST"]])
            self.memset("pool", P["STb"].ap, 0.0, [P["STb"]])
            P["hx"] = sb("hx_%d" % l, [64, 6, 3])
            P["hbc"] = sb("hbc_%d" % l, [128, 4, 3])
            self.memset("pool", P["hx"].ap, 0.0, [P["hx"]])
            self.memset("pool", P["hbc"].ap, 0.0, [P["hbc"]])
            P["hu"] = sb("hu_%d" % l, [128, 2, 15])
            self.memset("pool", P["hu"].ap, 0.0, [P["hu"]])
            P["latT"] = sb("latT_%d" % l, [128, SEQ], BF16)
            P["kpeT"] = sb("kpeT_%d" % l, [32, SEQ], BF16)
            P["vaug"] = sb("vaug_%d" % l, [128, 16, 6, 66], BF16)
            self.memset("pool", P["vaug"].ap, 1.0, [P["vaug"]])
            P["rks"] = sb("rks_%d" % l, [128, 16, 6])
            P["mod"] = sb("mod_%d" % l, [128, 48, 17])
            P["A1"] = sb("A1_%d" % l, [128, 8, 17])
            P["A2"] = sb("A2_%d" % l, [128, 8, 17])
            L.append(P)
        self.L = L

        self.ckpt(11)
        call = sb("call", [17, D])
        self.dma("sp", call.ap[0:1, :], I["cp"], (), [call], "call")
        self.dma("sp", call.ap[1:17, :], I["cs"], (), [call], "call")
        self.act(call.ap, call.ap, AF.Silu, [call], [call])
        cT = sb("cT", [128, 8, 18], BF16)
        self.memset("pool", cT.ap, 0.0, [cT])
        for c in range(8):
            tb = self.bank("t")
            self.tr(tb.ap[:, 0:17], call.ap[:, c * 128:(c + 1) * 128], self.identf.ap[0:17, 0:17], [call, self.identf], [tb])
            self.cp("dve", cT.ap[:, c, 0:17], tb.ap[:, 0:17], [tb], [cT])
        self.ckpt(12)
        for l in range(DEPTH):
            P = L[l]
            wa = I["w_ada"][l].rearrange("(c p) m -> p c m", p=128)
            for half in range(2):
                pb = self.bank("d")
                for t in range(6):
                    wbuf, wv_ = self.wload(wa[:, :, (half * 6 + t) * 512:(half * 6 + t + 1) * 512], 128, 8, 512)
                    for mi in range(4):
                        mloc = t * 4 + mi
                        for kc in range(8):
                            self.mm(pb.ap[:, mloc * 18:(mloc + 1) * 18], wv_[:, kc, mi * 128:(mi + 1) * 128], cT.ap[:, kc, :], [wbuf, cT], [pb],
                                    start=(kc == 0), stop=(kc == 7))
                self.tt("dve", P["mod"].ap[:, half * 24:(half + 1) * 24, :], pb.ap[:, 0:24 * 18].rearrange("p (a b) -> p a b", a=24)[:, :, 0:17],
                        P["bada"].ap[:, half * 24:(half + 1) * 24].unsqueeze(2).to_broadcast([128, 24, 17]), ALU.add, [pb, P["bada"]], [P["mod"]])
            for (A, g, so) in ((P["A1"], P["g1"], 8), (P["A2"], P["g2"], 32)):
                self.ts("dve", A.ap, P["mod"].ap[:, so:so + 8, :], 1.0, None, ALU.add, None, [P["mod"]], [A])
                self.tt("dve", A.ap, A.ap, g.ap.unsqueeze(2).to_broadcast([128, 8, 17]), ALU.mult, [A, g], [A])

    def rope_tables(self, base, step, nb):
        cos = self.aa("cos", 128, [nb])
        sin = self.aa("sin", 128, [nb])
        m = self.amark()
        pos = self.aa("pos", 128, [nb])
        ki = self.aa("ki", 128, [nb], I32)
        kf = self.aa("kf", 128, [nb])
        r = self.aa("r", 128, [nb])
        msk = self.aa("msk", 128, [nb])
        self.op("pool", lambda e: e.iota(pos.ap, pattern=[[step, nb]], base=base, channel_multiplier=0, allow_small_or_imprecise_dtypes=True), (), [pos])
        self.ts("dve", pos.ap, pos.ap, self.kc.ap[:, 0:1], None, ALU.mult, None, [pos, self.kc], [pos])
        TWO_PI = 2.0 * math.pi
        C1 = 6.28125
        C2 = TWO_PI - C1
        for (dst, shift) in ((sin, 0.0), (cos, math.pi / 2.0)):
            self.ts("dve", kf.ap, pos.ap, 1.0 / TWO_PI, shift / TWO_PI, ALU.mult, ALU.add, [pos], [kf])
            self.cp("dve", ki.ap, kf.ap, [kf], [ki])
            self.cp("dve", kf.ap, ki.ap, [ki], [kf])
            self.stt(r.ap, kf.ap, -C1, pos.ap, ALU.mult, ALU.add, [kf, pos], [r])
            self.stt(r.ap, kf.ap, -C2, r.ap, ALU.mult, ALU.add, [kf, r], [r])
            if shift != 0.0:
                self.ts("dve", r.ap, r.ap, shift, None, ALU.add, None, [r], [r])
            self.ts("dve", msk.ap, r.ap, math.pi, -TWO_PI, ALU.is_gt, ALU.mult, [r], [msk])
            self.tt("dve", r.ap, r.ap, msk.ap, ALU.add, [r, msk], [r])
            self.ts("dve", msk.ap, r.ap, -math.pi, TWO_PI, ALU.is_lt, ALU.mult, [r], [msk])
            self.tt("dve", r.ap, r.ap, msk.ap, ALU.add, [r, msk], [r])
            self.ts("dve", r.ap, r.ap, math.pi, -math.pi, ALU.min, ALU.max, [r], [r])
            self.act(dst.ap, r.ap, AF.Sin, [r], [dst])
        self.ts("dve", sin.ap, sin.ap, self.kc.ap[:, 1:2], None, ALU.mult, None, [sin, self.kc], [sin])
        self.arelease(m)
        return cos, sin

    def norm_mod(self, xT, hT, A, Bm, bo, nb, sample):
        m = self.amark()
        pb = self.bank("s")
        sqs = [self.aa("nsq", 128, [nb], BF16) for _ in range(2)]
        for c in range(8):
            sq = sqs[c % 2]
            self.act(sq.ap, xT.ap[:, c, :], AF.Square, [xT], [sq])
            self.mm(pb.ap[:, 0:nb], self.onesb.ap, sq.ap, [self.onesb, sq], [pb], start=(c == 0), stop=(c == 7))
        rstd = self.aa("nrstd", 128, [nb])
        self.rsqrt(rstd.ap, pb.ap[:, 0:nb], 1.0 / D, [pb], [rstd])
        if not sample:
            tmps = [self.aa("ntmp", 128, [nb]) for _ in range(2)]
            for c in range(8):
                t = tmps[c % 2]
                self.tt("dve", t.ap, xT.ap[:, c, :], rstd.ap, ALU.mult, [xT, rstd], [t])
                self.act(hT.ap[:, c, :], t.ap, AF.Identity, [t, A, Bm], [hT], scale=A.ap[:, c, 0:1], bias=Bm.ap[:, bo + c, 0:1])
        else:
            t = self.aa("ntmp", 128, [8, nb])
            self.tt("dve", t.ap, xT.ap, rstd.ap.unsqueeze(1).to_broadcast([128, 8, nb]), ALU.mult, [xT, rstd], [t])
            self.tt("dve", t.ap, t.ap, A.ap[:, :, 1:17], ALU.mult, [t, A], [t])
            self.tt("dve", hT.ap, t.ap, Bm.ap[:, bo:bo + 8, 1:17], ALU.add, [t, Bm], [hT])
        self.arelease(m)

    def fm2tm(self, dst_ap, dst_buf, src_ap, src_buf, K, n):
        tb = self.bank("t")
        self.tr(tb.ap[0:n, 0:K], src_ap, self.identf.ap[0:K, 0:K], [src_buf, self.identf], [tb])
        self.cp("act", dst_ap, tb.ap[0:n, 0:K], [tb], [dst_buf])

    def conv_silu(self, P_, raw, nb, w4, wbuf, bcol, bbuf, out_ap, out_buf, acc):
        self.ts("dve", acc.ap, raw.ap[:, 3:3 + nb], w4[:, 3:4], bcol, ALU.mult, ALU.add, [raw, wbuf, bbuf], [acc])
        for k in (2, 1, 0):
            self.stt(acc.ap, raw.ap[:, k:k + nb], w4[:, k:k + 1], acc.ap, ALU.mult, ALU.add, [raw, wbuf, acc], [acc])
        self.act(out_ap, acc.ap, AF.Silu, [acc], [out_buf])

    def ssd_prompt(self, l, hT, mixs, last):
        P = self.L[l]
        I = self.I
        nb = NB
        win = I["w_in"][l].rearrange("(c p) m -> p c m", p=128)
        m0 = self.amark()
        zs = self.aa("zs", 64, [6, nb], BF16)
        xsc = self.aa("xsc", 64, [6, nb], BF16)
        Bc = self.aa("Bc", 128, [2, nb], BF16)
        Cc = self.aa("Cc", 128, [2, nb], BF16)
        dttm = self.aa("dttm", 128, [4, 6])
        atm = self.aa("atm", 128, [4, 6])
        wb, wv = self.wload(win[:, :, C_Z:C_Z + 384], 128, 8, 384)
        for h in range(6):
            pb = self.bank("d")
            for kc in range(8):
                self.mm(pb.ap[0:64, 0:nb], wv[:, kc, h * 64:(h + 1) * 64], hT.ap[:, kc, :], [wb, hT], [pb], start=(kc == 0), stop=(kc == 7))
            self.act(zs.ap[:, h, :], pb.ap[0:64, 0:nb], AF.Silu, [pb], [zs])
        self.ckpt(30)
        m1 = self.amark()
        raws = [self.aa("raw", 128, [3 + nb]) for _ in range(2)]
        accs = [self.aa("cacc", 128, [nb]) for _ in range(2)]
        ri = 0
        wb, wv = self.wload(win[:, :, C_XS:C_XS + 512], 128, 8, 512)
        for h in range(6):
            pb = self.bank("d")
            for kc in range(8):
                self.mm(pb.ap[0:64, 0:nb], wv[:, kc, h * 64:(h + 1) * 64], hT.ap[:, kc, :], [wb, hT], [pb], start=(kc == 0), stop=(kc == 7))
            raw, acc = raws[ri % 2], accs[ri % 2]
            ri += 1
            self.cp("dve", raw.ap[0:64, 0:3], P["hx"].ap[:, h, :], [P["hx"]], [raw])
            self.cp("act", raw.ap[0:64, 3:3 + nb], pb.ap[0:64, 0:nb], [pb], [raw])
            self.cp("dve", P["hx"].ap[:, h, :], raw.ap[0:64, nb:nb + 3], [raw], [P["hx"]])
            rawv = Buf(raw.ap[0:64, :], "x"); rawv.tok = raw.tok
            accv = Buf(acc.ap[0:64, :], "x"); accv.tok = acc.tok
            self.conv_silu(64, rawv, nb, P["cwx"].ap[:, h, :], P["cwx"], P["cbx"].ap[:, h:h + 1], P["cbx"], xsc.ap[:, h, :], xsc, accv)

        def bc_chunk(ci, wb, wv, col0):
            pb = self.bank("d")
            for kc in range(8):
                self.mm(pb.ap[:, 0:nb], wv[:, kc, col0:col0 + 128], hT.ap[:, kc, :], [wb, hT], [pb], start=(kc == 0), stop=(kc == 7))
            nonlocal ri
            raw, acc = raws[ri % 2], accs[ri % 2]
            ri += 1
            self.cp("dve", raw.ap[:, 0:3], P["hbc"].ap[:, ci, :], [P["hbc"]], [raw])
            self.cp("act", raw.ap[:, 3:3 + nb], pb.ap[:, 0:nb], [pb], [raw])
            self.cp("dve", P["hbc"].ap[:, ci, :], raw.ap[:, nb:nb + 3], [raw], [P["hbc"]])
            dst = Bc if ci < 2 else Cc
            self.conv_silu(128, raw, nb, P["cwbc"].ap[:, ci, :], P["cwbc"], P["cbbc"].ap[:, ci:ci + 1], P["cbbc"], dst.ap[:, ci % 2, :], dst, acc)
        bc_chunk(0, wb, wv, 384)
        self.ckpt(31)
        wb, wv = self.wload(win[:, :, 896:1286], 128, 8, 390)
        bc_chunk(1, wb, wv, 0)
        bc_chunk(2, wb, wv, 128)
        bc_chunk(3, wb, wv, 256)
        self.ckpt(32)
        pbd = self.bank("s")
        for t in range(4):
            for kc in range(8):
                self.mm(pbd.ap[:, t * 6:(t + 1) * 6], hT.ap[:, kc, t * 128:(t + 1) * 128], wv[:, kc, 384:390], [wb, hT], [pbd], start=(kc == 0), stop=(kc == 7))
        self.tt("dve", dttm.ap, pbd.ap[:, 0:24].rearrange("p (a b) -> p a b", a=4), P["dtb"].ap.unsqueeze(1).to_broadcast([128, 4, 6]), ALU.add, [pbd, P["dtb"]], [dttm])
        self.act(dttm.ap, dttm.ap, AF.Exp, [dttm], [dttm])
        self.act(dttm.ap, dttm.ap, AF.Ln, [dttm], [dttm], bias=self.onesf.ap[:, 0:1])
        self.tt("dve", atm.ap, dttm.ap, P["negA"].ap.unsqueeze(1).to_broadcast([128, 4, 6]), ALU.mult, [dttm, P["negA"]], [atm])
        self.ckpt(33)
        self.arelease(m1)
        for t in range(4):
            self.ssd_chunk(l, t, t * 128, 128, zs, xsc, Bc, Cc, dttm, atm, mixs)
        if last:
            self.ssd_final_outputs(l)
        self.arelease(m0)

    def ssd_chunk(self, l, t, c0, T, zs, xsc, Bc, Cc, dttm, atm, mixs):
        P = self.L[l]
        m = self.amark()
        cs = slice(c0, c0 + T)
        tb = self.bank("t")
        tv = tb.ap.bitcast(BF16)
        for h in range(6):
            self.tr(tv[:, h * 64:(h + 1) * 64], xsc.ap[:, h, cs], self.identb.ap[0:64, 0:64], [xsc, self.identb], [tb])
        xdt = self.aa("xdt", 128, [6, 64], BF16)
        self.tt("dve", xdt.ap, tv[:, 0:384].rearrange("p (h q) -> p h q", h=6), dttm.ap[:, t, :].unsqueeze(2).to_broadcast([128, 6, 64]), ALU.mult, [tb, dttm], [xdt])
        tb2 = self.bank("t")
        tv2 = tb2.ap.bitcast(BF16)
        for g in range(2):
            self.tr(tv2[:, g * 128:(g + 1) * 128], Bc.ap[:, g, cs], self.identb.ap, [Bc, self.identb], [tb2])
        Btm = self.aa("Btm", 128, [2, 128], BF16)
        self.cp("act", Btm.ap, tv2[:, 0:256].rearrange("p (g n) -> p g n", g=2), [tb2], [Btm])
        self.ckpt(34)
        pa = self.bank("s")
        self.mm(pa.ap[:, 0:6], self.trif.ap, atm.ap[:, t, :], [self.trif, atm], [pa])
        acum = self.aa("acum", 128, [6])
        self.cp("dve", acum.ap, pa.ap[:, 0:6], [pa], [acum])
        self.ckpt(35)
        atri = self.aa("atri", 128, [6, 128])
        self.tt("pool", atri.ap, self.trif.ap.unsqueeze(1).to_broadcast([128, 6, 128]), atm.ap[:, t, :].unsqueeze(2).to_broadcast([128, 6, 128]), ALU.mult,
                [self.trif, atm], [atri])
        self.ckpt(41)
        pbc0, pbc1 = self.PB[4], self.PB[5]
        af = atri.ap.rearrange("p h l -> p (h l)")
        self.mm(pbc0.ap[:, 0:512], self.onesf.ap, af[:, 0:512], [self.onesf, atri], [pbc0])
        self.mm(pbc1.ap[:, 0:256], self.onesf.ap, af[:, 512:768], [self.onesf, atri], [pbc1])
        self.ckpt(42)
        abc = self.ps_t[:, 4 * 512:4 * 512 + 768].rearrange("p (h l) -> p h l", h=6)
        EA = self.aa("EA", 128, [6, 128])
        self.act(EA.ap, abc, AF.Exp, [pbc0, pbc1], [EA])
        self.ckpt(43)
        Dm = self.aa("Dm", 128, [6, 128])
        self.tt("dve", Dm.ap[:, 0:4, :], pbc0.ap[:, 0:512].rearrange("p (h l) -> p h l", h=4), acum.ap[:, 0:4].unsqueeze(2).to_broadcast([128, 4, 128]),
                ALU.subtract, [pbc0, acum], [Dm])
        self.tt("dve", Dm.ap[:, 4:6, :], pbc1.ap[:, 0:256].rearrange("p (h l) -> p h l", h=2), acum.ap[:, 4:6].unsqueeze(2).to_broadcast([128, 2, 128]),
                ALU.subtract, [pbc1, acum], [Dm])
        self.ckpt(44)
        dec = self.aa("dec", 128, [6])
        self.ts("dve", Dm.ap, Dm.ap, 0.0, None, ALU.min, None, [Dm], [Dm])
        self.ckpt(45)
        self.act(dec.ap, Dm.ap[:, :, T - 1], AF.Exp, [Dm], [dec])
        self.ckpt(46)
        self.act(Dm.ap, Dm.ap, AF.Exp, [Dm], [Dm])
        self.ckpt(36)
        pg = self.bank("d")
        for g in range(2):
            self.mm(pg.ap[:, g * 128:(g + 1) * 128], Bc.ap[:, g, cs], Cc.ap[:, g, cs], [Bc, Cc], [pg])
        GmT = self.aa("GmT", 128, [2, 128])
        self.tt("dve", GmT.ap, pg.ap[:, 0:256].rearrange("p (g l) -> p g l", g=2), self.trif.ap.unsqueeze(1).to_broadcast([128, 2, 128]), ALU.mult, [pg, self.trif], [GmT])
        Mm = self.aa("Mm", 128, [6, 128], BF16)
        self.tt("dve", Mm.ap.rearrange("p (g j) l -> p g j l", g=2), Dm.ap.rearrange("p (g j) l -> p g j l", g=2),
                GmT.ap.unsqueeze(2).to_broadcast([128, 2, 3, 128]), ALU.mult, [Dm, GmT], [Mm])
        Cs = self.aa("Cs", 128, [6, 128], BF16)
        self.tt("pool", Cs.ap.rearrange("p (g j) l -> p g j l", g=2), EA.ap.rearrange("p (g j) l -> p g j l", g=2),
                Cc.ap[:, :, cs].unsqueeze(2).to_broadcast([128, 2, 3, 128]), ALU.mult, [EA, Cc], [Cs])
        xdtd = self.aa("xdtd", 128, [6, 64], BF16)
        self.tt("pool", xdtd.ap, xdt.ap, dec.ap.unsqueeze(2).to_broadcast([128, 6, 64]), ALU.mult, [xdt, dec], [xdtd])
        self.ckpt(37)
        py0, py1 = self.PB[0], self.PB[1]
        for h in range(6):
            pb = py0 if h < 4 else py1
            o = (h % 4) * 128
            self.mm(pb.ap[0:64, o:o + 128], xdt.ap[:, h, :], Mm.ap[:, h, :], [xdt, Mm], [pb], start=True, stop=False)
            self.mm(pb.ap[0:64, o:o + 128], P["STb"].ap[:, h, :], Cs.ap[:, h, :], [P["STb"], Cs], [pb], start=False, stop=True)
        yv = self.ps_t[0:64, 0:768].rearrange("p (h l) -> p h l", h=6)
        yz = self.aa("yz", 64, [6, 128])
        self.tt("pool", yz.ap, xsc.ap[:, :, cs], P["dskip"].ap.unsqueeze(2).to_broadcast([64, 6, 128]), ALU.mult, [xsc, P["dskip"]], [yz])
        self.tt("dve", yz.ap[:, 0:4, :], yz.ap[:, 0:4, :], py0.ap[0:64, 0:512].rearrange("p (h l) -> p h l", h=4), ALU.add, [yz, py0], [yz])
        self.tt("dve", yz.ap[:, 4:6, :], yz.ap[:, 4:6, :], py1.ap[0:64, 0:256].rearrange("p (h l) -> p h l", h=2), ALU.add, [yz, py1], [yz])
        self.tt("dve", yz.ap, yz.ap, zs.ap[:, :, cs], ALU.mult, [yz, zs], [yz])
        self.ckpt(38)
        self.ssd_gnorm(l, yz, mixs, cs, T)
        self.ckpt(39)
        pst = self.PB[2]
        for h in range(6):
            self.mm(pst.ap[:, h * 64:(h + 1) * 64], Btm.ap[:, h // 3, :], xdtd.ap[:, h, :], [Btm, xdtd], [pst])
        self.tt("dve", P["ST"].ap, P["ST"].ap, EA.ap[:, :, T - 1].unsqueeze(2).to_broadcast([128, 6, 64]), ALU.mult, [P["ST"], EA], [P["ST"]])
        self.tt("dve", P["ST"].ap, P["ST"].ap, pst.ap[:, 0:384].rearrange("p (h q) -> p h q", h=6), ALU.add, [P["ST"], pst], [P["ST"]])
        self.cp("act", P["STb"].ap, P["ST"].ap, [P["ST"]], [P["STb"]])
        self.arelease(m)

    def ssd_gnorm(self, l, yz, mixs, cs, T):
        P = self.L[l]
        m = self.amark()
        sq = self.aa("gsq", 64, [6, T], BF16)
        self.act(sq.ap, yz.ap, AF.Square, [yz], [sq])
        pb = self.bank("s")
        for g in range(2):
            for j in range(3):
                self.mm(pb.ap[0:64, g * T:(g + 1) * T], self.onesb.ap[0:64, 0:64], sq.ap[:, g * 3 + j, :], [self.onesb, sq], [pb], start=(j == 0), stop=(j == 2))
        rs = self.aa("grs", 64, [2, T])
        self.rsqrt(rs.ap, pb.ap[0:64, 0:2 * T].rearrange("p (g l) -> p g l", g=2), 1.0 / 192.0, [pb], [rs])
        self.tt("dve", yz.ap.rearrange("p (g j) l -> p g j l", g=2), yz.ap.rearrange("p (g j) l -> p g j l", g=2),
                rs.ap.unsqueeze(2).to_broadcast([64, 2, 3, T]), ALU.mult, [yz, rs], [yz])
        self.tt("dve", mixs.ap[:, :, cs], yz.ap, P["gssd"].ap.unsqueeze(2).to_broadcast([64, 6, T]), ALU.mult, [yz, P["gssd"]], [mixs])
        self.arelease(m)

    def ssd_final_outputs(self, l):
        P = self.L[l]
        O = self.O
        m = self.amark()
        stg = self.aa("stg", 128, [3, 128])
        for i in range(3):
            tb = self.bank("t")
            self.tr(tb.ap[:, 0:128], P["ST"].ap[:, 2 * i:2 * i + 2, :].rearrange("p a b -> p (a b)"), self.identf.ap, [P["ST"], self.identf], [tb])
            self.cp("act", stg.ap[:, i, :], tb.ap[:, 0:128], [tb], [stg])
        self.store(O["p_ssm"][l].rearrange("(i a) p n -> (a p) i n", a=2), stg.ap, [stg], "st_pssm%d" % l)
        for k in range(3):
            self.store(O["p_conv"][l][k, 0:384].rearrange("(h p) -> p h", p=64), P["hx"].ap[:, :, k], [P["hx"]], "st_pconv%d" % l, slow=True)
            self.store(O["p_conv"][l][k, 384:896].rearrange("(c p) -> p c", p=128), P["hbc"].ap[:, :, k], [P["hbc"]], "st_pconv%d" % l, slow=True)
        self.arelease(m)

    def mla_front(self, l, hT, nb, cos, sin):
        P = self.L[l]
        win = self.I["w_in"][l].rearrange("(c p) m -> p c m", p=128)
        cqn = self.aa("cqn", 128, [2, nb], BF16)
        lat = self.aa("lat", 128, [nb])
        kper = self.aa("kper", 32, [nb])
        m = self.amark()
        wb, wv = self.wload(win[:, :, C_CQ:C_CQ + 416], 128, 8, 416)
        self.ckpt(61)
        cqr = self.aa("cqr", 128, [2, nb])
        sq = self.aa("msq", 128, [nb], BF16)
        ps = self.bank("s")
        for c in range(2):
            pb = self.bank("d")
            for kc in range(8):
                self.mm(pb.ap[:, 0:nb], wv[:, kc, c * 128:(c + 1) * 128], hT.ap[:, kc, :], [wb, hT], [pb], start=(kc == 0), stop=(kc == 7))
            self.cp("dve", cqr.ap[:, c, :], pb.ap[:, 0:nb], [pb], [cqr])
            self.ckpt(62)
            self.act(sq.ap, cqr.ap[:, c, :], AF.Square, [cqr], [sq])
            self.mm(ps.ap[:, 0:nb], self.onesb.ap, sq.ap, [self.onesb, sq], [ps], start=(c == 0), stop=(c == 1))
            self.ckpt(63)
        rs = self.aa("mrs", 128, [nb])
        self.rsqrt(rs.ap, ps.ap[:, 0:nb], 1.0 / 256.0, [ps], [rs])
        self.ckpt(64)
        for c in range(2):
            self.stt(cqn.ap[:, c, :], cqr.ap[:, c, :], P["gql"].ap[:, c:c + 1], rs.ap, ALU.mult, ALU.mult, [cqr, P["gql"], rs], [cqn])
        self.ckpt(58)
        pb = self.bank("d")
        for kc in range(8):
            self.mm(pb.ap[:, 0:nb], wv[:, kc, 256:384], hT.ap[:, kc, :], [wb, hT], [pb], start=(kc == 0), stop=(kc == 7))
        self.act(sq.ap, pb.ap[:, 0:nb], AF.Square, [pb], [sq])
        ps = self.bank("s")
        self.mm(ps.ap[:, 0:nb], self.onesb.ap, sq.ap, [self.onesb, sq], [ps])
        rs2 = self.aa("mrs2", 128, [nb])
        self.rsqrt(rs2.ap, ps.ap[:, 0:nb], 1.0 / 128.0, [ps], [rs2])
        self.stt(lat.ap, pb.ap[:, 0:nb], P["gkv"].ap[:, 0:1], rs2.ap, ALU.mult, ALU.mult, [pb, P["gkv"], rs2], [lat])
        self.ckpt(59)
        pa = self.bank("d")
        for kc in range(8):
            self.mm(pa.ap[0:32, 0:nb], wv[:, kc, 384:416], hT.ap[:, kc, :], [wb, hT], [pa], start=(kc == 0), stop=(kc == 7))
        pr = self.bank("d")
        for kc in range(8):
            self.mm(pr.ap[0:32, 0:nb], P["wkr"].ap[:, kc, :], hT.ap[:, kc, :], [P["wkr"], hT], [pr], start=(kc == 0), stop=(kc == 7))
        t1 = self.aa("kt1", 32, [nb])
        self.tt("dve", t1.ap, pa.ap[0:32, 0:nb], cos.ap[0:32, :], ALU.mult, [pa, cos], [t1])
        self.tt("dve", kper.ap, pr.ap[0:32, 0:nb], sin.ap[0:32, :], ALU.mult, [pr, sin], [kper])
        self.tt("dve", kper.ap, kper.ap, t1.ap, ALU.add, [kper, t1], [kper])
        self.ckpt(60)
        self.arelease(m)
        return cqn, lat, kper

    def mla_q_head(self, l, h, cqn, nb, cos, sin, PP, qn_ap, qn_buf, qp_ap, qp_buf):
        P = self.L[l]
        m = self.amark()
        pn = self.bank("d")
        for c in range(2):
            self.mm(pn.ap[0:64, 0:nb], P["wqn"].ap[:, c, h, :], cqn.ap[:, c, :], [P["wqn"], cqn], [pn], start=(c == 0), stop=(c == 1))
        pa = self.bank("d")
        for c in range(2):
            self.mm(pa.ap[0:PP, 0:nb], P["wqp4"].ap[:, c, h, 0:PP], cqn.ap[:, c, :], [P["wqp4"], cqn], [pa], start=(c == 0), stop=(c == 1))
        pr = self.bank("d")
        for c in range(2):
            self.mm(pr.ap[0:PP, 0:nb], P["wqr4"].ap[:, c, h, 0:PP], cqn.ap[:, c, :], [P["wqr4"], cqn], [pr], start=(c == 0), stop=(c == 1))
        qpf = self.aa("qpf", PP, [nb])
        t1 = self.aa("qt1", PP, [nb])
        self.tt("dve", t1.ap, pa.ap[0:PP, 0:nb], cos.ap[0:PP, :], ALU.mult, [pa, cos], [t1])
        self.tt("dve", qpf.ap, pr.ap[0:PP, 0:nb], sin.ap[0:PP, :], ALU.mult, [pr, sin], [qpf])
        self.tt("dve", qpf.ap, qpf.ap, t1.ap, ALU.add, [qpf, t1], [qpf])
        sqn = self.aa("sqn", 64, [nb], BF16)
        sqp = self.aa("sqp", 32, [nb], BF16)
        self.act(sqn.ap, pn.ap[0:64, 0:nb], AF.Square, [pn], [sqn])
        self.act(sqp.ap, qpf.ap[0:32, :], AF.Square, [qpf], [sqp])
        MP = max(PP, 64)
        ps = self.bank("s")
        self.mm(ps.ap[0:MP, 0:nb], self.onesb.ap[0:64, 0:MP], sqn.ap, [self.onesb, sqn], [ps], start=True, stop=False)
        self.mm(ps.ap[0:MP, 0:nb], self.onesb.ap[0:32, 0:MP], sqp.ap, [self.onesb, sqp], [ps], start=False, stop=True)
        rs = self.aa("qrs", MP, [nb])
        self.rsqrt(rs.ap, ps.ap[0:MP, 0:nb], 1.0 / 96.0, [ps], [rs])
        self.stt(qn_ap, pn.ap[0:64, 0:nb], P["gqkn"].ap[:, 0:1], rs.ap[0:64, :], ALU.mult, ALU.mult, [pn, P["gqkn"], rs], [qn_buf])
        self.stt(qp_ap, qpf.ap, P["gqkp"].ap[0:PP, 0:1], rs.ap[0:PP, :], ALU.mult, ALU.mult, [qpf, P["gqkp"], rs], [qp_buf])
        self.arelease(m)

    def mla_prompt(self, l, blk, hT, cos, sin, mixm):
        P = self.L[l]
        O = self.O
        nb = NB
        m0 = self.amark()
        cqn, lat, kper = self.mla_front(l, hT, nb, cos, sin)
        c0 = blk * NB
        self.cp("act", P["latT"].ap[:, c0:c0 + nb], lat.ap, [lat], [P["latT"]])
        self.cp("act", P["kpeT"].ap[:, c0:c0 + nb], kper.ap, [kper], [P["kpeT"]])
        self.ckpt(51)
        for t in range(4):
            j = blk * 4 + t
            cs = slice(t * 128, (t + 1) * 128)
            m1 = self.amark()
            lt = self.aa("lt", 128, [128])
            self.fm2tm(lt.ap, lt, lat.ap[:, cs], lat, 128, 128)
            self.store(O["p_lat"][l, c0 + t * 128:c0 + (t + 1) * 128, :], lt.ap, [lt], "st_lt", )
            kt = self.aa("kt", 128, [32])
            self.fm2tm(kt.ap, kt, kper.ap[:, cs], kper, 32, 128)
            self.store(O["p_kpe"][l, c0 + t * 128:c0 + (t + 1) * 128, :], kt.ap, [kt], "st_kt")
            self.ckpt(52)
            pk = self.bank("d")
            self.mm(pk.ap[:, 0:384], P["latT"].ap[:, c0 + t * 128:c0 + (t + 1) * 128], P["wk"].ap, [P["latT"], P["wk"]], [pk])
            ksq = self.aa("ksq", 128, [384], BF16)
            self.act(ksq.ap, pk.ap[:, 0:384], AF.Square, [pk], [ksq])
            ss = self.aa("kss", 128, [8])
            self.red(ss.ap[:, 0:6], ksq.ap.rearrange("p (h d) -> p h d", h=6), ALU.add, [ksq], [ss])
            kq = self.aa("kq", 128, [32])
            self.tt("dve", kq.ap, kt.ap, kt.ap, ALU.mult, [kt], [kq])
            self.red(ss.ap[:, 6:7], kq.ap, ALU.add, [kq], [ss])
            self.ts("dve", ss.ap[:, 0:6], ss.ap[:, 0:6], ss.ap[:, 6:7], None, ALU.add, None, [ss], [ss])
            self.rsqrt(P["rks"].ap[:, j, :], ss.ap[:, 0:6], 1.0 / 96.0, [ss], [P["rks"]])
            self.ts("dve", P["rks"].ap[:, j, :], P["rks"].ap[:, j, :], ATTN_SCALE, None, ALU.mult, None, [P["rks"]], [P["rks"]])
            self.ckpt(53)
            pv = self.bank("d")
            self.mm(pv.ap[:, 0:384], P["latT"].ap[:, c0 + t * 128:c0 + (t + 1) * 128], P["wv"].ap, [P["latT"], P["wv"]], [pv])
            self.cp("act", P["vaug"].ap[:, j, :, 0:64], pv.ap[:, 0:384].rearrange("p (h d) -> p h d", h=6), [pv], [P["vaug"]])
            self.ckpt(54)
            self.arelease(m1)
        qabs = self.aa("qabs", 128, [6, nb], BF16)
        qpa = self.aa("qpa", 32, [6, nb], BF16)
        for h in range(6):
            m1 = self.amark()
            qn = self.aa("qn", 64, [nb], BF16)
            self.mla_q_head(l, h, cqn, nb, cos, sin, 32, qn.ap, qn, qpa.ap[:, h, :], qpa)
            pq = self.bank("s")
            self.mm(pq.ap[:, 0:nb], P["wkT"].ap[:, h, :], qn.ap, [P["wkT"], qn], [pq])
            self.cp("act", qabs.ap[:, h, :], pq.ap[:, 0:nb], [pq], [qabs])
            self.arelease(m1)
        self.ckpt(55)
        OB = [self.PB[0], self.PB[1], self.PB[2], self.PB[3]]
        first = [True] * 4
        njt = blk * 4 + 4
        PTs = [self.aa("PT", 128, [nb], BF16) for _ in range(3)]
        pi = 0
        for h in range(6):
            for j in range(njt):
                tq = max(0, j - blk * 4)
                q0 = tq * 128
                w = nb - q0
                ps = self.PB[4 + (pi % 4)]
                self.mm(ps.ap[:, 0:w], P["latT"].ap[:, j * 128:(j + 1) * 128], qabs.ap[:, h, q0:nb], [P["latT"], qabs], [ps], start=True, stop=False)
                self.mm(ps.ap[:, 0:w], P["kpeT"].ap[:, j * 128:(j + 1) * 128], qpa.ap[:, h, q0:nb], [P["kpeT"], qpa], [ps], start=False, stop=True)
                PT = PTs[pi % 3]
                pi += 1
                self.act(PT.ap[:, 0:w], ps.ap[:, 0:w], AF.Exp, [ps, P["rks"], P["negM"]], [PT], scale=P["rks"].ap[:, j, h:h + 1], bias=P["negM"].ap[:, 0:1])
                if j >= blk * 4:
                    self.tt("pool", PT.ap[:, 0:128], PT.ap[:, 0:128], self.trib.ap, ALU.mult, [PT, self.trib], [PT])
                for t in range(tq, 4):
                    o = (t - tq) * 128
                    self.mm(OB[t].ap[:, h * 66:(h + 1) * 66], PT.ap[:, o:o + 128], P["vaug"].ap[:, j, h, :], [PT, P["vaug"]], [OB[t]],
                            start=first[t], stop=(h == 5 and j == blk * 4 + t), sgc=True)
                    first[t] = False
        self.ckpt(56)
        for t in range(4):
            m1 = self.amark()
            ov = OB[t].ap[:, 0:396].rearrange("p (h d) -> p h d", h=6)
            rinv = self.aa("rinv", 128, [6, 1])
            self.recip(rinv.ap, ov[:, :, 64:65], [OB[t]], [rinv])
            on = self.aa("on", 128, [6, 64])
            self.tt("dve", on.ap, ov[:, :, 0:64], rinv.ap.to_broadcast([128, 6, 64]), ALU.mult, [OB[t], rinv], [on])
            junk = self.aa("junk", 128, [384], BF16)
            ssq = self.aa("ssq", 128, [1])
            self.act(junk.ap, on.ap.rearrange("p h d -> p (h d)"), AF.Square, [on], [junk, ssq], accum=ssq.ap)
            self.rsqrt(ssq.ap, ssq.ap, 1.0 / 384.0, [ssq], [ssq])
            mn = self.aa("mn", 128, [384], BF16)
            self.stt(mn.ap, on.ap.rearrange("p h d -> p (h d)"), ssq.ap[:, 0:1], P["gmla_bc"].ap, ALU.mult, ALU.mult, [on, ssq, P["gmla_bc"]], [mn])
            tb = self.bank("t")
            tv = tb.ap.bitcast(BF16)
            for c in range(3):
                self.tr(tv[:, c * 128:(c + 1) * 128], mn.ap[:, c * 128:(c + 1) * 128], self.identb.ap, [mn, self.identb], [tb])
            self.cp("act", mixm.ap[:, :, t * 128:(t + 1) * 128], tv[:, 0:384].rearrange("p (c q) -> p c q", c=3), [tb], [mixm])
            self.arelease(m1)
        self.arelease(m0)

    def pool_prompt(self, l, blk, hT, mixp, last):
        P = self.L[l]
        nb = NB
        win = self.I["w_in"][l].rearrange("(c p) m -> p c m", p=128)
        m0 = self.amark()
        LW = 15 + nb
        up = self.aa("up", 128, [2, LW])
        wa = self.aa("pwa", 128, [2, LW])
        wb_ = self.aa("pwb", 128, [2, LW])
        pooled = self.aa("pooled", 128, [2, nb], BF16)
        self.memset("pool", wa.ap, 0.0, [wa])
        self.memset("pool", wb_.ap, 0.0, [wb_])
        wb, wv = self.wload(win[:, :, C_U:C_U + 256], 128, 8, 256)
        self.cp("dve", up.ap[:, :, 0:15], P["hu"].ap, [P["hu"]], [up])
        for c in range(2):
            pb = self.bank("d")
            for kc in range(8):
                self.mm(pb.ap[:, 0:nb], wv[:, kc, c * 128:(c + 1) * 128], hT.ap[:, kc, :], [wb, hT], [pb], start=(kc == 0), stop=(kc == 7))
            self.cp("act", up.ap[:, c, 15:LW], pb.ap[:, 0:nb], [pb], [up])
        self.cp("dve", P["hu"].ap, up.ap[:, :, nb:LW], [up], [P["hu"]])
        src = up
        dsts = [wa, wb_, wa, wb_]
        grp = [(0, 0), (0, 64), (1, 0), (1, 64)]
        for g in range(4):
            sh = 1 << g
            dst = dsts[g]
            self.tt("pool", dst.ap[:, :, sh:LW], src.ap[:, :, sh:LW], src.ap[:, :, 0:LW - sh], ALU.add, [src], [dst])
            c, po = grp[g]
            self.stt(pooled.ap[po:po + 64, c, :], dst.ap[po:po + 64, c, 15:LW], self.kc.ap[po:po + 64, 4 + c:5 + c], up.ap[po:po + 64, c, 15:LW],
                     ALU.mult, ALU.subtract, [dst, self.kc, up], [pooled])
            if blk == 0:
                tmp = self.aa("ptmp", 128, [16])
                self.tt("dve", tmp.ap[po:po + 64, :], dst.ap[po:po + 64, c, 15:31], self.rct.ap[po:po + 64, c, :], ALU.mult, [dst, self.rct], [tmp])
                self.tt("dve", pooled.ap[po:po + 64, c, 0:16], tmp.ap[po:po + 64, :], up.ap[po:po + 64, c, 15:31], ALU.subtract, [tmp, up], [pooled])
            src = dst
        for c in range(2):
            pb = self.bank("d")
            self.mm(pb.ap[:, 0:nb], P["wpool"].ap[:, c, :], pooled.ap[:, c, :], [P["wpool"], pooled], [pb])
            self.ts("dve", mixp.ap[:, c, :], pb.ap[:, 0:nb], P["pscale"].ap[:, c:c + 1], None, ALU.mult, None, [pb, P["pscale"]], [mixp])
        if last:
            for c in range(2):
                self.store(self.O["p_pool"][l][:, c * 128:(c + 1) * 128].rearrange("t p -> p t"), P["hu"].ap[:, c, :], [P["hu"]], "st_ppool%d" % l, slow=True)
        self.arelease(m0)

    def out_proj(self, l, xT, mixs, mixm, mixp, nb, sample):
        P = self.L[l]
        wo = self.I["w_out"][l]
        gate = P["mod"]
        for half in range(2):
            cs_ = slice(half * 512, (half + 1) * 512)
            wbA, wvA = self.wload(wo[0:384, cs_].rearrange("(h p) m -> p h m", p=64), 64, 6, 512)
            if sample:
                wbA2, wvA2 = self.wload(wo[384:768, cs_].rearrange("(h p) m -> p h m", p=64), 64, 6, 512)
                wbB, wvB = self.wload(wo[768:D, cs_].rearrange("(c p) m -> p c m", p=128), 128, 2, 512)
            else:
                wbB, wvB = self.wload(wo[384:D, cs_].rearrange("(c p) m -> p c m", p=128), 128, 5, 512)
            for mi in range(4):
                mc = half * 4 + mi
                ms = slice(mi * 128, (mi + 1) * 128)
                pb = self.bank("d")
                ops = []
                for h in range(6):
                    ops.append((wvA[:, h, ms], wbA, mixs.ap[:, h, :], mixs))
                if sample:
                    for h in range(6):
                        ops.append((wvA2[:, h, ms], wbA2, mixm.ap[:, h, :], mixm))
                    for c in range(2):
                        ops.append((wvB[:, c, ms], wbB, mixp.ap[:, c, :], mixp))
                else:
                    for c in range(3):
                        ops.append((wvB[:, c, ms], wbB, mixm.ap[:, c, :], mixm))
                    for c in range(2):
                        ops.append((wvB[:, 3 + c, ms], wbB, mixp.ap[:, c, :], mixp))
                for i, (lh, lb, rh, rb) in enumerate(ops):
                    self.mm(pb.ap[:, 0:nb], lh, rh, [lb, rb], [pb], start=(i == 0), stop=(i == len(ops) - 1))
                self.resid(xT, mc, pb, gate, 16 + mc, nb, sample)

    def resid(self, xT, mc, pb, gate, gi, nb, sample):
        if not sample:
            self.stt(xT.ap[:, mc, :], pb.ap[:, 0:nb], gate.ap[:, gi, 0:1], xT.ap[:, mc, :], ALU.mult, ALU.add, [pb, gate, xT], [xT])
        else:
            m = self.amark()
            t = self.aa("rtmp", 128, [nb])
            self.tt("dve", t.ap, pb.ap[:, 0:nb], gate.ap[:, gi, 1:17], ALU.mult, [pb, gate], [t])
            self.tt("dve", xT.ap[:, mc, :], xT.ap[:, mc, :], t.ap, ALU.add, [xT, t], [xT])
            self.arelease(m)

    def ffn(self, l, xT, hT, nb, sample):
        P = self.L[l]
        I = self.I
        m0 = self.amark()
        hid = self.aa("hid", 128, [NFC, nb], BF16)
        sgs = [self.aa("sg", 128, [nb]) for _ in range(2)]
        wg = I["w_gate"][l].rearrange("(c p) m -> p c m", p=128)
        wu = I["w_up"][l].rearrange("(c p) m -> p c m", p=128)
        fi = 0
        for t in range(6):
            ncol = 512 if t < 5 else 256
            wbg, wvg = self.wload(wg[:, :, t * 512:t * 512 + ncol], 128, 8, ncol)
            wbu, wvu = self.wload(wu[:, :, t * 512:t * 512 + ncol], 128, 8, ncol)
            for i in range(ncol // 128):
                fc = t * 4 + i
                pg = self.bank("d")
                for kc in range(8):
                    self.mm(pg.ap[:, 0:nb], wvg[:, kc, i * 128:(i + 1) * 128], hT.ap[:, kc, :], [wbg, hT], [pg], start=(kc == 0), stop=(kc == 7))
                pu = self.bank("d")
                for kc in range(8):
                    self.mm(pu.ap[:, 0:nb], wvu[:, kc, i * 128:(i + 1) * 128], hT.ap[:, kc, :], [wbu, hT], [pu], start=(kc == 0), stop=(kc == 7))
                sg = sgs[fi % 2]
                fi += 1
                self.act(sg.ap, pg.ap[:, 0:nb], AF.Silu, [pg], [sg])
                self.tt("dve", hid.ap[:, fc, :], sg.ap, pu.ap[:, 0:nb], ALU.mult, [sg, pu], [hid])
        wd = I["w_down"][l].rearrange("(c p) m -> p c m", p=128)
        for mg in range(4):
            pbs = [self.bank("d"), self.bank("d")]
            for kh in range(2):
                wb, wv = self.wload(wd[:, kh * 11:(kh + 1) * 11, mg * 256:(mg + 1) * 256], 128, 11, 256)
                for mi in range(2):
                    for k in range(11):
                        self.mm(pbs[mi].ap[:, 0:nb], wv[:, k, mi * 128:(mi + 1) * 128], hid.ap[:, kh * 11 + k, :], [wb, hid], [pbs[mi]],
                                start=(kh == 0 and k == 0), stop=(kh == 1 and k == 10))
            for mi in range(2):
                mc = mg * 2 + mi
                self.resid(xT, mc, pbs[mi], P["mod"], 40 + mc, nb, sample)
        self.arelease(m0)

    def prompt_block(self, blk):
        I, O = self.I, self.O
        nb = NB
        c0 = blk * NB
        last = (blk == NBLK - 1)
        m0 = self.amark()
        xT = self.aa("xT", 128, [8, nb])
        hT = self.aa("hT", 128, [8, nb], BF16)
        mx = self.amark()
        xins = [self.aa("xin", 128, [D]) for _ in range(2)]
        for t in range(4):
            xin = xins[t % 2]
            self.dma("sp", xin.ap, I["xp"][c0 + t * 128:c0 + (t + 1) * 128, :], (), [xin], "xin%d" % (t % 2))
            for half in range(2):
                tb = self.bank("t")
                for i in range(4):
                    c = half * 4 + i
                    self.tr(tb.ap[:, i * 128:(i + 1) * 128], xin.ap[:, c * 128:(c + 1) * 128], self.identf.ap, [xin, self.identf], [tb])
                self.cp("act" if half else "dve", xT.ap[:, half * 4:half * 4 + 4, t * 128:(t + 1) * 128], tb.ap.rearrange("p (c q) -> p c q", c=4), [tb], [xT])
        self.arelease(mx)
        self.ckpt(20)
        cos, sin = self.rope_tables(c0, 1, nb)
        self.ckpt(21)
        for l in range(DEPTH):
            P = self.L[l]
            m1 = self.amark()
            mixs = self.aa("mixs", 64, [6, nb], BF16)
            mixm = self.aa("mixm", 128, [3, nb], BF16)
            mixp = self.aa("mixp", 128, [2, nb], BF16)
            self.norm_mod(xT, hT, P["A1"], P["mod"], 0, nb, False)
            self.ckpt(22)
            self.ssd_prompt(l, hT, mixs, last)
            self.ckpt(23)
            self.mla_prompt(l, blk, hT, cos, sin, mixm)
            self.ckpt(24)
            self.pool_prompt(l, blk, hT, mixp, last)
            self.ckpt(25)
            self.out_proj(l, xT, mixs, mixm, mixp, nb, False)
            self.ckpt(26)
            self.norm_mod(xT, hT, P["A2"], P["mod"], 24, nb, False)
            self.ckpt(27)
            self.ffn(l, xT, hT, nb, False)
            self.ckpt(28)
            self.arelease(m1)
        youts = [self.aa("yout", 128, [D]) for _ in range(2)]
        for t in range(4):
            yo = youts[t % 2]
            for half in range(2):
                tb = self.bank("t")
                for i in range(4):
                    c = half * 4 + i
                    self.tr(tb.ap[:, i * 128:(i + 1) * 128], xT.ap[:, c, t * 128:(t + 1) * 128], self.identf.ap, [xT, self.identf], [tb])
                self.cp("act" if half else "dve", yo.ap[:, half * 512:(half + 1) * 512], tb.ap, [tb], [yo])
            self.store(O["y_p"][c0 + t * 128:c0 + (t + 1) * 128, :], yo.ap, [yo], "st_y%d" % (t % 2))
        self.arelease(m0)

    def ssd_sample(self, l, hT, mixs):
        P = self.L[l]
        I, O = self.I, self.O
        nb = NS
        win = I["w_in"][l].rearrange("(c p) m -> p c m", p=128)
        m0 = self.amark()
        zs = self.aa("zs", 64, [6, nb], BF16)
        xsc = self.aa("xsc", 64, [6, nb])
        BCc = self.aa("BCc", 128, [4, nb])
        xnx = self.aa("xnx", 64, [6, nb])
        xnbc = self.aa("xnbc", 128, [4, nb])
        cvx = self.aa("cvx", 64, [6, 48])
        cvbc = self.aa("cvbc", 128, [4, 48])
        m1 = self.amark()
        cvtm = self.aa("cvtm", 48, [896])
        self.dma("sp", cvtm.ap, I["sconv"][l].rearrange("s k c -> (s k) c"), (), [cvtm], "cvtm")
        for h in range(6):
            tb = self.bank("t")
            self.tr(tb.ap[0:64, 0:48], cvtm.ap[:, h * 64:(h + 1) * 64], self.identf.ap[0:48, 0:48], [cvtm, self.identf], [tb])
            self.cp("act", cvx.ap[:, h, :], tb.ap[0:64, 0:48], [tb], [cvx])
        for ci in range(4):
            tb = self.bank("t")
            self.tr(tb.ap[:, 0:48], cvtm.ap[:, 384 + ci * 128:384 + (ci + 1) * 128], self.identf.ap[0:48, 0:48], [cvtm, self.identf], [tb])
            self.cp("act", cvbc.ap[:, ci, :], tb.ap[:, 0:48], [tb], [cvbc])
        self.arelease(m1)
        wb, wv = self.wload(win[:, :, C_Z:C_Z + 384], 128, 8, 384)
        for h in range(6):
            pb = self.bank("d")
            for kc in range(8):
                self.mm(pb.ap[0:64, 0:nb], wv[:, kc, h * 64:(h + 1) * 64], hT.ap[:, kc, :], [wb, hT], [pb], start=(kc == 0), stop=(kc == 7))
            self.act(zs.ap[:, h, :], pb.ap[0:64, 0:nb], AF.Silu, [pb], [zs])
        acc = self.aa("sacc", 128, [nb])

        def conv1(PP, pb, new_ap, new_buf, cv_ap, cv_buf, w4, wbuf, bcol, bbuf, out_ap, out_buf):
            self.cp("act", new_ap, pb.ap[0:PP, 0:nb], [pb], [new_buf])
            self.ts("dve", acc.ap[0:PP, :], new_ap, w4[:, 3:4], bcol, ALU.mult, ALU.add, [new_buf, wbuf, bbuf], [acc])
            cvv = cv_ap.rearrange("p (s k) -> p s k", k=3)
            for k in range(3):
                self.stt(acc.ap[0:PP, :], cvv[:, :, k], w4[:, k:k + 1], acc.ap[0:PP, :], ALU.mult, ALU.add, [cv_buf, wbuf, acc], [acc])
            self.act(out_ap, acc.ap[0:PP, :], AF.Silu, [acc], [out_buf])
        wb, wv = self.wload(win[:, :, C_XS:C_XS + 512], 128, 8, 512)
        for h in range(6):
            pb = self.bank("d")
            for kc in range(8):
                self.mm(pb.ap[0:64, 0:nb], wv[:, kc, h * 64:(h + 1) * 64], hT.ap[:, kc, :], [wb, hT], [pb], start=(kc == 0), stop=(kc == 7))
            conv1(64, pb, xnx.ap[:, h, :], xnx, cvx.ap[:, h, :], cvx, P["cwx"].ap[:, h, :], P["cwx"], P["cbx"].ap[:, h:h + 1], P["cbx"], xsc.ap[:, h, :], xsc)

        def bc1(ci, wb, wv, col0):
            pb = self.bank("d")
            for kc in range(8):
                self.mm(pb.ap[:, 0:nb], wv[:, kc, col0:col0 + 128], hT.ap[:, kc, :], [wb, hT], [pb], start=(kc == 0), stop=(kc == 7))
            conv1(128, pb, xnbc.ap[:, ci, :], xnbc, cvbc.ap[:, ci, :], cvbc, P["cwbc"].ap[:, ci, :], P["cwbc"], P["cbbc"].ap[:, ci:ci + 1], P["cbbc"], BCc.ap[:, ci, :], BCc)
        bc1(0, wb, wv, 384)
        wb, wv = self.wload(win[:, :, 896:1286], 128, 8, 390)
        bc1(1, wb, wv, 0)
        bc1(2, wb, wv, 128)
        bc1(3, wb, wv, 256)
        pbd = self.bank("s")
        for kc in range(8):
            self.mm(pbd.ap[0:16, 0:6], hT.ap[:, kc, :], wv[:, kc, 384:390], [wb, hT], [pbd], start=(kc == 0), stop=(kc == 7))
        dtea = self.aa("dtea", 16, [12])
        self.tt("dve", dtea.ap[:, 0:6], pbd.ap[0:16, 0:6], P["dtb"].ap[0:16, :], ALU.add, [pbd, P["dtb"]], [dtea])
        self.act(dtea.ap[:, 0:6], dtea.ap[:, 0:6], AF.Exp, [dtea], [dtea])
        self.act(dtea.ap[:, 0:6], dtea.ap[:, 0:6], AF.Ln, [dtea], [dtea], bias=self.onesf.ap[0:16, 0:1])
        self.tt("dve", dtea.ap[:, 6:12], dtea.ap[:, 0:6], P["negA"].ap[0:16, :], ALU.mult, [dtea, P["negA"]], [dtea])
        self.act(dtea.ap[:, 6:12], dtea.ap[:, 6:12], AF.Exp, [dtea], [dtea])
        ev = self.dma("sp", O["s_conv"][l][:, 0:2, :], I["sconv"][l][:, 1:3, :], (), (), "st_sconv_cp%d" % l)
        self.out_events.append(ev)
        cvnew = self.aa("cvnew", 16, [896])
        for h in range(6):
            self.fm2tm(cvnew.ap[:, h * 64:(h + 1) * 64], cvnew, xnx.ap[:, h, :], xnx, 64, 16)
        for ci in range(4):
            self.fm2tm(cvnew.ap[:, 384 + ci * 128:384 + (ci + 1) * 128], cvnew, xnbc.ap[:, ci, :], xnbc, 128, 16)
        self.store(O["s_conv"][l][:, 2, :], cvnew.ap, [cvnew], "st_sconv%d" % l)
        R = self.aa("Rdt", 16, [16, 12])
        self.tt("dve", R.ap, dtea.ap.unsqueeze(1).to_broadcast([16, 16, 12]), self.identf.ap[0:16, 0:16].unsqueeze(2).to_broadcast([16, 16, 12]), ALU.mult,
                [dtea, self.identf], [R])
        pr = self.bank("s")
        self.mm(pr.ap[0:64, 0:192], self.onesf.ap[0:16, 0:64], R.ap.rearrange("p s j -> p (s j)"), [self.onesf, R], [pr])
        dtb64 = self.aa("dtb64", 64, [16, 12])
        self.cp("dve", dtb64.ap, pr.ap[0:64, 0:192].rearrange("p (s j) -> p s j", s=16), [pr], [dtb64])
        xdt = self.aa("xdts", 64, [16, 6])
        self.tt("dve", xdt.ap, xsc.ap.rearrange("p h s -> p s h"), dtb64.ap[:, :, 0:6], ALU.mult, [xsc, dtb64], [xdt])
        BCtm = self.aa("BCtm", 16, [4, 128])
        for ci in range(4):
            self.fm2tm(BCtm.ap[:, ci, :], BCtm, BCc.ap[:, ci, :], BCc, 128, 16)
        ysm = self.aa("ysm", 64, [16, 6])
        hss = [self.aa("hs", 64, [6, 128]) for _ in range(2)]
        hns = [self.aa("hn", 64, [6, 128]) for _ in range(2)]
        tmp = self.aa("stmp", 64, [6, 128])
        BCd = self.aa("BCd", 16, [512])
        for s in range(NS):
            hs, hn = hss[s % 2], hns[s % 2]
            self.dma("sp", hs.ap, I["sssm"][l, s].rearrange("h p n -> p h n"), (), [hs], "hs%d" % (s % 2))
            self.ts("dve", BCd.ap, BCtm.ap.rearrange("p c n -> p (c n)"), self.identf.ap[0:16, s:s + 1], None, ALU.mult, None, [BCtm, self.identf], [BCd])
            pbc = self.bank("d")
            self.mm(pbc.ap[0:64, 0:512], self.onesf.ap[0:16, 0:64], BCd.ap, [self.onesf, BCd], [pbc])
            bv = pbc.ap[0:64, 0:256].rearrange("p (g n) -> p g n", g=2)
            cv = pbc.ap[0:64, 256:512].rearrange("p (g n) -> p g n", g=2)
            self.tt("dve", hn.ap, hs.ap, dtb64.ap[:, s, 6:12].unsqueeze(2).to_broadcast([64, 6, 128]), ALU.mult, [hs, dtb64], [hn])
            self.tt("dve", tmp.ap.rearrange("p (g j) n -> p g j n", g=2), bv.unsqueeze(2).to_broadcast([64, 2, 3, 128]),
                    xdt.ap[:, s, :].rearrange("p (g j) -> p g j", g=2).unsqueeze(3).to_broadcast([64, 2, 3, 128]), ALU.mult, [pbc, xdt], [tmp])
            self.tt("dve", hn.ap, hn.ap, tmp.ap, ALU.add, [hn, tmp], [hn])
            self.store(O["s_ssm"][l, s].rearrange("h p n -> p h n"), hn.ap, [hn], "st_hn%d" % (s % 2))
            self.tt("dve", tmp.ap.rearrange("p (g j) n -> p g j n", g=2), hn.ap.rearrange("p (g j) n -> p g j n", g=2),
                    cv.unsqueeze(2).to_broadcast([64, 2, 3, 128]), ALU.mult, [hn, pbc], [tmp])
            self.red(ysm.ap[:, s, :], tmp.ap, ALU.add, [tmp], [ysm])
        yz = self.aa("yzs", 64, [6, nb])
        self.tt("dve", yz.ap, xsc.ap, P["dskip"].ap.unsqueeze(2).to_broadcast([64, 6, nb]), ALU.mult, [xsc, P["dskip"]], [yz])
        self.tt("dve", yz.ap, yz.ap, ysm.ap.rearrange("p s h -> p h s"), ALU.add, [yz, ysm], [yz])
        self.tt("dve", yz.ap, yz.ap, zs.ap, ALU.mult, [yz, zs], [yz])
        self.ssd_gnorm(l, yz, mixs, slice(0, nb), nb)
        self.arelease(m0)

    def pool_sample(self, l, hT, mixp):
        P = self.L[l]
        I, O = self.I, self.O
        nb = NS
        win = I["w_in"][l].rearrange("(c p) m -> p c m", p=128)
        m0 = self.amark()
        upad = self.aa("upads", 128, [2, 16, 16])
        unew = self.aa("unew", 128, [2, nb])
        pooled = self.aa("pooleds", 128, [2, nb], BF16)
        for i in range(2):
            ptm = self.aa("ptm%d" % i, 120, [256])
            self.dma("sp", ptm.ap, I["spool"][l, 8 * i:8 * i + 8].rearrange("s t c -> (s t) c"), (), [ptm], "ptm%d" % i)
            for c in range(2):
                tb = self.bank("t")
                self.tr(tb.ap[:, 0:120], ptm.ap[:, c * 128:(c + 1) * 128], self.identf.ap[0:120, 0:120], [ptm, self.identf], [tb])
                self.cp("act", upad.ap[:, c, 8 * i:8 * i + 8, 0:15], tb.ap[:, 0:120].rearrange("p (s t) -> p s t", s=8), [tb], [upad])
        wb, wv = self.wload(win[:, :, C_U:C_U + 256], 128, 8, 256)
        for c in range(2):
            pb = self.bank("d")
            for kc in range(8):
                self.mm(pb.ap[:, 0:nb], wv[:, kc, c * 128:(c + 1) * 128], hT.ap[:, kc, :], [wb, hT], [pb], start=(kc == 0), stop=(kc == 7))
            self.cp("act", unew.ap[:, c, :], pb.ap[:, 0:nb], [pb], [unew])
            self.cp("dve", upad.ap[:, c, :, 15], unew.ap[:, c, :], [unew], [upad])
        W = self.aa("Wsum", 128, [nb])
        grp = [(0, 0), (0, 64), (1, 0), (1, 64)]
        for g in range(4):
            w = 2 << g
            c, po = grp[g]
            self.red(W.ap[po:po + 64, :], upad.ap[po:po + 64, c, :, 16 - w:16], ALU.add, [upad], [W])
            self.stt(pooled.ap[po:po + 64, c, :], W.ap[po:po + 64, :], self.kc.ap[po:po + 64, 4 + c:5 + c], unew.ap[po:po + 64, c, :],
                     ALU.mult, ALU.subtract, [W, self.kc, unew], [pooled])
        for c in range(2):
            pb = self.bank("d")
            self.mm(pb.ap[:, 0:nb], P["wpool"].ap[:, c, :], pooled.ap[:, c, :], [P["wpool"], pooled], [pb])
            self.ts("dve", mixp.ap[:, c, :], pb.ap[:, 0:nb], P["pscale"].ap[:, c:c + 1], None, ALU.mult, None, [pb, P["pscale"]], [mixp])
        ev = self.dma("sp", O["s_pool"][l][:, 0:14, :], I["spool"][l][:, 1:15, :], (), (), "st_spool_cp%d" % l)
        self.out_events.append(ev)
        ut = self.aa("ut", 16, [256])
        for c in range(2):
            self.fm2tm(ut.ap[:, c * 128:(c + 1) * 128], ut, unew.ap[:, c, :], unew, 128, 16)
        self.store(O["s_pool"][l][:, 14, :], ut.ap, [ut], "st_spool%d" % l)
        self.arelease(m0)

    def mla_sample(self, l, hT, cos, sin, mixm):
        P = self.L[l]
        I, O = self.I, self.O
        nb = NS
        m0 = self.amark()
        cqn, lat, kper = self.mla_front(l, hT, nb, cos, sin)
        latnewT = self.aa("latnewT", 128, [128], BF16)
        latnew_tm = self.aa("latnew_tm", 128, [128], BF16)
        kpenewT4 = self.aa("kpenewT4", 128, [128], BF16)
        sspnew = self.aa("sspnew", 128, [1])
        for b in (latnewT, latnew_tm, kpenewT4, sspnew):
            self.memset("pool", b.ap, 0.0, [b])
        self.cp("act", latnewT.ap[:, 0:nb], lat.ap, [lat], [latnewT])
        self.cp("act", kpenewT4.ap[0:32, 0:nb], kper.ap, [kper], [kpenewT4])
        lnt = self.aa("lnt", 16, [128])
        self.fm2tm(lnt.ap, lnt, lat.ap, lat, 128, 16)
        self.store(O["s_lat"][l], lnt.ap, [lnt], "st_slat%d" % l)
        self.cp("dve", latnew_tm.ap[0:16, :], lnt.ap, [lnt], [latnew_tm])
        knt = self.aa("knt", 16, [32])
        self.fm2tm(knt.ap, knt, kper.ap, kper, 32, 16)
        self.store(O["s_kpe"][l], knt.ap, [knt], "st_skpe%d" % l)
        ksq0 = self.aa("ksq0", 16, [32])
        self.tt("dve", ksq0.ap, knt.ap, knt.ap, ALU.mult, [knt], [ksq0])
        self.red(sspnew.ap[0:16, :], ksq0.ap, ALU.add, [ksq0], [sspnew])
        qp4 = self.aa("qp4", 128, [6, nb], BF16)
        qabs = self.aa("qabss", 128, [nb, 6], BF16)
        for h in range(6):
            m1 = self.amark()
            qn = self.aa("qn", 64, [nb], BF16)
            self.mla_q_head(l, h, cqn, nb, cos, sin, 128, qn.ap, qn, qp4.ap[:, h, :], qp4)
            pq = self.bank("s")
            self.mm(pq.ap[:, 0:nb], P["wkT"].ap[:, h, :], qn.ap, [P["wkT"], qn], [pq])
            self.cp("act", qabs.ap[:, :, h], pq.ap[:, 0:nb], [pq], [qabs])
            self.arelease(m1)
        qpad = self.aa("qpad", 128, [nb, 4, 6], BF16)
        for i in range(4):
            self.ts("dve", qpad.ap[:, :, i, :], qp4.ap.rearrange("p h s -> p s h"), self.bmask.ap[:, i:i + 1], None, ALU.mult, None, [qp4, self.bmask], [qpad])
        rhsb = [self.aa("rhsb", 128, [390], BF16) for _ in range(2)]
        for b in rhsb:
            self.cp("dve", b.ap[:, 0:384], P["wk"].ap, [P["wk"]], [b])
        cl8 = I["cl"].rearrange("l n (c a) r -> (l n c) (a r)", c=8)
        ck8 = I["ck"].rearrange("l n (c a) r -> (l n c) (a r)", c=8)
        latcs = [self.aa("latc", 128, [16, 128], BF16) for _ in range(2)]
        kpecs = [self.aa("kpec", 128, [16, 32], BF16) for _ in range(2)]
        latT4s = [self.aa("latT4", 128, [512], BF16) for _ in range(2)]
        kpeT4s = [self.aa("kpeT4", 128, [128], BF16) for _ in range(2)]
        ksqbs = [self.aa("ksqb", 128, [4, 384], BF16) for _ in range(2)]
        ssns = [self.aa("ssn", 128, [4, 6]) for _ in range(2)]
        scs = [self.aa("scs", 128, [4, 6]) for _ in range(2)]
        pbf = [self.aa("pbf", 128, [4, 6], BF16) for _ in range(2)]
        ssps = [self.aa("ssp", 128, [16]) for _ in range(2)]
        kq = self.aa("kqs", 128, [16, 32])
        oacc = self.PB[0]
        po = self.PB[1]
        PK = [self.PB[2], self.PB[3], self.PB[4], self.PB[5]]
        state = {"g": 0, "first": True}

        def front(G):
            gi = state["g"]
            state["g"] += 1
            G["gi"] = gi
            s, nt = G["s"], G["nt"]
            if G.get("pre") is not None:
                G["pre"]()
            rb = rhsb[s % 2]
            for i in range(nt):
                pk = PK[i]
                lh, lbufs = G["lat_lhsT"](i)
                kp_ap, kp_buf = G["kpeT4"]()
                self.mm(pk.ap[:, 384:390], kp_ap, qpad.ap[:, s, G["kidx"](i), :], [kp_buf, qpad], [pk], start=True, stop=False, sgc=True)
                self.mm(pk.ap[:, 0:390], lh, rb.ap, lbufs + [rb], [pk], start=False, stop=True, sgc=True)
            kv = self.ps_t[:, 2 * 512:(2 + nt) * 512].rearrange("p (a b) -> p a b", a=nt)
            ksqb, sc = ksqbs[gi % 2], scs[gi % 2]
            self.act(ksqb.ap[:, 0:nt, :], kv[:, :, 0:384], AF.Square, PK[0:nt], [ksqb])
            self.cp("act", sc.ap[:, 0:nt, :], kv[:, :, 384:390], PK[0:nt], [sc])

        def tail(G):
            gi = G["gi"]
            s, nt = G["s"], G["nt"]
            ksqb, sc, ssn = ksqbs[gi % 2], scs[gi % 2], ssns[gi % 2]
            ssp_ap, ssp_buf = G["ssp"]
            self.red(ssn.ap[:, 0:nt, :].rearrange("p a h -> p (a h)"), ksqb.ap[:, 0:nt, :].rearrange("p a (h d) -> p (a h) d", h=6), ALU.add, [ksqb], [ssn])
            self.tt("dve", ssn.ap[:, 0:nt, :], ssn.ap[:, 0:nt, :], ssp_ap.unsqueeze(2).to_broadcast([128, nt, 6]), ALU.add, [ssn, ssp_buf], [ssn])
            self.rsqrt(ssn.ap[:, 0:nt, :], ssn.ap[:, 0:nt, :], 1.0 / 96.0, [ssn], [ssn])
            self.tt("dve", sc.ap[:, 0:nt, :], sc.ap[:, 0:nt, :], ssn.ap[:, 0:nt, :], ALU.mult, [sc, ssn], [sc])
            pb_ = pbf[gi % 2]
            self.act(pb_.ap[:, 0:nt, :], sc.ap[:, 0:nt, :], AF.Exp, [sc, P["negM"]], [pb_], scale=ATTN_SCALE, bias=P["negM"].ap[:, 0:1])
            onehot = G.get("onehot")
            if onehot is not None:
                self.ts("dve", pb_.ap[:, 0:nt, :], pb_.ap[:, 0:nt, :], onehot, None, ALU.mult, None, [pb_, self.identf], [pb_])
            for i in range(nt):
                rh, rbuf = G["pv_rhs"](i)
                self.mm(oacc.ap[0:6, 0:128], pb_.ap[:, i, :], rh, [pb_, rbuf], [oacc], start=state["first"], stop=False, sgc=True)
                state["first"] = False
                self.mm(oacc.ap[0:6, 128:130], pb_.ap[:, i, :], self.onesb.ap[:, 0:2], [pb_, self.onesb], [oacc], start=False,
                        stop=(onehot is not None and i == nt - 1), sgc=True)

        ci_ = 0
        for s in range(NS):
            rb = rhsb[s % 2]
            self.cp("dve", rb.ap[:, 384:390], qabs.ap[:, s, :], [qabs], [rb])
            state["first"] = True
            groups = []
            for c in range(8):
                latc, kpec, ssp = latcs[ci_ % 2], kpecs[ci_ % 2], ssps[ci_ % 2]
                sl = ci_ % 2
                ci_ += 1

                def load_chunk(latc=latc, kpec=kpec, ssp=ssp, s=s, c=c, sl=sl):
                    self.op("pool", lambda e: e.indirect_dma_start(
                        out=latc.ap.rearrange("p a r -> p (a r)"), out_offset=None, in_=cl8[:, :],
                        in_offset=bass.IndirectOffsetOnAxis(ap=self.idx8.ap[:, l, s, c:c + 1], axis=0)), [self.idx8], [latc], "latc%d" % sl)
                    self.op("pool", lambda e: e.indirect_dma_start(
                        out=kpec.ap.rearrange("p a r -> p (a r)"), out_offset=None, in_=ck8[:, :],
                        in_offset=bass.IndirectOffsetOnAxis(ap=self.idx8.ap[:, l, s, c:c + 1], axis=0)), [self.idx8], [kpec], "kpec%d" % sl)
                    self.tt("pool", kq.ap, kpec.ap, kpec.ap, ALU.mult, [kpec], [kq])
                    self.red(ssp.ap, kq.ap, ALU.add, [kq], [ssp])
                for g in range(4):
                    def pre(latc=latc, kpec=kpec, g=g, first_in_chunk=(g == 0), load=load_chunk, G_holder=[]):
                        if first_in_chunk:
                            load()
                    G = dict(s=s, nt=4, kidx=lambda i: i, onehot=None)
                    G["ssp"] = (ssp.ap[:, 4 * g:4 * g + 4], ssp)

                    def mk(latc=latc, kpec=kpec, g=g, G=G, first_in_chunk=(g == 0), load=load_chunk):
                        def pre():
                            if first_in_chunk:
                                load()
                            gi = state["g"]
                            latT4, kpeT4 = latT4s[gi % 2], kpeT4s[gi % 2]
                            tb = self.PB[6]
                            tv = tb.ap.bitcast(BF16)
                            for i in range(4):
                                self.tr(tv[:, i * 128:(i + 1) * 128], latc.ap[:, 4 * g + i, :], self.identb.ap, [latc, self.identb], [tb])
                            self.cp("act", latT4.ap, tv[:, 0:512], [tb], [latT4])
                            tb2 = self.PB[7]
                            tv2 = tb2.ap.bitcast(BF16)
                            self.tr(tv2[:, 0:128], kpec.ap[:, 4 * g:4 * g + 4, :].rearrange("p a r -> p (a r)"), self.identb.ap, [kpec, self.identb], [tb2])
                            self.cp("dve", kpeT4.ap, tv2[:, 0:128], [tb2], [kpeT4])
                            G["lat_lhsT"] = lambda i, latT4=latT4: (latT4.ap[:, i * 128:(i + 1) * 128], [latT4])
                            G["kpeT4"] = lambda kpeT4=kpeT4: (kpeT4.ap, kpeT4)
                        return pre
                    G["pre"] = mk()
                    G["pv_rhs"] = lambda i, latc=latc, g=g: (latc.ap[:, 4 * g + i, :], latc)
                    groups.append(G)
            Gn = dict(s=s, nt=1, kidx=lambda i: 0, onehot=self.identf.ap[:, s:s + 1], pre=None)
            Gn["lat_lhsT"] = lambda i: (latnewT.ap, [latnewT])
            Gn["kpeT4"] = lambda: (kpenewT4.ap, kpenewT4)
            Gn["ssp"] = (sspnew.ap[:, 0:1], sspnew)
            Gn["pv_rhs"] = lambda i: (latnew_tm.ap, latnew_tm)
            groups.append(Gn)
            front(groups[0])
            for gi_ in range(len(groups)):
                if gi_ + 1 < len(groups):
                    front(groups[gi_ + 1])
                tail(groups[gi_])
            m1 = self.amark()
            oa = self.aa("oa", 6, [129])
            self.cp("act", oa.ap, oacc.ap[0:6, 0:129], [oacc], [oa])
            rinv = self.aa("orinv", 6, [1])
            self.recip(rinv.ap, oa.ap[:, 128:129], [oa], [rinv])
            ol = self.aa("ol", 6, [128])
            self.ts("dve", ol.ap, oa.ap[:, 0:128], rinv.ap[:, 0:1], None, ALU.mult, None, [oa, rinv], [ol])
            tb = self.PB[6]
            self.tr(tb.ap[:, 0:6], ol.ap, self.identf.ap[0:6, 0:6], [ol, self.identf], [tb])
            olT = self.aa("olT", 128, [8], BF16)
            self.memset("pool", olT.ap, 0.0, [olT])
            self.cp("act", olT.ap[:, 0:6], tb.ap[:, 0:6], [tb], [olT])
            for h in range(6):
                self.mm(po.ap[0:64, h * 18 + s:h * 18 + s + 2], P["wv"].ap[:, h * 64:(h + 1) * 64], olT.ap[:, h:h + 2], [P["wv"], olT], [po], start=True, stop=True, sgc=True)
            self.arelease(m1)
        of = self.aa("of", 64, [6, nb])
        self.cp("dve", of.ap, po.ap[0:64, 0:108].rearrange("p (h s) -> p h s", h=6)[:, :, 0:16], [po], [of])
        sq = self.aa("osq", 64, [6, nb], BF16)
        self.act(sq.ap, of.ap, AF.Square, [of], [sq])
        ps = self.bank("s")
        for h in range(6):
            self.mm(ps.ap[0:64, 0:nb], self.onesb.ap[0:64, 0:64], sq.ap[:, h, :], [self.onesb, sq], [ps], start=(h == 0), stop=(h == 5))
        rs = self.aa("ors", 64, [nb])
        self.rsqrt(rs.ap, ps.ap[0:64, 0:nb], 1.0 / 384.0, [ps], [rs])
        self.tt("dve", of.ap, of.ap, rs.ap.unsqueeze(1).to_broadcast([64, 6, nb]), ALU.mult, [of, rs], [of])
        self.tt("dve", mixm.ap, of.ap, P["gmla_fm"].ap.unsqueeze(2).to_broadcast([64, 6, nb]), ALU.mult, [of, P["gmla_fm"]], [mixm])
        self.arelease(m0)

    def sample_block(self):
        I, O = self.I, self.O
        nb = NS
        m0 = self.amark()
        xT = self.aa("xTs", 128, [8, nb])
        hT = self.aa("hTs", 128, [8, nb], BF16)
        xin = self.aa("xins", 16, [D])
        self.dma("sp", xin.ap, I["xs"], (), [xin], "xins")
        for c in range(8):
            tb = self.bank("t")
            self.tr(tb.ap[:, 0:nb], xin.ap[:, c * 128:(c + 1) * 128], self.identf.ap[0:16, 0:16], [xin, self.identf], [tb])
            self.cp("act", xT.ap[:, c, :], tb.ap[:, 0:nb], [tb], [xT])
        cos, sin = self.rope_tables(PAST, 0, nb)
        for l in range(DEPTH):
            P = self.L[l]
            m1 = self.amark()
            mixs = self.aa("mixs", 64, [6, nb], BF16)
            mixm = self.aa("mixm", 64, [6, nb], BF16)
            mixp = self.aa("mixp", 128, [2, nb], BF16)
            self.norm_mod(xT, hT, P["A1"], P["mod"], 0, nb, True)
            self.ssd_sample(l, hT, mixs)
            self.mla_sample(l, hT, cos, sin, mixm)
            self.pool_sample(l, hT, mixp)
            self.out_proj(l, xT, mixs, mixm, mixp, nb, True)
            self.norm_mod(xT, hT, P["A2"], P["mod"], 24, nb, True)
            self.ffn(l, xT, hT, nb, True)
            self.arelease(m1)
        yo = self.aa("youts", 16, [D])
        for c in range(8):
            self.fm2tm(yo.ap[:, c * 128:(c + 1) * 128], yo, xT.ap[:, c, :], xT, 128, 16)
        self.store(O["y_s"], yo.ap, [yo], "st_ys")
        self.arelease(m0)

    def build(self, do_prompt=True, do_sample=True, nblk=NBLK, arena_bytes=88 * 1024):
        self.declare()
        self.arena_init(arena_bytes)
        try:
            self.setup()
            if do_prompt:
                for blk in range(nblk):
                    self.prompt_block(blk)
            if do_sample:
                self.sample_block()
        except Builder._Stop:
            pass
        self.S.ins["sp"].append(dict(fn=None, deps=list(self.out_events), signal=False, dma=None))
        self.S.finalize()
        nc = self.nc
        esems = {e: self.es.enter_context(nc.semaphore("sem_" + e)) for e in ENGS}
        dsems = {v[0]: self.es.enter_context(nc.semaphore("dsem_%d" % v[0])) for k, v in self.S.dma_slots.items()}
        S = self.S
        with nc.Block() as block:
            @block.sync
            def _(eng):
                S.run_engine("sp", eng, esems, dsems)

            @block.scalar
            def _(eng):
                S.run_engine("act", eng, esems, dsems)

            @block.vector
            def _(eng):
                S.run_engine("dve", eng, esems, dsems)

            @block.gpsimd
            def _(eng):
                S.run_engine("pool", eng, esems, dsems)

            @block.tensor
            def _(eng):
                S.run_engine("pe", eng, esems, dsems)
        self.es.close()
        return nc


def host_consts():
    kc = np.zeros((128, 8), np.float32)
    half = 16
    inv = (1.0 / (np.float32(10000.0) ** (np.arange(half, dtype=np.float32) / np.float32(half)))).astype(np.float32)
    p = np.arange(128)
    kc[:, 0] = inv[p % 16]
    kc[:, 1] = np.where((p % 32) < 16, -1.0, 1.0)
    kc[:, 2] = np.where(p < 64, 2.0, 4.0)
    kc[:, 3] = np.where(p < 64, 8.0, 16.0)
    kc[:, 4] = 1.0 / kc[:, 2]
    kc[:, 5] = 1.0 / kc[:, 3]
    return kc


WEIGHTS = ["w_ada", "b_ada", "g_norm1", "w_in", "conv_w", "conv_b", "dt_bias", "a_log", "d_skip", "g_ssd_norm", "g_q_lora", "w_q_up",
           "g_kv_lora", "w_k_up", "w_v_up", "g_qk_q", "g_qk_k", "g_mla_out", "w_pool", "pool_scale", "w_out", "g_norm2", "w_gate", "w_up", "w_down"]


def make_in_map(inp, c):
    f = lambda a: np.ascontiguousarray(np.asarray(a))
    m = {
        "xp": f(inp["x_prompt"][c]),
        "xs": f(inp["x_sample"][NS * c:NS * (c + 1), 0]),
        "cl": inp["cache_kv_latent"],
        "ck": inp["cache_k_rope"],
        "sssm": f(inp["state_ssm"][:, NS * c:NS * (c + 1)]),
        "sconv": f(inp["state_conv"][:, NS * c:NS * (c + 1)]),
        "spool": f(inp["state_pool"][:, NS * c:NS * (c + 1)]),
        "pt": f(inp["page_table"][NS * c:NS * (c + 1)]).astype(np.int32, copy=False),
        "cp": f(inp["c_prompt"][c:c + 1]),
        "cs": f(inp["c_sample"][NS * c:NS * (c + 1)]),
        "kconst": host_consts(),
    }
    for w in WEIGHTS:
        m[w] = inp[w]
    return m


_CACHE = {}


def kernel(**inp):
    inp = {k: np.asarray(v) for k, v in inp.items()}
    n_pool = inp["cache_kv_latent"].shape[1]
    ncores = inp["x_prompt"].shape[0]
    if n_pool not in _CACHE:
        _CACHE[n_pool] = Builder(n_pool).build()
    nc = _CACHE[n_pool]
    in_maps = [make_in_map(inp, c) for c in range(ncores)]
    res = run_bass_kernel_spmd(nc, in_maps, core_ids=list(range(ncores)))
    R = res.results
    cat = lambda k, ax: np.concatenate([np.asarray(r[k]) for r in R], axis=ax)
    y_p = np.stack([np.asarray(r["y_p"]) for r in R], 0)
    y_s = cat("y_s", 0)[:, None, :]
    p_lat = np.stack([np.asarray(r["p_lat"]) for r in R], 1)
    p_kpe = np.stack([np.asarray(r["p_kpe"]) for r in R], 1)
    p_ssm = np.stack([np.asarray(r["p_ssm"]) for r in R], 1)
    p_conv = np.stack([np.asarray(r["p_conv"]) for r in R], 1)
    p_pool = np.stack([np.asarray(r["p_pool"]) for r in R], 1)
    s_lat = cat("s_lat", 1)[:, :, None, :]
    s_kpe = cat("s_kpe", 1)[:, :, None, :]
    s_ssm = cat("s_ssm", 1)
    s_conv = cat("s_conv", 1)
    s_pool = cat("s_pool", 1)
    return (y_p, y_s, p_lat, p_kpe, p_ssm, p_conv, p_pool, s_lat, s_kpe, s_ssm, s_conv, s_pool)
```

```python
import math
from contextlib import ExitStack

import numpy as np
import concourse.bass as bass
import concourse.mybir as mybir
from concourse.bass_utils import run_bass_kernel_spmd

F32 = mybir.dt.float32
BF16 = mybir.dt.bfloat16
I32 = mybir.dt.int32
AF = mybir.ActivationFunctionType
ALU = mybir.AluOpType
AX = mybir.AxisListType

NCORES = 8
D = 1024
SEQ = 2048
NBLK = 4
NB = 512
NS = 16
DEPTH = 2
EPS = 1e-6
DFF = 2816
NFC = DFF // 128
ATTN_SCALE = 96 ** -0.5
PAST = 16384
NPG = 128
C_Z, C_XS, C_B, C_C, C_DT, C_CQ, C_CKV, C_KPE, C_U = 0, 384, 768, 1024, 1280, 1286, 1542, 1670, 1702
IN_COLS = 1958

ENGS = ("pe", "act", "dve", "pool", "sp")


class Tok:
    __slots__ = ("name", "w", "rs")

    def __init__(self, name):
        self.name = name
        self.w = None
        self.rs = []


class Buf:
    def __init__(self, ap, name):
        self.ap = ap
        self.tok = Tok(name)

    def __getitem__(self, k):
        return self.ap[k]


class Sched:
    SAME = True
    SAME_EXEMPT = ("pe",)

    def __init__(self):
        self.ins = {e: [] for e in ENGS}
        self.dma_slots = {}

    def op(self, eng, fn, R=(), W=(), slot=None):
        deps = []
        for b in R:
            t = b.tok
            if t.w is not None:
                deps.append(t.w)
        for b in W:
            t = b.tok
            if t.w is not None:
                deps.append(t.w)
            deps.extend(t.rs)
        idx = len(self.ins[eng])
        rec = dict(fn=fn, deps=deps, signal=False, dma=None)
        if slot is not None:
            s = self.dma_slots.setdefault(slot, [len(self.dma_slots), 0])
            if s[1] > 0:
                deps.append(("D", s[0], s[1]))
            s[1] += 16
            rec["dma"] = (s[0], s[1])
            ev = ("D", s[0], s[1])
        else:
            ev = ("E", eng, idx)
        self.ins[eng].append(rec)
        for b in R:
            b.tok.rs.append(ev)
        for b in W:
            b.tok.w = ev
            b.tok.rs = []
        return ev

    def finalize(self):
        for e in ENGS:
            for i, rec in enumerate(self.ins[e]):
                for d in rec["deps"]:
                    if d[0] == "E" and (d[1] != e or (self.SAME and e not in self.SAME_EXEMPT and d[2] < i)):
                        self.ins[d[1]][d[2]]["signal"] = True
        self.val = {}
        for e in ENGS:
            c = 0
            for i, rec in enumerate(self.ins[e]):
                if rec["signal"]:
                    c += 1
                    self.val[(e, i)] = c

    def run_engine(self, e, engobj, esems, dsems):
        seen = {}
        for i, rec in enumerate(self.ins[e]):
            need = {}
            for d in rec["deps"]:
                if d[0] == "E":
                    if d[1] == e and (not self.SAME or e in self.SAME_EXEMPT or d[2] >= i):
                        continue
                    key = ("E", d[1])
                    v = self.val[(d[1], d[2])]
                else:
                    key = ("D", d[1])
                    v = d[2]
                if seen.get(key, 0) >= v:
                    continue
                if need.get(key, 0) < v:
                    need[key] = v
            for key, v in need.items():
                sem = esems[key[1]] if key[0] == "E" else dsems[key[1]]
                engobj.wait_ge(sem, v)
                seen[key] = v
            if rec["fn"] is None:
                continue
            r = rec["fn"](engobj)
            if rec["dma"] is not None:
                r.then_inc(dsems[rec["dma"][0]], 16)
            elif rec["signal"]:
                r.then_inc(esems[e], 1)


class Builder:
    def __init__(self, n_pool, dbg=False):
        self.n_pool = n_pool
        self.nc = bass.Bass("TRN2", target_bir_lowering=False)
        self.S = Sched()
        self.es = ExitStack()
        self.out_events = []
        self.nbuf = 0
        self.dbg_on = dbg
        self.dbg_list = []

    def dram_in(self, name, shape, dt=F32):
        return self.nc.dram_tensor(name, list(shape), dt, kind="ExternalInput").ap()

    def dram_out(self, name, shape, dt=F32):
        return self.nc.dram_tensor(name, list(shape), dt, kind="ExternalOutput").ap()

    def sb(self, name, shape, dt=F32):
        t = self.es.enter_context(self.nc.sbuf_tensor(name, list(shape), dt))
        return Buf(t[:], name)

    def arena_init(self, nbytes):
        self.ar_words = nbytes // 4
        t = self.es.enter_context(self.nc.sbuf_tensor("arena", [128, self.ar_words], F32))
        self.ar_t = t
        self.ar_ptr = 0
        self.ar_hist = []
        self.ar_peak = 0

    def amark(self):
        return self.ar_ptr

    def arelease(self, mark):
        self.ar_ptr = mark

    def aa(self, name, parts, fshape, dt=F32):
        n = 1
        for s in fshape:
            n *= s
        esz = 4 if dt in (F32, I32) else 2
        words = (n * esz + 3) // 4
        lo = self.ar_ptr
        hi = lo + words
        assert hi <= self.ar_words, "arena overflow %s need %d have %d" % (name, hi, self.ar_words)
        self.ar_ptr = hi
        self.ar_peak = max(self.ar_peak, hi)
        ap = self.ar_t[0:parts, lo:hi]
        if dt != F32:
            ap = ap.bitcast(dt)
            if esz == 2:
                ap = ap[:, 0:n]
        if len(fshape) == 2:
            ap = ap.rearrange("p (a b) -> p a b", a=fshape[0])
        elif len(fshape) == 3:
            ap = ap.rearrange("p (a b c) -> p a b c", a=fshape[0], b=fshape[1])
        self.nbuf += 1
        b = Buf(ap, "%s#%d" % (name, self.nbuf))
        keep = []
        for (l2, h2, ob) in self.ar_hist:
            if l2 < hi and lo < h2:
                if ob.tok.w is not None:
                    b.tok.rs.append(ob.tok.w)
                b.tok.rs.extend(ob.tok.rs)
                if not (lo <= l2 and h2 <= hi):
                    keep.append((l2, h2, ob))
            else:
                keep.append((l2, h2, ob))
        keep.append((lo, hi, b))
        self.ar_hist = keep
        return b

    def op(self, eng, fn, R=(), W=(), slot=None):
        Rn = [b for b in R if not getattr(b, "excl", False)]
        Wn = list(W) + [b for b in R if getattr(b, "excl", False) and b not in W]
        return self.S.op(eng, fn, Rn, Wn, slot)

    def mm(self, out, lhsT, rhs, R, W, start=True, stop=True, sgc=False):
        return self.op("pe", lambda e: e.matmul(out, lhsT=lhsT, rhs=rhs, start=start, stop=stop, skip_group_check=sgc), R, W)

    def tr(self, out, in_, ident, R, W):
        return self.op("pe", lambda e: e.transpose(out=out, in_=in_, identity=ident), R, W)

    def act(self, out, in_, func, R, W, scale=None, bias=None, accum=None):
        kw = {}
        if scale is not None:
            kw["scale"] = scale
        if bias is not None:
            kw["bias"] = bias
        if accum is not None:
            kw["accum_out"] = accum
        return self.op("act", lambda e: e.activation(out=out, in_=in_, func=func, **kw), R, W)

    def tt(self, eng, out, in0, in1, op, R, W):
        return self.op(eng, lambda e: e.tensor_tensor(out=out, in0=in0, in1=in1, op=op), R, W)

    def ts(self, eng, out, in0, s1, s2, op0, op1, R, W):
        if op1 is None:
            return self.op(eng, lambda e: e.tensor_scalar(out=out, in0=in0, scalar1=s1, scalar2=None, op0=op0), R, W)
        return self.op(eng, lambda e: e.tensor_scalar(out=out, in0=in0, scalar1=s1, scalar2=s2, op0=op0, op1=op1), R, W)

    def stt(self, out, in0, scalar, in1, op0, op1, R, W):
        return self.op("dve", lambda e: e.scalar_tensor_tensor(out=out, in0=in0, scalar=scalar, in1=in1, op0=op0, op1=op1), R, W)

    def red(self, out, in_, op, R, W):
        return self.op("dve", lambda e: e.tensor_reduce(out=out, in_=in_, axis=AX.X, op=op), R, W)

    def cp(self, eng, out, in_, R, W):
        if eng == "act":
            return self.op("act", lambda e: e.copy(out=out, in_=in_), R, W)
        return self.op(eng, lambda e: e.tensor_copy(out=out, in_=in_), R, W)

    def memset(self, eng, ap, val, W):
        return self.op(eng, lambda e: e.memset(ap, val), (), W)

    def recip(self, out, in_, R, W):
        return self.op("dve", lambda e: e.reciprocal(out=out, in_=in_), R, W)

    def dma(self, eng, out, in_, R, W, slot, slow=False):
        if slow:
            return self.op(eng, lambda e: e.dma_start(out=out, in_=in_, allow_slow_non_contiguous=True), R, W, slot)
        return self.op(eng, lambda e: e.dma_start(out=out, in_=in_), R, W, slot)

    def store(self, out, in_, R, slot, slow=False):
        ev = self.dma("sp", out, in_, R, (), slot, slow)
        self.out_events.append(ev)
        return ev

    def rsqrt(self, out, in_, scale, R, W):
        P = out.shape[0]
        self.act(out, in_, AF.Ln, list(R) + [self.eps], W, scale=scale, bias=self.eps.ap[0:P, 0:1])
        self.act(out, out, AF.Exp, W, W, scale=-0.5)

    def psum_init(self):
        t = self.es.enter_context(self.nc.psum_tensor("ps", [128, 4096], F32))
        self.ps_t = t
        self.PB = [Buf(t[:, b * 512:(b + 1) * 512], "psb%d" % b) for b in range(8)]
        for b in self.PB:
            b.excl = True
        self.rot = {"d": 0, "s": 0, "t": 0}

    def bank(self, kind):
        if kind == "d":
            b = self.rot["d"]
            self.rot["d"] = (b + 1) % 4
            return self.PB[b]
        if kind == "s":
            b = self.rot["s"]
            self.rot["s"] = (b + 1) % 2
            return self.PB[4 + b]
        b = self.rot["t"]
        self.rot["t"] = (b + 1) % 2
        return self.PB[6 + b]

    def wbuf_init(self, n=3):
        self.wb = [self.sb("wbuf%d" % i, [128, 4096], BF16) for i in range(n)]
        self.wi = 0

    def wload(self, dram_view, parts, kc, ncols):
        b = self.wb[self.wi]
        slot = "wbuf%d" % self.wi
        self.wi = (self.wi + 1) % len(self.wb)
        v = b.ap[0:parts, 0:kc * ncols].rearrange("p (k n) -> p k n", k=kc)
        self.dma("pool", v, dram_view, (), [b], slot)
        return b, v

    def declare(self):
        npool = self.n_pool
        I = {}
        I["xp"] = self.dram_in("xp", [SEQ, D])
        I["xs"] = self.dram_in("xs", [NS, D])
        I["cl"] = self.dram_in("cl", [DEPTH, npool, 128, 128])
        I["ck"] = self.dram_in("ck", [DEPTH, npool, 128, 32])
        I["sssm"] = self.dram_in("sssm", [DEPTH, NS, 6, 64, 128])
        I["sconv"] = self.dram_in("sconv", [DEPTH, NS, 3, 896])
        I["spool"] = self.dram_in("spool", [DEPTH, NS, 15, 256])
        I["pt"] = self.dram_in("pt", [NS, NPG], I32)
        I["cp"] = self.dram_in("cp", [1, D])
        I["cs"] = self.dram_in("cs", [NS, D])
        I["w_ada"] = self.dram_in("w_ada", [DEPTH, D, 6 * D])
        I["b_ada"] = self.dram_in("b_ada", [DEPTH, 6 * D])
        I["g_norm1"] = self.dram_in("g_norm1", [DEPTH, D])
        I["w_in"] = self.dram_in("w_in", [DEPTH, D, IN_COLS])
        I["conv_w"] = self.dram_in("conv_w", [DEPTH, 4, 896])
        I["conv_b"] = self.dram_in("conv_b", [DEPTH, 896])
        I["dt_bias"] = self.dram_in("dt_bias", [DEPTH, 6])
        I["a_log"] = self.dram_in("a_log", [DEPTH, 6])
        I["d_skip"] = self.dram_in("d_skip", [DEPTH, 6])
        I["g_ssd_norm"] = self.dram_in("g_ssd_norm", [DEPTH, 384])
        I["g_q_lora"] = self.dram_in("g_q_lora", [DEPTH, 256])
        I["w_q_up"] = self.dram_in("w_q_up", [DEPTH, 256, 6, 96])
        I["g_kv_lora"] = self.dram_in("g_kv_lora", [DEPTH, 128])
        I["w_k_up"] = self.dram_in("w_k_up", [DEPTH, 128, 6, 64])
        I["w_v_up"] = self.dram_in("w_v_up", [DEPTH, 128, 6, 64])
        I["g_qk_q"] = self.dram_in("g_qk_q", [DEPTH, 96])
        I["g_qk_k"] = self.dram_in("g_qk_k", [DEPTH, 96])
        I["g_mla_out"] = self.dram_in("g_mla_out", [DEPTH, 384])
        I["w_pool"] = self.dram_in("w_pool", [DEPTH, 4, 64, 64])
        I["pool_scale"] = self.dram_in("pool_scale", [DEPTH, 256])
        I["w_out"] = self.dram_in("w_out", [DEPTH, D, D])
        I["g_norm2"] = self.dram_in("g_norm2", [DEPTH, D])
        I["w_gate"] = self.dram_in("w_gate", [DEPTH, D, DFF])
        I["w_up"] = self.dram_in("w_up", [DEPTH, D, DFF])
        I["w_down"] = self.dram_in("w_down", [DEPTH, DFF, D])
        I["kconst"] = self.dram_in("kconst", [128, 8])
        self.I = I
        O = {}
        O["y_p"] = self.dram_out("y_p", [SEQ, D])
        O["y_s"] = self.dram_out("y_s", [NS, D])
        O["p_lat"] = self.dram_out("p_lat", [DEPTH, SEQ, 128])
        O["p_kpe"] = self.dram_out("p_kpe", [DEPTH, SEQ, 32])
        O["p_ssm"] = self.dram_out("p_ssm", [DEPTH, 6, 64, 128])
        O["p_conv"] = self.dram_out("p_conv", [DEPTH, 3, 896])
        O["p_pool"] = self.dram_out("p_pool", [DEPTH, 15, 256])
        O["s_lat"] = self.dram_out("s_lat", [DEPTH, NS, 128])
        O["s_kpe"] = self.dram_out("s_kpe", [DEPTH, NS, 32])
        O["s_ssm"] = self.dram_out("s_ssm", [DEPTH, NS, 6, 64, 128])
        O["s_conv"] = self.dram_out("s_conv", [DEPTH, NS, 3, 896])
        O["s_pool"] = self.dram_out("s_pool", [DEPTH, NS, 15, 256])
        self.O = O

    class _Stop(Exception):
        pass

    def ckpt(self, k):
        if getattr(self, "stop_at", None) == k:
            raise Builder._Stop()

    def setup_slot(self, eng):
        self._ss = getattr(self, "_ss", 0) + 1
        return "%s_ld%d" % (eng, self._ss % 4)

    def dbg(self, name, buf, ap, shape):
        if not self.dbg_on:
            return
        o = self.dram_out("dbg_" + name, list(shape))
        self.dbg_list.append("dbg_" + name)
        self.store(o, ap, [buf], "dbgst_" + name)

    def setup(self):
        I = self.I
        self.psum_init()
        self.wbuf_init(4)
        sb = self.sb
        self.eps = sb("eps", [128, 1])
        self.memset("pool", self.eps.ap, EPS, [self.eps])
        self.identf = sb("identf", [128, 128])
        self.identb = sb("identb", [128, 128], BF16)
        self.memset("pool", self.identf.ap, 0.0, [self.identf])
        self.op("pool", lambda e: e.affine_select(out=self.identf.ap, in_=self.identf.ap, pattern=[[-1, 128]], compare_op=ALU.not_equal,
                                                  fill=1.0, base=0, channel_multiplier=1), [self.identf], [self.identf])
        self.cp("pool", self.identb.ap, self.identf.ap, [self.identf], [self.identb])
        self.trif = sb("trif", [128, 128])
        self.trib = sb("trib", [128, 128], BF16)
        self.memset("pool", self.trif.ap, 1.0, [self.trif])
        self.op("pool", lambda e: e.affine_select(out=self.trif.ap, in_=self.trif.ap, pattern=[[1, 128]], compare_op=ALU.is_ge,
                                                  fill=0.0, base=0, channel_multiplier=-1), [self.trif], [self.trif])
        self.cp("pool", self.trib.ap, self.trif.ap, [self.trif], [self.trib])
        self.onesf = sb("onesf", [128, 128])
        self.onesb = sb("onesb", [128, 128], BF16)
        self.memset("pool", self.onesf.ap, 1.0, [self.onesf])
        self.memset("pool", self.onesb.ap, 1.0, [self.onesb])
        self.ckpt(1)
        self.bmask = sb("bmask", [128, 4])
        self.memset("pool", self.bmask.ap, 0.0, [self.bmask])
        for i in range(4):
            self.memset("pool", self.bmask.ap[32 * i:32 * i + 32, i:i + 1], 1.0, [self.bmask])
        self.ckpt(2)
        self.kc = sb("kc", [128, 8])
        self.dma("sp", self.kc.ap, I["kconst"], (), [self.kc], "kc")
        self.rct = sb("rct", [128, 2, 16])
        tmpi = sb("tmpi", [128, 16])
        self.op("pool", lambda e: e.iota(tmpi.ap, pattern=[[1, 16]], base=1, channel_multiplier=0, allow_small_or_imprecise_dtypes=True), (), [tmpi])
        for c in range(2):
            self.ts("dve", self.rct.ap[:, c, :], tmpi.ap, self.kc.ap[:, 2 + c:3 + c], None, ALU.min, None, [tmpi, self.kc], [self.rct])
        self.recip(self.rct.ap, self.rct.ap, [self.rct], [self.rct])
        self.ckpt(3)
        ptT = sb("ptT", [128, NS], I32)
        self.dma("sp", ptT.ap, I["pt"].rearrange("s j -> j s"), (), [ptT], "ptT", slow=True)
        ptf = sb("ptf", [128, NS])
        self.cp("dve", ptf.ap, ptT.ap, [ptT], [ptf])
        self.idx8 = sb("idx8", [128, DEPTH, NS, 8], I32)
        for l in range(DEPTH):
            for c in range(8):
                self.ts("dve", self.idx8.ap[:, l, :, c], ptf.ap, 8.0, float(c + 8 * l * self.n_pool), ALU.mult, ALU.add, [ptf], [self.idx8])

        self.ckpt(4)
        L = []
        for l in range(DEPTH):
            P = {}

            def ld(name, shape, src, slow=True, eng="sp", dt=F32):
                b = sb("%s_%d" % (name, l), shape, dt)
                self.dma(eng, b.ap, src, (), [b], self.setup_slot(eng), slow=slow)
                return b
            P["g1"] = ld("g1", [128, 8], I["g_norm1"][l].rearrange("(c p) -> p c", p=128))
            P["g2"] = ld("g2", [128, 8], I["g_norm2"][l].rearrange("(c p) -> p c", p=128))
            P["bada"] = ld("bada", [128, 48], I["b_ada"][l].rearrange("(c p) -> p c", p=128))
            P["cwx"] = sb("cwx_%d" % l, [64, 6, 4])
            P["cwbc"] = sb("cwbc_%d" % l, [128, 4, 4])
            for k in range(4):
                self.dma("sp", P["cwx"].ap[:, :, k], I["conv_w"][l][k, 0:384].rearrange("(h p) -> p h", p=64), (), [P["cwx"]], self.setup_slot("sp"), slow=True)
                self.dma("sp", P["cwbc"].ap[:, :, k], I["conv_w"][l][k, 384:896].rearrange("(c p) -> p c", p=128), (), [P["cwbc"]], self.setup_slot("sp"), slow=True)
            P["cbx"] = ld("cbx", [64, 6], I["conv_b"][l][0:384].rearrange("(h p) -> p h", p=64))
            P["cbbc"] = ld("cbbc", [128, 4], I["conv_b"][l][384:896].rearrange("(c p) -> p c", p=128))
            P["gssd"] = ld("gssd", [64, 6], I["g_ssd_norm"][l].rearrange("(h p) -> p h", p=64))
            P["dskip"] = ld("dskip", [64, 6], I["d_skip"][l:l + 1, :].partition_broadcast(64), slow=False)
            P["dtb"] = ld("dtb", [128, 6], I["dt_bias"][l:l + 1, :].partition_broadcast(128), slow=False)
            alog = ld("alog", [128, 6], I["a_log"][l:l + 1, :].partition_broadcast(128), slow=False)
            P["negA"] = sb("negA_%d" % l, [128, 6])
            self.act(P["negA"].ap, alog.ap, AF.Exp, [alog], [P["negA"]])
            self.ts("dve", P["negA"].ap, P["negA"].ap, -1.0, None, ALU.mult, None, [P["negA"]], [P["negA"]])
            self.ckpt(5)
            P["gql"] = ld("gql", [128, 2], I["g_q_lora"][l].rearrange("(c p) -> p c", p=128))
            P["gkv"] = ld("gkv", [128, 1], I["g_kv_lora"][l].rearrange("(c p) -> p c", p=128))
            gqn = ld("gqn", [64, 1], I["g_qk_q"][l][0:64].rearrange("(c p) -> p c", p=64))
            gkn = ld("gkn", [64, 1], I["g_qk_k"][l][0:64].rearrange("(c p) -> p c", p=64))
            gqp = sb("gqp_%d" % l, [128, 1])
            gkp = sb("gkp_%d" % l, [128, 1])
            for i in range(4):
                self.dma("sp", gqp.ap[32 * i:32 * i + 32, :], I["g_qk_q"][l][64:96].rearrange("(c p) -> p c", p=32), (), [gqp], self.setup_slot("sp"), slow=True)
                self.dma("sp", gkp.ap[32 * i:32 * i + 32, :], I["g_qk_k"][l][64:96].rearrange("(c p) -> p c", p=32), (), [gkp], self.setup_slot("sp"), slow=True)
            P["gqkn"] = sb("gqkn_%d" % l, [64, 1])
            P["gqkp"] = sb("gqkp_%d" % l, [128, 1])
            self.tt("dve", P["gqkn"].ap, gqn.ap, gkn.ap, ALU.mult, [gqn, gkn], [P["gqkn"]])
            self.tt("dve", P["gqkp"].ap, gqp.ap, gkp.ap, ALU.mult, [gqp, gkp], [P["gqkp"]])
            self.ckpt(6)
            gq1 = ld("gq1", [1, 96], I["g_qk_q"][l:l + 1, :], slow=False)
            gk1 = ld("gk1", [1, 96], I["g_qk_k"][l:l + 1, :], slow=False)
            self.tt("dve", gq1.ap, gq1.ap, gk1.ap, ALU.mult, [gq1, gk1], [gq1])
            m1 = sb("m1_%d" % l, [1, 1])
            self.op("dve", lambda e, m1=m1, gq1=gq1: e.tensor_reduce(out=m1.ap, in_=gq1.ap, axis=AX.X, op=ALU.max, apply_absolute_value=True), [gq1], [m1])
            pb = self.bank("s")
            self.mm(pb.ap[:, 0:1], self.onesf.ap[0:1, 0:128], m1.ap, [self.onesf, m1], [pb])
            P["negM"] = sb("negM_%d" % l, [128, 1])
            self.ts("dve", P["negM"].ap, pb.ap[:, 0:1], -ATTN_SCALE * 96.0, None, ALU.mult, None, [pb], [P["negM"]])
            self.ckpt(7)
            P["gmla_bc"] = ld("gmla_bc", [128, 384], I["g_mla_out"][l:l + 1, :].partition_broadcast(128), slow=False)
            P["gmla_fm"] = ld("gmla_fm", [64, 6], I["g_mla_out"][l].rearrange("(h p) -> p h", p=64))
            P["pscale"] = ld("pscale", [128, 2], I["pool_scale"][l].rearrange("(c p) -> p c", p=128))
            wq = I["w_q_up"][l].rearrange("(c p) h d -> p c h d", p=128)
            P["wqn"] = sb("wqn_%d" % l, [128, 2, 6, 64], BF16)
            P["wqp4"] = sb("wqp4_%d" % l, [128, 2, 6, 128], BF16)
            P["wqr4"] = sb("wqr4_%d" % l, [128, 2, 6, 128], BF16)
            for c in range(2):
                self.dma("pool", P["wqn"].ap[:, c, :, :], wq[:, c, :, 0:64], (), [P["wqn"]], self.setup_slot("pool"))
                for i in range(4):
                    self.dma("pool", P["wqp4"].ap[:, c, :, 32 * i:32 * i + 32], wq[:, c, :, 64:96], (), [P["wqp4"]], self.setup_slot("pool"))
                    self.dma("pool", P["wqr4"].ap[:, c, :, 32 * i:32 * i + 16], wq[:, c, :, 80:96], (), [P["wqr4"]], self.setup_slot("pool"))
                    self.dma("pool", P["wqr4"].ap[:, c, :, 32 * i + 16:32 * i + 32], wq[:, c, :, 64:80], (), [P["wqr4"]], self.setup_slot("pool"))
            self.ckpt(8)
            P["wk"] = ld("wk", [128, 384], I["w_k_up"][l].rearrange("r h d -> r (h d)"), slow=False, eng="pool", dt=BF16)
            P["wv"] = ld("wv", [128, 384], I["w_v_up"][l].rearrange("r h d -> r (h d)"), slow=False, eng="pool", dt=BF16)
            P["wkT"] = sb("wkT_%d" % l, [64, 6, 128], BF16)
            for h in range(6):
                tb = self.bank("t")
                tv = tb.ap.bitcast(BF16)
                self.tr(tv[0:64, 0:128], P["wk"].ap[:, h * 64:(h + 1) * 64], self.identb.ap, [P["wk"], self.identb], [tb])
                self.cp("dve", P["wkT"].ap[:, h, :], tv[0:64, 0:128], [tb], [P["wkT"]])
            self.ckpt(9)
            win = I["w_in"][l].rearrange("(c p) m -> p c m", p=128)
            P["wkr"] = sb("wkr_%d" % l, [128, 8, 32], BF16)
            self.dma("pool", P["wkr"].ap[:, :, 0:16], win[:, :, C_KPE + 16:C_KPE + 32], (), [P["wkr"]], self.setup_slot("pool"))
            self.dma("pool", P["wkr"].ap[:, :, 16:32], win[:, :, C_KPE:C_KPE + 16], (), [P["wkr"]], self.setup_slot("pool"))
            P["wpool"] = sb("wpool_%d" % l, [128, 2, 128], BF16)
            self.memset("pool", P["wpool"].ap, 0.0, [P["wpool"]])
            for g in range(4):
                o = 64 * (g % 2)
                self.dma("pool", P["wpool"].ap[o:o + 64, g // 2, o:o + 64], I["w_pool"][l, g], (), [P["wpool"]], self.setup_slot("pool"))
            self.ckpt(10)
            P["ST"] = sb("ST_%d" % l, [128, 6, 64])
            P["STb"] = sb("STb_%d" % l, [128, 6, 64], BF16)
            self.memset("pool", P["ST"].ap, 0.0, [P["ST"]])
            self.memset("pool", P["STb"].ap, 0.0, [P["STb"]])
            P["hx"] = sb("hx_%d" % l, [64, 6, 3])
            P["hbc"] = sb("hbc_%d" % l, [128, 4, 3])
            self.memset("pool", P["hx"].ap, 0.0, [P["hx"]])
            self.memset("pool", P["hbc"].ap, 0.0, [P["hbc"]])
            P["hu"] = sb("hu_%d" % l, [128, 2, 15])
            self.memset("pool", P["hu"].ap, 0.0, [P["hu"]])
            P["latT"] = sb("latT_%d" % l, [128, SEQ], BF16)
            P["kpeT"] = sb("kpeT_%d" % l, [32, SEQ], BF16)
            P["vaug"] = sb("vaug_%d" % l, [128, 16, 6, 66], BF16)
            self.memset("pool", P["vaug"].ap, 1.0, [P["vaug"]])
            P["rks"] = sb("rks_%d" % l, [128, 16, 6])
            P["mod"] = sb("mod_%d" % l, [128, 48, 17])
            P["A1"] = sb("A1_%d" % l, [128, 8, 17])
            P["A2"] = sb("A2_%d" % l, [128, 8, 17])
            L.append(P)
        self.L = L

        self.ckpt(11)
        call = sb("call", [17, D])
        self.dma("sp", call.ap[0:1, :], I["cp"], (), [call], "call")
        self.dma("sp", call.ap[1:17, :], I["cs"], (), [call], "call")
        self.act(call.ap, call.ap, AF.Silu, [call], [call])
        cT = sb("cT", [128, 8, 18], BF16)
        self.memset("pool", cT.ap, 0.0, [cT])
        for c in range(8):
            tb = self.bank("t")
            self.tr(tb.ap[:, 0:17], call.ap[:, c * 128:(c + 1) * 128], self.identf.ap[0:17, 0:17], [call, self.identf], [tb])
            self.cp("dve", cT.ap[:, c, 0:17], tb.ap[:, 0:17], [tb], [cT])
        self.ckpt(12)
        for l in range(DEPTH):
            P = L[l]
            wa = I["w_ada"][l].rearrange("(c p) m -> p c m", p=128)
            for half in range(2):
                pb = self.bank("d")
                for t in range(6):
                    wbuf, wv_ = self.wload(wa[:, :, (half * 6 + t) * 512:(half * 6 + t + 1) * 512], 128, 8, 512)
                    for mi in range(4):
                        mloc = t * 4 + mi
                        for kc in range(8):
                            self.mm(pb.ap[:, mloc * 18:(mloc + 1) * 18], wv_[:, kc, mi * 128:(mi + 1) * 128], cT.ap[:, kc, :], [wbuf, cT], [pb],
                                    start=(kc == 0), stop=(kc == 7))
                self.tt("dve", P["mod"].ap[:, half * 24:(half + 1) * 24, :], pb.ap[:, 0:24 * 18].rearrange("p (a b) -> p a b", a=24)[:, :, 0:17],
                        P["bada"].ap[:, half * 24:(half + 1) * 24].unsqueeze(2).to_broadcast([128, 24, 17]), ALU.add, [pb, P["bada"]], [P["mod"]])
            for (A, g, so) in ((P["A1"], P["g1"], 8), (P["A2"], P["g2"], 32)):
                self.ts("dve", A.ap, P["mod"].ap[:, so:so + 8, :], 1.0, None, ALU.add, None, [P["mod"]], [A])
                self.tt("dve", A.ap, A.ap, g.ap.unsqueeze(2).to_broadcast([128, 8, 17]), ALU.mult, [A, g], [A])

    def rope_tables(self, base, step, nb):
        cos = self.aa("cos", 128, [nb])
        sin = self.aa("sin", 128, [nb])
        m = self.amark()
        pos = self.aa("pos", 128, [nb])
        ki = self.aa("ki", 128, [nb], I32)
        kf = self.aa("kf", 128, [nb])
        r = self.aa("r", 128, [nb])
        msk = self.aa("msk", 128, [nb])
        self.op("pool", lambda e: e.iota(pos.ap, pattern=[[step, nb]], base=base, channel_multiplier=0, allow_small_or_imprecise_dtypes=True), (), [pos])
        self.ts("dve", pos.ap, pos.ap, self.kc.ap[:, 0:1], None, ALU.mult, None, [pos, self.kc], [pos])
        TWO_PI = 2.0 * math.pi
        C1 = 6.28125
        C2 = TWO_PI - C1
        for (dst, shift) in ((sin, 0.0), (cos, math.pi / 2.0)):
            self.ts("dve", kf.ap, pos.ap, 1.0 / TWO_PI, shift / TWO_PI, ALU.mult, ALU.add, [pos], [kf])
            self.cp("dve", ki.ap, kf.ap, [kf], [ki])
            self.cp("dve", kf.ap, ki.ap, [ki], [kf])
            self.stt(r.ap, kf.ap, -C1, pos.ap, ALU.mult, ALU.add, [kf, pos], [r])
            self.stt(r.ap, kf.ap, -C2, r.ap, ALU.mult, ALU.add, [kf, r], [r])
            if shift != 0.0:
                self.ts("dve", r.ap, r.ap, shift, None, ALU.add, None, [r], [r])
            self.ts("dve", msk.ap, r.ap, math.pi, -TWO_PI, ALU.is_gt, ALU.mult, [r], [msk])
            self.tt("dve", r.ap, r.ap, msk.ap, ALU.add, [r, msk], [r])
            self.ts("dve", msk.ap, r.ap, -math.pi, TWO_PI, ALU.is_lt, ALU.mult, [r], [msk])
            self.tt("dve", r.ap, r.ap, msk.ap, ALU.add, [r, msk], [r])
            self.ts("dve", r.ap, r.ap, math.pi, -math.pi, ALU.min, ALU.max, [r], [r])
            self.act(dst.ap, r.ap, AF.Sin, [r], [dst])
        self.ts("dve", sin.ap, sin.ap, self.kc.ap[:, 1:2], None, ALU.mult, None, [sin, self.kc], [sin])
        self.arelease(m)
        return cos, sin

    def norm_mod(self, xT, hT, A, Bm, bo, nb, sample):
        m = self.amark()
        pb = self.bank("s")
        sqs = [self.aa("nsq", 128, [nb], BF16) for _ in range(2)]
        for c in range(8):
            sq = sqs[c % 2]
            self.act(sq.ap, xT.ap[:, c, :], AF.Square, [xT], [sq])
            self.mm(pb.ap[:, 0:nb], self.onesb.ap, sq.ap, [self.onesb, sq], [pb], start=(c == 0), stop=(c == 7))
        rstd = self.aa("nrstd", 128, [nb])
        self.rsqrt(rstd.ap, pb.ap[:, 0:nb], 1.0 / D, [pb], [rstd])
        if not sample:
            tmps = [self.aa("ntmp", 128, [nb]) for _ in range(2)]
            for c in range(8):
                t = tmps[c % 2]
                self.tt("dve", t.ap, xT.ap[:, c, :], rstd.ap, ALU.mult, [xT, rstd], [t])
                self.act(hT.ap[:, c, :], t.ap, AF.Identity, [t, A, Bm], [hT], scale=A.ap[:, c, 0:1], bias=Bm.ap[:, bo + c, 0:1])
        else:
            t = self.aa("ntmp", 128, [8, nb])
            self.tt("dve", t.ap, xT.ap, rstd.ap.unsqueeze(1).to_broadcast([128, 8, nb]), ALU.mult, [xT, rstd], [t])
            self.tt("dve", t.ap, t.ap, A.ap[:, :, 1:17], ALU.mult, [t, A], [t])
            self.tt("dve", hT.ap, t.ap, Bm.ap[:, bo:bo + 8, 1:17], ALU.add, [t, Bm], [hT])
        self.arelease(m)

    def fm2tm(self, dst_ap, dst_buf, src_ap, src_buf, K, n):
        tb = self.bank("t")
        self.tr(tb.ap[0:n, 0:K], src_ap, self.identf.ap[0:K, 0:K], [src_buf, self.identf], [tb])
        self.cp("act", dst_ap, tb.ap[0:n, 0:K], [tb], [dst_buf])

    def conv_silu(self, P_, raw, nb, w4, wbuf, bcol, bbuf, out_ap, out_buf, acc):
        self.ts("dve", acc.ap, raw.ap[:, 3:3 + nb], w4[:, 3:4], bcol, ALU.mult, ALU.add, [raw, wbuf, bbuf], [acc])
        for k in (2, 1, 0):
            self.stt(acc.ap, raw.ap[:, k:k + nb], w4[:, k:k + 1], acc.ap, ALU.mult, ALU.add, [raw, wbuf, acc], [acc])
        self.act(out_ap, acc.ap, AF.Silu, [acc], [out_buf])

    def ssd_prompt(self, l, hT, mixs, last):
        P = self.L[l]
        I = self.I
        nb = NB
        win = I["w_in"][l].rearrange("(c p) m -> p c m", p=128)
        m0 = self.amark()
        zs = self.aa("zs", 64, [6, nb], BF16)
        xsc = self.aa("xsc", 64, [6, nb], BF16)
        Bc = self.aa("Bc", 128, [2, nb], BF16)
        Cc = self.aa("Cc", 128, [2, nb], BF16)
        dttm = self.aa("dttm", 128, [4, 6])
        atm = self.aa("atm", 128, [4, 6])
        wb, wv = self.wload(win[:, :, C_Z:C_Z + 384], 128, 8, 384)
        for h in range(6):
            pb = self.bank("d")
            for kc in range(8):
                self.mm(pb.ap[0:64, 0:nb], wv[:, kc, h * 64:(h + 1) * 64], hT.ap[:, kc, :], [wb, hT], [pb], start=(kc == 0), stop=(kc == 7))
            self.act(zs.ap[:, h, :], pb.ap[0:64, 0:nb], AF.Silu, [pb], [zs])
        self.ckpt(30)
        m1 = self.amark()
        raws = [self.aa("raw", 128, [3 + nb]) for _ in range(2)]
        accs = [self.aa("cacc", 128, [nb]) for _ in range(2)]
        ri = 0
        wb, wv = self.wload(win[:, :, C_XS:C_XS + 512], 128, 8, 512)
        for h in range(6):
            pb = self.bank("d")
            for kc in range(8):
                self.mm(pb.ap[0:64, 0:nb], wv[:, kc, h * 64:(h + 1) * 64], hT.ap[:, kc, :], [wb, hT], [pb], start=(kc == 0), stop=(kc == 7))
            raw, acc = raws[ri % 2], accs[ri % 2]
            ri += 1
            self.cp("dve", raw.ap[0:64, 0:3], P["hx"].ap[:, h, :], [P["hx"]], [raw])
            self.cp("act", raw.ap[0:64, 3:3 + nb], pb.ap[0:64, 0:nb], [pb], [raw])
            self.cp("dve", P["hx"].ap[:, h, :], raw.ap[0:64, nb:nb + 3], [raw], [P["hx"]])
            rawv = Buf(raw.ap[0:64, :], "x"); rawv.tok = raw.tok
            accv = Buf(acc.ap[0:64, :], "x"); accv.tok = acc.tok
            self.conv_silu(64, rawv, nb, P["cwx"].ap[:, h, :], P["cwx"], P["cbx"].ap[:, h:h + 1], P["cbx"], xsc.ap[:, h, :], xsc, accv)

        def bc_chunk(ci, wb, wv, col0):
            pb = self.bank("d")
            for kc in range(8):
                self.mm(pb.ap[:, 0:nb], wv[:, kc, col0:col0 + 128], hT.ap[:, kc, :], [wb, hT], [pb], start=(kc == 0), stop=(kc == 7))
            nonlocal ri
            raw, acc = raws[ri % 2], accs[ri % 2]
            ri += 1
            self.cp("dve", raw.ap[:, 0:3], P["hbc"].ap[:, ci, :], [P["hbc"]], [raw])
            self.cp("act", raw.ap[:, 3:3 + nb], pb.ap[:, 0:nb], [pb], [raw])
            self.cp("dve", P["hbc"].ap[:, ci, :], raw.ap[:, nb:nb + 3], [raw], [P["hbc"]])
            dst = Bc if ci < 2 else Cc
            self.conv_silu(128, raw, nb, P["cwbc"].ap[:, ci, :], P["cwbc"], P["cbbc"].ap[:, ci:ci + 1], P["cbbc"], dst.ap[:, ci % 2, :], dst, acc)
        bc_chunk(0, wb, wv, 384)
        self.ckpt(31)
        wb, wv = self.wload(win[:, :, 896:1286], 128, 8, 390)
        bc_chunk(1, wb, wv, 0)
        bc_chunk(2, wb, wv, 128)
        bc_chunk(3, wb, wv, 256)
        self.ckpt(32)
        pbd = self.bank("s")
        for t in range(4):
            for kc in range(8):
                self.mm(pbd.ap[:, t * 6:(t + 1) * 6], hT.ap[:, kc, t * 128:(t + 1) * 128], wv[:, kc, 384:390], [wb, hT], [pbd], start=(kc == 0), stop=(kc == 7))
        self.tt("dve", dttm.ap, pbd.ap[:, 0:24].rearrange("p (a b) -> p a b", a=4), P["dtb"].ap.unsqueeze(1).to_broadcast([128, 4, 6]), ALU.add, [pbd, P["dtb"]], [dttm])
        self.act(dttm.ap, dttm.ap, AF.Exp, [dttm], [dttm])
        self.act(dttm.ap, dttm.ap, AF.Ln, [dttm], [dttm], bias=self.onesf.ap[:, 0:1])
        self.tt("dve", atm.ap, dttm.ap, P["negA"].ap.unsqueeze(1).to_broadcast([128, 4, 6]), ALU.mult, [dttm, P["negA"]], [atm])
        self.ckpt(33)
        self.arelease(m1)
        for t in range(4):
            self.ssd_chunk(l, t, t * 128, 128, zs, xsc, Bc, Cc, dttm, atm, mixs)
        if last:
            self.ssd_final_outputs(l)
        self.arelease(m0)

    def ssd_chunk(self, l, t, c0, T, zs, xsc, Bc, Cc, dttm, atm, mixs):
        P = self.L[l]
        m = self.amark()
        cs = slice(c0, c0 + T)
        tb = self.bank("t")
        tv = tb.ap.bitcast(BF16)
        for h in range(6):
            self.tr(tv[:, h * 64:(h + 1) * 64], xsc.ap[:, h, cs], self.identb.ap[0:64, 0:64], [xsc, self.identb], [tb])
        xdt = self.aa("xdt", 128, [6, 64], BF16)
        self.tt("dve", xdt.ap, tv[:, 0:384].rearrange("p (h q) -> p h q", h=6), dttm.ap[:, t, :].unsqueeze(2).to_broadcast([128, 6, 64]), ALU.mult, [tb, dttm], [xdt])
        tb2 = self.bank("t")
        tv2 = tb2.ap.bitcast(BF16)
        for g in range(2):
            self.tr(tv2[:, g * 128:(g + 1) * 128], Bc.ap[:, g, cs], self.identb.ap, [Bc, self.identb], [tb2])
        Btm = self.aa("Btm", 128, [2, 128], BF16)
        self.cp("act", Btm.ap, tv2[:, 0:256].rearrange("p (g n) -> p g n", g=2), [tb2], [Btm])
        self.ckpt(34)
        pa = self.bank("s")
        self.mm(pa.ap[:, 0:6], self.trif.ap, atm.ap[:, t, :], [self.trif, atm], [pa])
        acum = self.aa("acum", 128, [6])
        self.cp("dve", acum.ap, pa.ap[:, 0:6], [pa], [acum])
        self.ckpt(35)
        atri = self.aa("atri", 128, [6, 128])
        self.tt("pool", atri.ap, self.trif.ap.unsqueeze(1).to_broadcast([128, 6, 128]), atm.ap[:, t, :].unsqueeze(2).to_broadcast([128, 6, 128]), ALU.mult,
                [self.trif, atm], [atri])
        self.ckpt(41)
        pbc0, pbc1 = self.PB[4], self.PB[5]
        af = atri.ap.rearrange("p h l -> p (h l)")
        self.mm(pbc0.ap[:, 0:512], self.onesf.ap, af[:, 0:512], [self.onesf, atri], [pbc0])
        self.mm(pbc1.ap[:, 0:256], self.onesf.ap, af[:, 512:768], [self.onesf, atri], [pbc1])
        self.ckpt(42)
        abc = self.ps_t[:, 4 * 512:4 * 512 + 768].rearrange("p (h l) -> p h l", h=6)
        EA = self.aa("EA", 128, [6, 128])
        self.act(EA.ap, abc, AF.Exp, [pbc0, pbc1], [EA])
        self.ckpt(43)
        Dm = self.aa("Dm", 128, [6, 128])
        self.tt("dve", Dm.ap[:, 0:4, :], pbc0.ap[:, 0:512].rearrange("p (h l) -> p h l", h=4), acum.ap[:, 0:4].unsqueeze(2).to_broadcast([128, 4, 128]),
                ALU.subtract, [pbc0, acum], [Dm])
        self.tt("dve", Dm.ap[:, 4:6, :], pbc1.ap[:, 0:256].rearrange("p (h l) -> p h l", h=2), acum.ap[:, 4:6].unsqueeze(2).to_broadcast([128, 2, 128]),
                ALU.subtract, [pbc1, acum], [Dm])
        self.ckpt(44)
        dec = self.aa("dec", 128, [6])
        self.ts("dve", Dm.ap, Dm.ap, 0.0, None, ALU.min, None, [Dm], [Dm])
        self.ckpt(45)
        self.act(dec.ap, Dm.ap[:, :, T - 1], AF.Exp, [Dm], [dec])
        self.ckpt(46)
        self.act(Dm.ap, Dm.ap, AF.Exp, [Dm], [Dm])
        self.ckpt(36)
        pg = self.bank("d")
        for g in range(2):
            self.mm(pg.ap[:, g * 128:(g + 1) * 128], Bc.ap[:, g, cs], Cc.ap[:, g, cs], [Bc, Cc], [pg])
        GmT = self.aa("GmT", 128, [2, 128])
        self.tt("dve", GmT.ap, pg.ap[:, 0:256].rearrange("p (g l) -> p g l", g=2), self.trif.ap.unsqueeze(1).to_broadcast([128, 2, 128]), ALU.mult, [pg, self.trif], [GmT])
        Mm = self.aa("Mm", 128, [6, 128], BF16)
        self.tt("dve", Mm.ap.rearrange("p (g j) l -> p g j l", g=2), Dm.ap.rearrange("p (g j) l -> p g j l", g=2),
                GmT.ap.unsqueeze(2).to_broadcast([128, 2, 3, 128]), ALU.mult, [Dm, GmT], [Mm])
        Cs = self.aa("Cs", 128, [6, 128], BF16)
        self.tt("pool", Cs.ap.rearrange("p (g j) l -> p g j l", g=2), EA.ap.rearrange("p (g j) l -> p g j l", g=2),
                Cc.ap[:, :, cs].unsqueeze(2).to_broadcast([128, 2, 3, 128]), ALU.mult, [EA, Cc], [Cs])
        xdtd = self.aa("xdtd", 128, [6, 64], BF16)
        self.tt("pool", xdtd.ap, xdt.ap, dec.ap.unsqueeze(2).to_broadcast([128, 6, 64]), ALU.mult, [xdt, dec], [xdtd])
        self.ckpt(37)
        py0, py1 = self.PB[0], self.PB[1]
        for h in range(6):
            pb = py0 if h < 4 else py1
            o = (h % 4) * 128
            self.mm(pb.ap[0:64, o:o + 128], xdt.ap[:, h, :], Mm.ap[:, h, :], [xdt, Mm], [pb], start=True, stop=False)
            self.mm(pb.ap[0:64, o:o + 128], P["STb"].ap[:, h, :], Cs.ap[:, h, :], [P["STb"], Cs], [pb], start=False, stop=True)
        yv = self.ps_t[0:64, 0:768].rearrange("p (h l) -> p h l", h=6)
        yz = self.aa("yz", 64, [6, 128])
        self.tt("pool", yz.ap, xsc.ap[:, :, cs], P["dskip"].ap.unsqueeze(2).to_broadcast([64, 6, 128]), ALU.mult, [xsc, P["dskip"]], [yz])
        self.tt("dve", yz.ap[:, 0:4, :], yz.ap[:, 0:4, :], py0.ap[0:64, 0:512].rearrange("p (h l) -> p h l", h=4), ALU.add, [yz, py0], [yz])
        self.tt("dve", yz.ap[:, 4:6, :], yz.ap[:, 4:6, :], py1.ap[0:64, 0:256].rearrange("p (h l) -> p h l", h=2), ALU.add, [yz, py1], [yz])
        self.tt("dve", yz.ap, yz.ap, zs.ap[:, :, cs], ALU.mult, [yz, zs], [yz])
        self.ckpt(38)
        self.ssd_gnorm(l, yz, mixs, cs, T)
        self.ckpt(39)
        pst = self.PB[2]
        for h in range(6):
            self.mm(pst.ap[:, h * 64:(h + 1) * 64], Btm.ap[:, h // 3, :], xdtd.ap[:, h, :], [Btm, xdtd], [pst])
        self.tt("dve", P["ST"].ap, P["ST"].ap, EA.ap[:, :, T - 1].unsqueeze(2).to_broadcast([128, 6, 64]), ALU.mult, [P["ST"], EA], [P["ST"]])
        self.tt("dve", P["ST"].ap, P["ST"].ap, pst.ap[:, 0:384].rearrange("p (h q) -> p h q", h=6), ALU.add, [P["ST"], pst], [P["ST"]])
        self.cp("act", P["STb"].ap, P["ST"].ap, [P["ST"]], [P["STb"]])
        self.arelease(m)

    def ssd_gnorm(self, l, yz, mixs, cs, T):
        P = self.L[l]
        m = self.amark()
        sq = self.aa("gsq", 64, [6, T], BF16)
        self.act(sq.ap, yz.ap, AF.Square, [yz], [sq])
        pb = self.bank("s")
        for g in range(2):
            for j in range(3):
                self.mm(pb.ap[0:64, g * T:(g + 1) * T], self.onesb.ap[0:64, 0:64], sq.ap[:, g * 3 + j, :], [self.onesb, sq], [pb], start=(j == 0), stop=(j == 2))
        rs = self.aa("grs", 64, [2, T])
        self.rsqrt(rs.ap, pb.ap[0:64, 0:2 * T].rearrange("p (g l) -> p g l", g=2), 1.0 / 192.0, [pb], [rs])
        self.tt("dve", yz.ap.rearrange("p (g j) l -> p g j l", g=2), yz.ap.rearrange("p (g j) l -> p g j l", g=2),
                rs.ap.unsqueeze(2).to_broadcast([64, 2, 3, T]), ALU.mult, [yz, rs], [yz])
        self.tt("dve", mixs.ap[:, :, cs], yz.ap, P["gssd"].ap.unsqueeze(2).to_broadcast([64, 6, T]), ALU.mult, [yz, P["gssd"]], [mixs])
        self.arelease(m)

    def ssd_final_outputs(self, l):
        P = self.L[l]
        O = self.O
        m = self.amark()
        stg = self.aa("stg", 128, [3, 128])
        for i in range(3):
            tb = self.bank("t")
            self.tr(tb.ap[:, 0:128], P["ST"].ap[:, 2 * i:2 * i + 2, :].rearrange("p a b -> p (a b)"), self.identf.ap, [P["ST"], self.identf], [tb])
            self.cp("act", stg.ap[:, i, :], tb.ap[:, 0:128], [tb], [stg])
        self.store(O["p_ssm"][l].rearrange("(i a) p n -> (a p) i n", a=2), stg.ap, [stg], "st_pssm%d" % l)
        for k in range(3):
            self.store(O["p_conv"][l][k, 0:384].rearrange("(h p) -> p h", p=64), P["hx"].ap[:, :, k], [P["hx"]], "st_pconv%d" % l, slow=True)
            self.store(O["p_conv"][l][k, 384:896].rearrange("(c p) -> p c", p=128), P["hbc"].ap[:, :, k], [P["hbc"]], "st_pconv%d" % l, slow=True)
        self.arelease(m)

    def mla_front(self, l, hT, nb, cos, sin):
        P = self.L[l]
        win = self.I["w_in"][l].rearrange("(c p) m -> p c m", p=128)
        cqn = self.aa("cqn", 128, [2, nb], BF16)
        lat = self.aa("lat", 128, [nb])
        kper = self.aa("kper", 32, [nb])
        m = self.amark()
        wb, wv = self.wload(win[:, :, C_CQ:C_CQ + 416], 128, 8, 416)
        self.ckpt(61)
        cqr = self.aa("cqr", 128, [2, nb])
        sq = self.aa("msq", 128, [nb], BF16)
        ps = self.bank("s")
        for c in range(2):
            pb = self.bank("d")
            for kc in range(8):
                self.mm(pb.ap[:, 0:nb], wv[:, kc, c * 128:(c + 1) * 128], hT.ap[:, kc, :], [wb, hT], [pb], start=(kc == 0), stop=(kc == 7))
            self.cp("dve", cqr.ap[:, c, :], pb.ap[:, 0:nb], [pb], [cqr])
            self.ckpt(62)
            self.act(sq.ap, cqr.ap[:, c, :], AF.Square, [cqr], [sq])
            self.mm(ps.ap[:, 0:nb], self.onesb.ap, sq.ap, [self.onesb, sq], [ps], start=(c == 0), stop=(c == 1))
            self.ckpt(63)
        rs = self.aa("mrs", 128, [nb])
        self.rsqrt(rs.ap, ps.ap[:, 0:nb], 1.0 / 256.0, [ps], [rs])
        self.ckpt(64)
        for c in range(2):
            self.stt(cqn.ap[:, c, :], cqr.ap[:, c, :], P["gql"].ap[:, c:c + 1], rs.ap, ALU.mult, ALU.mult, [cqr, P["gql"], rs], [cqn])
        self.ckpt(58)
        pb = self.bank("d")
        for kc in range(8):
            self.mm(pb.ap[:, 0:nb], wv[:, kc, 256:384], hT.ap[:, kc, :], [wb, hT], [pb], start=(kc == 0), stop=(kc == 7))
        self.act(sq.ap, pb.ap[:, 0:nb], AF.Square, [pb], [sq])
        ps = self.bank("s")
        self.mm(ps.ap[:, 0:nb], self.onesb.ap, sq.ap, [self.onesb, sq], [ps])
        rs2 = self.aa("mrs2", 128, [nb])
        self.rsqrt(rs2.ap, ps.ap[:, 0:nb], 1.0 / 128.0, [ps], [rs2])
        self.stt(lat.ap, pb.ap[:, 0:nb], P["gkv"].ap[:, 0:1], rs2.ap, ALU.mult, ALU.mult, [pb, P["gkv"], rs2], [lat])
        self.ckpt(59)
        pa = self.bank("d")
        for kc in range(8):
            self.mm(pa.ap[0:32, 0:nb], wv[:, kc, 384:416], hT.ap[:, kc, :], [wb, hT], [pa], start=(kc == 0), stop=(kc == 7))
        pr = self.bank("d")
        for kc in range(8):
            self.mm(pr.ap[0:32, 0:nb], P["wkr"].ap[:, kc, :], hT.ap[:, kc, :], [P["wkr"], hT], [pr], start=(kc == 0), stop=(kc == 7))
        t1 = self.aa("kt1", 32, [nb])
        self.tt("dve", t1.ap, pa.ap[0:32, 0:nb], cos.ap[0:32, :], ALU.mult, [pa, cos], [t1])
        self.tt("dve", kper.ap, pr.ap[0:32, 0:nb], sin.ap[0:32, :], ALU.mult, [pr, sin], [kper])
        self.tt("dve", kper.ap, kper.ap, t1.ap, ALU.add, [kper, t1], [kper])
        self.ckpt(60)
        self.arelease(m)
        return cqn, lat, kper

    def mla_q_head(self, l, h, cqn, nb, cos, sin, PP, qn_ap, qn_buf, qp_ap, qp_buf):
        P = self.L[l]
        m = self.amark()
        pn = self.bank("d")
        for c in range(2):
            self.mm(pn.ap[0:64, 0:nb], P["wqn"].ap[:, c, h, :], cqn.ap[:, c, :], [P["wqn"], cqn], [pn], start=(c == 0), stop=(c == 1))
        pa = self.bank("d")
        for c in range(2):
            self.mm(pa.ap[0:PP, 0:nb], P["wqp4"].ap[:, c, h, 0:PP], cqn.ap[:, c, :], [P["wqp4"], cqn], [pa], start=(c == 0), stop=(c == 1))
        pr = self.bank("d")
        for c in range(2):
            self.mm(pr.ap[0:PP, 0:nb], P["wqr4"].ap[:, c, h, 0:PP], cqn.ap[:, c, :], [P["wqr4"], cqn], [pr], start=(c == 0), stop=(c == 1))
        qpf = self.aa("qpf", PP, [nb])
        t1 = self.aa("qt1", PP, [nb])
        self.tt("dve", t1.ap, pa.ap[0:PP, 0:nb], cos.ap[0:PP, :], ALU.mult, [pa, cos], [t1])
        self.tt("dve", qpf.ap, pr.ap[0:PP, 0:nb], sin.ap[0:PP, :], ALU.mult, [pr, sin], [qpf])
        self.tt("dve", qpf.ap, qpf.ap, t1.ap, ALU.add, [qpf, t1], [qpf])
        sqn = self.aa("sqn", 64, [nb], BF16)
        sqp = self.aa("sqp", 32, [nb], BF16)
        self.act(sqn.ap, pn.ap[0:64, 0:nb], AF.Square, [pn], [sqn])
        self.act(sqp.ap, qpf.ap[0:32, :], AF.Square, [qpf], [sqp])
        MP = max(PP, 64)
        ps = self.bank("s")
        self.mm(ps.ap[0:MP, 0:nb], self.onesb.ap[0:64, 0:MP], sqn.ap, [self.onesb, sqn], [ps], start=True, stop=False)
        self.mm(ps.ap[0:MP, 0:nb], self.onesb.ap[0:32, 0:MP], sqp.ap, [self.onesb, sqp], [ps], start=False, stop=True)
        rs = self.aa("qrs", MP, [nb])
        self.rsqrt(rs.ap, ps.ap[0:MP, 0:nb], 1.0 / 96.0, [ps], [rs])
        self.stt(qn_ap, pn.ap[0:64, 0:nb], P["gqkn"].ap[:, 0:1], rs.ap[0:64, :], ALU.mult, ALU.mult, [pn, P["gqkn"], rs], [qn_buf])
        self.stt(qp_ap, qpf.ap, P["gqkp"].ap[0:PP, 0:1], rs.ap[0:PP, :], ALU.mult, ALU.mult, [qpf, P["gqkp"], rs], [qp_buf])
        self.arelease(m)

    def mla_prompt(self, l, blk, hT, cos, sin, mixm):
        P = self.L[l]
        O = self.O
        nb = NB
        m0 = self.amark()
        cqn, lat, kper = self.mla_front(l, hT, nb, cos, sin)
        c0 = blk * NB
        self.cp("act", P["latT"].ap[:, c0:c0 + nb], lat.ap, [lat], [P["latT"]])
        self.cp("act", P["kpeT"].ap[:, c0:c0 + nb], kper.ap, [kper], [P["kpeT"]])
        self.ckpt(51)
        for t in range(4):
            j = blk * 4 + t
            cs = slice(t * 128, (t + 1) * 128)
            m1 = self.amark()
            lt = self.aa("lt", 128, [128])
            self.fm2tm(lt.ap, lt, lat.ap[:, cs], lat, 128, 128)
            self.store(O["p_lat"][l, c0 + t * 128:c0 + (t + 1) * 128, :], lt.ap, [lt], "st_lt", )
            kt = self.aa("kt", 128, [32])
            self.fm2tm(kt.ap, kt, kper.ap[:, cs], kper, 32, 128)
            self.store(O["p_kpe"][l, c0 + t * 128:c0 + (t + 1) * 128, :], kt.ap, [kt], "st_kt")
            self.ckpt(52)
            pk = self.bank("d")
            self.mm(pk.ap[:, 0:384], P["latT"].ap[:, c0 + t * 128:c0 + (t + 1) * 128], P["wk"].ap, [P["latT"], P["wk"]], [pk])
            ksq = self.aa("ksq", 128, [384], BF16)
            self.act(ksq.ap, pk.ap[:, 0:384], AF.Square, [pk], [ksq])
            ss = self.aa("kss", 128, [8])
            self.red(ss.ap[:, 0:6], ksq.ap.rearrange("p (h d) -> p h d", h=6), ALU.add, [ksq], [ss])
            kq = self.aa("kq", 128, [32])
            self.tt("dve", kq.ap, kt.ap, kt.ap, ALU.mult, [kt], [kq])
            self.red(ss.ap[:, 6:7], kq.ap, ALU.add, [kq], [ss])
            self.ts("dve", ss.ap[:, 0:6], ss.ap[:, 0:6], ss.ap[:, 6:7], None, ALU.add, None, [ss], [ss])
            self.rsqrt(P["rks"].ap[:, j, :], ss.ap[:, 0:6], 1.0 / 96.0, [ss], [P["rks"]])
            self.ts("dve", P["rks"].ap[:, j, :], P["rks"].ap[:, j, :], ATTN_SCALE, None, ALU.mult, None, [P["rks"]], [P["rks"]])
            self.ckpt(53)
            pv = self.bank("d")
            self.mm(pv.ap[:, 0:384], P["latT"].ap[:, c0 + t * 128:c0 + (t + 1) * 128], P["wv"].ap, [P["latT"], P["wv"]], [pv])
            self.cp("act", P["vaug"].ap[:, j, :, 0:64], pv.ap[:, 0:384].rearrange("p (h d) -> p h d", h=6), [pv], [P["vaug"]])
            self.ckpt(54)
            self.arelease(m1)
        qabs = self.aa("qabs", 128, [6, nb], BF16)
        qpa = self.aa("qpa", 32, [6, nb], BF16)
        for h in range(6):
            m1 = self.amark()
            qn = self.aa("qn", 64, [nb], BF16)
            self.mla_q_head(l, h, cqn, nb, cos, sin, 32, qn.ap, qn, qpa.ap[:, h, :], qpa)
            pq = self.bank("s")
            self.mm(pq.ap[:, 0:nb], P["wkT"].ap[:, h, :], qn.ap, [P["wkT"], qn], [pq])
            self.cp("act", qabs.ap[:, h, :], pq.ap[:, 0:nb], [pq], [qabs])
            self.arelease(m1)
        self.ckpt(55)
        OB = [self.PB[0], self.PB[1], self.PB[2], self.PB[3]]
        first = [True] * 4
        njt = blk * 4 + 4
        PTs = [self.aa("PT", 128, [nb], BF16) for _ in range(3)]
        pi = 0
        for h in range(6):
            for j in range(njt):
                tq = max(0, j - blk * 4)
                q0 = tq * 128
                w = nb - q0
                ps = self.PB[4 + (pi % 4)]
                self.mm(ps.ap[:, 0:w], P["latT"].ap[:, j * 128:(j + 1) * 128], qabs.ap[:, h, q0:nb], [P["latT"], qabs], [ps], start=True, stop=False)
                self.mm(ps.ap[:, 0:w], P["kpeT"].ap[:, j * 128:(j + 1) * 128], qpa.ap[:, h, q0:nb], [P["kpeT"], qpa], [ps], start=False, stop=True)
                PT = PTs[pi % 3]
                pi += 1
                self.act(PT.ap[:, 0:w], ps.ap[:, 0:w], AF.Exp, [ps, P["rks"], P["negM"]], [PT], scale=P["rks"].ap[:, j, h:h + 1], bias=P["negM"].ap[:, 0:1])
                if j >= blk * 4:
                    self.tt("pool", PT.ap[:, 0:128], PT.ap[:, 0:128], self.trib.ap, ALU.mult, [PT, self.trib], [PT])
                for t in range(tq, 4):
                    o = (t - tq) * 128
                    self.mm(OB[t].ap[:, h * 66:(h + 1) * 66], PT.ap[:, o:o + 128], P["vaug"].ap[:, j, h, :], [PT, P["vaug"]], [OB[t]],
                            start=first[t], stop=(h == 5 and j == blk * 4 + t), sgc=True)
                    first[t] = False
        self.ckpt(56)
        for t in range(4):
            m1 = self.amark()
            ov = OB[t].ap[:, 0:396].rearrange("p (h d) -> p h d", h=6)
            rinv = self.aa("rinv", 128, [6, 1])
            self.recip(rinv.ap, ov[:, :, 64:65], [OB[t]], [rinv])
            on = self.aa("on", 128, [6, 64])
            self.tt("dve", on.ap, ov[:, :, 0:64], rinv.ap.to_broadcast([128, 6, 64]), ALU.mult, [OB[t], rinv], [on])
            junk = self.aa("junk", 128, [384], BF16)
            ssq = self.aa("ssq", 128, [1])
            self.act(junk.ap, on.ap.rearrange("p h d -> p (h d)"), AF.Square, [on], [junk, ssq], accum=ssq.ap)
            self.rsqrt(ssq.ap, ssq.ap, 1.0 / 384.0, [ssq], [ssq])
            mn = self.aa("mn", 128, [384], BF16)
            self.stt(mn.ap, on.ap.rearrange("p h d -> p (h d)"), ssq.ap[:, 0:1], P["gmla_bc"].ap, ALU.mult, ALU.mult, [on, ssq, P["gmla_bc"]], [mn])
            tb = self.bank("t")
            tv = tb.ap.bitcast(BF16)
            for c in range(3):
                self.tr(tv[:, c * 128:(c + 1) * 128], mn.ap[:, c * 128:(c + 1) * 128], self.identb.ap, [mn, self.identb], [tb])
            self.cp("act", mixm.ap[:, :, t * 128:(t + 1) * 128], tv[:, 0:384].rearrange("p (c q) -> p c q", c=3), [tb], [mixm])
            self.arelease(m1)
        self.arelease(m0)

    def pool_prompt(self, l, blk, hT, mixp, last):
        P = self.L[l]
        nb = NB
        win = self.I["w_in"][l].rearrange("(c p) m -> p c m", p=128)
        m0 = self.amark()
        LW = 15 + nb
        up = self.aa("up", 128, [2, LW])
        wa = self.aa("pwa", 128, [2, LW])
        wb_ = self.aa("pwb", 128, [2, LW])
        pooled = self.aa("pooled", 128, [2, nb], BF16)
        self.memset("pool", wa.ap, 0.0, [wa])
        self.memset("pool", wb_.ap, 0.0, [wb_])
        wb, wv = self.wload(win[:, :, C_U:C_U + 256], 128, 8, 256)
        self.cp("dve", up.ap[:, :, 0:15], P["hu"].ap, [P["hu"]], [up])
        for c in range(2):
            pb = self.bank("d")
            for kc in range(8):
                self.mm(pb.ap[:, 0:nb], wv[:, kc, c * 128:(c + 1) * 128], hT.ap[:, kc, :], [wb, hT], [pb], start=(kc == 0), stop=(kc == 7))
            self.cp("act", up.ap[:, c, 15:LW], pb.ap[:, 0:nb], [pb], [up])
        self.cp("dve", P["hu"].ap, up.ap[:, :, nb:LW], [up], [P["hu"]])
        src = up
        dsts = [wa, wb_, wa, wb_]
        grp = [(0, 0), (0, 64), (1, 0), (1, 64)]
        for g in range(4):
            sh = 1 << g
            dst = dsts[g]
            self.tt("pool", dst.ap[:, :, sh:LW], src.ap[:, :, sh:LW], src.ap[:, :, 0:LW - sh], ALU.add, [src], [dst])
            c, po = grp[g]
            self.stt(pooled.ap[po:po + 64, c, :], dst.ap[po:po + 64, c, 15:LW], self.kc.ap[po:po + 64, 4 + c:5 + c], up.ap[po:po + 64, c, 15:LW],
                     ALU.mult, ALU.subtract, [dst, self.kc, up], [pooled])
            if blk == 0:
                tmp = self.aa("ptmp", 128, [16])
                self.tt("dve", tmp.ap[po:po + 64, :], dst.ap[po:po + 64, c, 15:31], self.rct.ap[po:po + 64, c, :], ALU.mult, [dst, self.rct], [tmp])
                self.tt("dve", pooled.ap[po:po + 64, c, 0:16], tmp.ap[po:po + 64, :], up.ap[po:po + 64, c, 15:31], ALU.subtract, [tmp, up], [pooled])
            src = dst
        for c in range(2):
            pb = self.bank("d")
            self.mm(pb.ap[:, 0:nb], P["wpool"].ap[:, c, :], pooled.ap[:, c, :], [P["wpool"], pooled], [pb])
            self.ts("dve", mixp.ap[:, c, :], pb.ap[:, 0:nb], P["pscale"].ap[:, c:c + 1], None, ALU.mult, None, [pb, P["pscale"]], [mixp])
        if last:
            for c in range(2):
                self.store(self.O["p_pool"][l][:, c * 128:(c + 1) * 128].rearrange("t p -> p t"), P["hu"].ap[:, c, :], [P["hu"]], "st_ppool%d" % l, slow=True)
        self.arelease(m0)

    def out_proj(self, l, xT, mixs, mixm, mixp, nb, sample):
        P = self.L[l]
        wo = self.I["w_out"][l]
        gate = P["mod"]
        for half in range(2):
            cs_ = slice(half * 512, (half + 1) * 512)
            wbA, wvA = self.wload(wo[0:384, cs_].rearrange("(h p) m -> p h m", p=64), 64, 6, 512)
            if sample:
                wbA2, wvA2 = self.wload(wo[384:768, cs_].rearrange("(h p) m -> p h m", p=64), 64, 6, 512)
                wbB, wvB = self.wload(wo[768:D, cs_].rearrange("(c p) m -> p c m", p=128), 128, 2, 512)
            else:
                wbB, wvB = self.wload(wo[384:D, cs_].rearrange("(c p) m -> p c m", p=128), 128, 5, 512)
            for mi in range(4):
                mc = half * 4 + mi
                ms = slice(mi * 128, (mi + 1) * 128)
                pb = self.bank("d")
                ops = []
                for h in range(6):
                    ops.append((wvA[:, h, ms], wbA, mixs.ap[:, h, :], mixs))
                if sample:
                    for h in range(6):
                        ops.append((wvA2[:, h, ms], wbA2, mixm.ap[:, h, :], mixm))
                    for c in range(2):
                        ops.append((wvB[:, c, ms], wbB, mixp.ap[:, c, :], mixp))
                else:
                    for c in range(3):
                        ops.append((wvB[:, c, ms], wbB, mixm.ap[:, c, :], mixm))
                    for c in range(2):
                        ops.append((wvB[:, 3 + c, ms], wbB, mixp.ap[:, c, :], mixp))
                for i, (lh, lb, rh, rb) in enumerate(ops):
                    self.mm(pb.ap[:, 0:nb], lh, rh, [lb, rb], [pb], start=(i == 0), stop=(i == len(ops) - 1))
                self.resid(xT, mc, pb, gate, 16 + mc, nb, sample)

    def resid(self, xT, mc, pb, gate, gi, nb, sample):
        if not sample:
            self.stt(xT.ap[:, mc, :], pb.ap[:, 0:nb], gate.ap[:, gi, 0:1], xT.ap[:, mc, :], ALU.mult, ALU.add, [pb, gate, xT], [xT])
        else:
            m = self.amark()
            t = self.aa("rtmp", 128, [nb])
            self.tt("dve", t.ap, pb.ap[:, 0:nb], gate.ap[:, gi, 1:17], ALU.mult, [pb, gate], [t])
            self.tt("dve", xT.ap[:, mc, :], xT.ap[:, mc, :], t.ap, ALU.add, [xT, t], [xT])
            self.arelease(m)

    def ffn(self, l, xT, hT, nb, sample):
        P = self.L[l]
        I = self.I
        m0 = self.amark()
        hid = self.aa("hid", 128, [NFC, nb], BF16)
        sgs = [self.aa("sg", 128, [nb]) for _ in range(2)]
        wg = I["w_gate"][l].rearrange("(c p) m -> p c m", p=128)
        wu = I["w_up"][l].rearrange("(c p) m -> p c m", p=128)
        fi = 0
        for t in range(6):
            ncol = 512 if t < 5 else 256
            wbg, wvg = self.wload(wg[:, :, t * 512:t * 512 + ncol], 128, 8, ncol)
            wbu, wvu = self.wload(wu[:, :, t * 512:t * 512 + ncol], 128, 8, ncol)
            for i in range(ncol // 128):
                fc = t * 4 + i
                pg = self.bank("d")
                for kc in range(8):
                    self.mm(pg.ap[:, 0:nb], wvg[:, kc, i * 128:(i + 1) * 128], hT.ap[:, kc, :], [wbg, hT], [pg], start=(kc == 0), stop=(kc == 7))
                pu = self.bank("d")
                for kc in range(8):
                    self.mm(pu.ap[:, 0:nb], wvu[:, kc, i * 128:(i + 1) * 128], hT.ap[:, kc, :], [wbu, hT], [pu], start=(kc == 0), stop=(kc == 7))
                sg = sgs[fi % 2]
                fi += 1
                self.act(sg.ap, pg.ap[:, 0:nb], AF.Silu, [pg], [sg])
                self.tt("dve", hid.ap[:, fc, :], sg.ap, pu.ap[:, 0:nb], ALU.mult, [sg, pu], [hid])
        wd = I["w_down"][l].rearrange("(c p) m -> p c m", p=128)
        for mg in range(4):
            pbs = [self.bank("d"), self.bank("d")]
            for kh in range(2):
                wb, wv = self.wload(wd[:, kh * 11:(kh + 1) * 11, mg * 256:(mg + 1) * 256], 128, 11, 256)
                for mi in range(2):
                    for k in range(11):
                        self.mm(pbs[mi].ap[:, 0:nb], wv[:, k, mi * 128:(mi + 1) * 128], hid.ap[:, kh * 11 + k, :], [wb, hid], [pbs[mi]],
                                start=(kh == 0 and k == 0), stop=(kh == 1 and k == 10))
            for mi in range(2):
                mc = mg * 2 + mi
                self.resid(xT, mc, pbs[mi], P["mod"], 40 + mc, nb, sample)
        self.arelease(m0)

    def prompt_block(self, blk):
        I, O = self.I, self.O
        nb = NB
        c0 = blk * NB
        last = (blk == NBLK - 1)
        m0 = self.amark()
        xT = self.aa("xT", 128, [8, nb])
        hT = self.aa("hT", 128, [8, nb], BF16)
        mx = self.amark()
        xins = [self.aa("xin", 128, [D]) for _ in range(2)]
        for t in range(4):
            xin = xins[t % 2]
            self.dma("sp", xin.ap, I["xp"][c0 + t * 128:c0 + (t + 1) * 128, :], (), [xin], "xin%d" % (t % 2))
            for half in range(2):
                tb = self.bank("t")
                for i in range(4):
                    c = half * 4 + i
                    self.tr(tb.ap[:, i * 128:(i + 1) * 128], xin.ap[:, c * 128:(c + 1) * 128], self.identf.ap, [xin, self.identf], [tb])
                self.cp("act" if half else "dve", xT.ap[:, half * 4:half * 4 + 4, t * 128:(t + 1) * 128], tb.ap.rearrange("p (c q) -> p c q", c=4), [tb], [xT])
        self.arelease(mx)
        self.ckpt(20)
        cos, sin = self.rope_tables(c0, 1, nb)
        self.ckpt(21)
        for l in range(DEPTH):
            P = self.L[l]
            m1 = self.amark()
            mixs = self.aa("mixs", 64, [6, nb], BF16)
            mixm = self.aa("mixm", 128, [3, nb], BF16)
            mixp = self.aa("mixp", 128, [2, nb], BF16)
            self.norm_mod(xT, hT, P["A1"], P["mod"], 0, nb, False)
            self.ckpt(22)
            self.ssd_prompt(l, hT, mixs, last)
            self.ckpt(23)
            self.mla_prompt(l, blk, hT, cos, sin, mixm)
            self.ckpt(24)
            self.pool_prompt(l, blk, hT, mixp, last)
            self.ckpt(25)
            self.out_proj(l, xT, mixs, mixm, mixp, nb, False)
            self.ckpt(26)
            self.norm_mod(xT, hT, P["A2"], P["mod"], 24, nb, False)
            self.ckpt(27)
            self.ffn(l, xT, hT, nb, False)
            self.ckpt(28)
            self.arelease(m1)
        youts = [self.aa("yout", 128, [D]) for _ in range(2)]
        for t in range(4):
            yo = youts[t % 2]
            for half in range(2):
                tb = self.bank("t")
                for i in range(4):
                    c = half * 4 + i
                    self.tr(tb.ap[:, i * 128:(i + 1) * 128], xT.ap[:, c, t * 128:(t + 1) * 128], self.identf.ap, [xT, self.identf], [tb])
                self.cp("act" if half else "dve", yo.ap[:, half * 512:(half + 1) * 512], tb.ap, [tb], [yo])
            self.store(O["y_p"][c0 + t * 128:c0 + (t + 1) * 128, :], yo.ap, [yo], "st_y%d" % (t % 2))
        self.arelease(m0)

    def ssd_sample(self, l, hT, mixs):
        P = self.L[l]
        I, O = self.I, self.O
        nb = NS
        win = I["w_in"][l].rearrange("(c p) m -> p c m", p=128)
        m0 = self.amark()
        zs = self.aa("zs", 64, [6, nb], BF16)
        xsc = self.aa("xsc", 64, [6, nb])
        BCc = self.aa("BCc", 128, [4, nb])
        xnx = self.aa("xnx", 64, [6, nb])
        xnbc = self.aa("xnbc", 128, [4, nb])
        cvx = self.aa("cvx", 64, [6, 48])
        cvbc = self.aa("cvbc", 128, [4, 48])
        m1 = self.amark()
        cvtm = self.aa("cvtm", 48, [896])
        self.dma("sp", cvtm.ap, I["sconv"][l].rearrange("s k c -> (s k) c"), (), [cvtm], "cvtm")
        for h in range(6):
            tb = self.bank("t")
            self.tr(tb.ap[0:64, 0:48], cvtm.ap[:, h * 64:(h + 1) * 64], self.identf.ap[0:48, 0:48], [cvtm, self.identf], [tb])
            self.cp("act", cvx.ap[:, h, :], tb.ap[0:64, 0:48], [tb], [cvx])
        for ci in range(4):
            tb = self.bank("t")
            self.tr(tb.ap[:, 0:48], cvtm.ap[:, 384 + ci * 128:384 + (ci + 1) * 128], self.identf.ap[0:48, 0:48], [cvtm, self.identf], [tb])
            self.cp("act", cvbc.ap[:, ci, :], tb.ap[:, 0:48], [tb], [cvbc])
        self.arelease(m1)
        wb, wv = self.wload(win[:, :, C_Z:C_Z + 384], 128, 8, 384)
        for h in range(6):
            pb = self.bank("d")
            for kc in range(8):
                self.mm(pb.ap[0:64, 0:nb], wv[:, kc, h * 64:(h + 1) * 64], hT.ap[:, kc, :], [wb, hT], [pb], start=(kc == 0), stop=(kc == 7))
            self.act(zs.ap[:, h, :], pb.ap[0:64, 0:nb], AF.Silu, [pb], [zs])
        acc = self.aa("sacc", 128, [nb])

        def conv1(PP, pb, new_ap, new_buf, cv_ap, cv_buf, w4, wbuf, bcol, bbuf, out_ap, out_buf):
            self.cp("act", new_ap, pb.ap[0:PP, 0:nb], [pb], [new_buf])
            self.ts("dve", acc.ap[0:PP, :], new_ap, w4[:, 3:4], bcol, ALU.mult, ALU.add, [new_buf, wbuf, bbuf], [acc])
            cvv = cv_ap.rearrange("p (s k) -> p s k", k=3)
            for k in range(3):
                self.stt(acc.ap[0:PP, :], cvv[:, :, k], w4[:, k:k + 1], acc.ap[0:PP, :], ALU.mult, ALU.add, [cv_buf, wbuf, acc], [acc])
            self.act(out_ap, acc.ap[0:PP, :], AF.Silu, [acc], [out_buf])
        wb, wv = self.wload(win[:, :, C_XS:C_XS + 512], 128, 8, 512)
        for h in range(6):
            pb = self.bank("d")
            for kc in range(8):
                self.mm(pb.ap[0:64, 0:nb], wv[:, kc, h * 64:(h + 1) * 64], hT.ap[:, kc, :], [wb, hT], [pb], start=(kc == 0), stop=(kc == 7))
            conv1(64, pb, xnx.ap[:, h, :], xnx, cvx.ap[:, h, :], cvx, P["cwx"].ap[:, h, :], P["cwx"], P["cbx"].ap[:, h:h + 1], P["cbx"], xsc.ap[:, h, :], xsc)

        def bc1(ci, wb, wv, col0):
            pb = self.bank("d")
            for kc in range(8):
                self.mm(pb.ap[:, 0:nb], wv[:, kc, col0:col0 + 128], hT.ap[:, kc, :], [wb, hT], [pb], start=(kc == 0), stop=(kc == 7))
            conv1(128, pb, xnbc.ap[:, ci, :], xnbc, cvbc.ap[:, ci, :], cvbc, P["cwbc"].ap[:, ci, :], P["cwbc"], P["cbbc"].ap[:, ci:ci + 1], P["cbbc"], BCc.ap[:, ci, :], BCc)
        bc1(0, wb, wv, 384)
        wb, wv = self.wload(win[:, :, 896:1286], 128, 8, 390)
        bc1(1, wb, wv, 0)
        bc1(2, wb, wv, 128)
        bc1(3, wb, wv, 256)
        pbd = self.bank("s")
        for kc in range(8):
            self.mm(pbd.ap[0:16, 0:6], hT.ap[:, kc, :], wv[:, kc, 384:390], [wb, hT], [pbd], start=(kc == 0), stop=(kc == 7))
        dtea = self.aa("dtea", 16, [12])
        self.tt("dve", dtea.ap[:, 0:6], pbd.ap[0:16, 0:6], P["dtb"].ap[0:16, :], ALU.add, [pbd, P["dtb"]], [dtea])
        self.act(dtea.ap[:, 0:6], dtea.ap[:, 0:6], AF.Exp, [dtea], [dtea])
        self.act(dtea.ap[:, 0:6], dtea.ap[:, 0:6], AF.Ln, [dtea], [dtea], bias=self.onesf.ap[0:16, 0:1])
        self.tt("dve", dtea.ap[:, 6:12], dtea.ap[:, 0:6], P["negA"].ap[0:16, :], ALU.mult, [dtea, P["negA"]], [dtea])
        self.act(dtea.ap[:, 6:12], dtea.ap[:, 6:12], AF.Exp, [dtea], [dtea])
        ev = self.dma("sp", O["s_conv"][l][:, 0:2, :], I["sconv"][l][:, 1:3, :], (), (), "st_sconv_cp%d" % l)
        self.out_events.append(ev)
        cvnew = self.aa("cvnew", 16, [896])
        for h in range(6):
            self.fm2tm(cvnew.ap[:, h * 64:(h + 1) * 64], cvnew, xnx.ap[:, h, :], xnx, 64, 16)
        for ci in range(4):
            self.fm2tm(cvnew.ap[:, 384 + ci * 128:384 + (ci + 1) * 128], cvnew, xnbc.ap[:, ci, :], xnbc, 128, 16)
        self.store(O["s_conv"][l][:, 2, :], cvnew.ap, [cvnew], "st_sconv%d" % l)
        R = self.aa("Rdt", 16, [16, 12])
        self.tt("dve", R.ap, dtea.ap.unsqueeze(1).to_broadcast([16, 16, 12]), self.identf.ap[0:16, 0:16].unsqueeze(2).to_broadcast([16, 16, 12]), ALU.mult,
                [dtea, self.identf], [R])
        pr = self.bank("s")
        self.mm(pr.ap[0:64, 0:192], self.onesf.ap[0:16, 0:64], R.ap.rearrange("p s j -> p (s j)"), [self.onesf, R], [pr])
        dtb64 = self.aa("dtb64", 64, [16, 12])
        self.cp("dve", dtb64.ap, pr.ap[0:64, 0:192].rearrange("p (s j) -> p s j", s=16), [pr], [dtb64])
        xdt = self.aa("xdts", 64, [16, 6])
        self.tt("dve", xdt.ap, xsc.ap.rearrange("p h s -> p s h"), dtb64.ap[:, :, 0:6], ALU.mult, [xsc, dtb64], [xdt])
        BCtm = self.aa("BCtm", 16, [4, 128])
        for ci in range(4):
            self.fm2tm(BCtm.ap[:, ci, :], BCtm, BCc.ap[:, ci, :], BCc, 128, 16)
        ysm = self.aa("ysm", 64, [16, 6])
        hss = [self.aa("hs", 64, [6, 128]) for _ in range(2)]
        hns = [self.aa("hn", 64, [6, 128]) for _ in range(2)]
        tmp = self.aa("stmp", 64, [6, 128])
        BCd = self.aa("BCd", 16, [512])
        for s in range(NS):
            hs, hn = hss[s % 2], hns[s % 2]
            self.dma("sp", hs.ap, I["sssm"][l, s].rearrange("h p n -> p h n"), (), [hs], "hs%d" % (s % 2))
            self.ts("dve", BCd.ap, BCtm.ap.rearrange("p c n -> p (c n)"), self.identf.ap[0:16, s:s + 1], None, ALU.mult, None, [BCtm, self.identf], [BCd])
            pbc = self.bank("d")
            self.mm(pbc.ap[0:64, 0:512], self.onesf.ap[0:16, 0:64], BCd.ap, [self.onesf, BCd], [pbc])
            bv = pbc.ap[0:64, 0:256].rearrange("p (g n) -> p g n", g=2)
            cv = pbc.ap[0:64, 256:512].rearrange("p (g n) -> p g n", g=2)
            self.tt("dve", hn.ap, hs.ap, dtb64.ap[:, s, 6:12].unsqueeze(2).to_broadcast([64, 6, 128]), ALU.mult, [hs, dtb64], [hn])
            self.tt("dve", tmp.ap.rearrange("p (g j) n -> p g j n", g=2), bv.unsqueeze(2).to_broadcast([64, 2, 3, 128]),
                    xdt.ap[:, s, :].rearrange("p (g j) -> p g j", g=2).unsqueeze(3).to_broadcast([64, 2, 3, 128]), ALU.mult, [pbc, xdt], [tmp])
            self.tt("dve", hn.ap, hn.ap, tmp.ap, ALU.add, [hn, tmp], [hn])
            self.store(O["s_ssm"][l, s].rearrange("h p n -> p h n"), hn.ap, [hn], "st_hn%d" % (s % 2))
            self.tt("dve", tmp.ap.rearrange("p (g j) n -> p g j n", g=2), hn.ap.rearrange("p (g j) n -> p g j n", g=2),
                    cv.unsqueeze(2).to_broadcast([64, 2, 3, 128]), ALU.mult, [hn, pbc], [tmp])
            self.red(ysm.ap[:, s, :], tmp.ap, ALU.add, [tmp], [ysm])
        yz = self.aa("yzs", 64, [6, nb])
        self.tt("dve", yz.ap, xsc.ap, P["dskip"].ap.unsqueeze(2).to_broadcast([64, 6, nb]), ALU.mult, [xsc, P["dskip"]], [yz])
        self.tt("dve", yz.ap, yz.ap, ysm.ap.rearrange("p s h -> p h s"), ALU.add, [yz, ysm], [yz])
        self.tt("dve", yz.ap, yz.ap, zs.ap, ALU.mult, [yz, zs], [yz])
        self.ssd_gnorm(l, yz, mixs, slice(0, nb), nb)
        self.arelease(m0)

    def pool_sample(self, l, hT, mixp):
        P = self.L[l]
        I, O = self.I, self.O
        nb = NS
        win = I["w_in"][l].rearrange("(c p) m -> p c m", p=128)
        m0 = self.amark()
        upad = self.aa("upads", 128, [2, 16, 16])
        unew = self.aa("unew", 128, [2, nb])
        pooled = self.aa("pooleds", 128, [2, nb], BF16)
        for i in range(2):
            ptm = self.aa("ptm%d" % i, 120, [256])
            self.dma("sp", ptm.ap, I["spool"][l, 8 * i:8 * i + 8].rearrange("s t c -> (s t) c"), (), [ptm], "ptm%d" % i)
            for c in range(2):
                tb = self.bank("t")
                self.tr(tb.ap[:, 0:120], ptm.ap[:, c * 128:(c + 1) * 128], self.identf.ap[0:120, 0:120], [ptm, self.identf], [tb])
                self.cp("act", upad.ap[:, c, 8 * i:8 * i + 8, 0:15], tb.ap[:, 0:120].rearrange("p (s t) -> p s t", s=8), [tb], [upad])
        wb, wv = self.wload(win[:, :, C_U:C_U + 256], 128, 8, 256)
        for c in range(2):
            pb = self.bank("d")
            for kc in range(8):
                self.mm(pb.ap[:, 0:nb], wv[:, kc, c * 128:(c + 1) * 128], hT.ap[:, kc, :], [wb, hT], [pb], start=(kc == 0), stop=(kc == 7))
            self.cp("act", unew.ap[:, c, :], pb.ap[:, 0:nb], [pb], [unew])
            self.cp("dve", upad.ap[:, c, :, 15], unew.ap[:, c, :], [unew], [upad])
        W = self.aa("Wsum", 128, [nb])
        grp = [(0, 0), (0, 64), (1, 0), (1, 64)]
        for g in range(4):
            w = 2 << g
            c, po = grp[g]
            self.red(W.ap[po:po + 64, :], upad.ap[po:po + 64, c, :, 16 - w:16], ALU.add, [upad], [W])
            self.stt(pooled.ap[po:po + 64, c, :], W.ap[po:po + 64, :], self.kc.ap[po:po + 64, 4 + c:5 + c], unew.ap[po:po + 64, c, :],
                     ALU.mult, ALU.subtract, [W, self.kc, unew], [pooled])
        for c in range(2):
            pb = self.bank("d")
            self.mm(pb.ap[:, 0:nb], P["wpool"].ap[:, c, :], pooled.ap[:, c, :], [P["wpool"], pooled], [pb])
            self.ts("dve", mixp.ap[:, c, :], pb.ap[:, 0:nb], P["pscale"].ap[:, c:c + 1], None, ALU.mult, None, [pb, P["pscale"]], [mixp])
        ev = self.dma("sp", O["s_pool"][l][:, 0:14, :], I["spool"][l][:, 1:15, :], (), (), "st_spool_cp%d" % l)
        self.out_events.append(ev)
        ut = self.aa("ut", 16, [256])
        for c in range(2):
            self.fm2tm(ut.ap[:, c * 128:(c + 1) * 128], ut, unew.ap[:, c, :], unew, 128, 16)
        self.store(O["s_pool"][l][:, 14, :], ut.ap, [ut], "st_spool%d" % l)
        self.arelease(m0)

    def mla_sample(self, l, hT, cos, sin, mixm):
        P = self.L[l]
        I, O = self.I, self.O
        nb = NS
        m0 = self.amark()
        cqn, lat, kper = self.mla_front(l, hT, nb, cos, sin)
        latnewT = self.aa("latnewT", 128, [128], BF16)
        latnew_tm = self.aa("latnew_tm", 128, [128], BF16)
        kpenewT4 = self.aa("kpenewT4", 128, [128], BF16)
        sspnew = self.aa("sspnew", 128, [1])
        for b in (latnewT, latnew_tm, kpenewT4, sspnew):
            self.memset("pool", b.ap, 0.0, [b])
        self.cp("act", latnewT.ap[:, 0:nb], lat.ap, [lat], [latnewT])
        self.cp("act", kpenewT4.ap[0:32, 0:nb], kper.ap, [kper], [kpenewT4])
        lnt = self.aa("lnt", 16, [128])
        self.fm2tm(lnt.ap, lnt, lat.ap, lat, 128, 16)
        self.store(O["s_lat"][l], lnt.ap, [lnt], "st_slat%d" % l)
        self.cp("dve", latnew_tm.ap[0:16, :], lnt.ap, [lnt], [latnew_tm])
        knt = self.aa("knt", 16, [32])
        self.fm2tm(knt.ap, knt, kper.ap, kper, 32, 16)
        self.store(O["s_kpe"][l], knt.ap, [knt], "st_skpe%d" % l)
        ksq0 = self.aa("ksq0", 16, [32])
        self.tt("dve", ksq0.ap, knt.ap, knt.ap, ALU.mult, [knt], [ksq0])
        self.red(sspnew.ap[0:16, :], ksq0.ap, ALU.add, [ksq0], [sspnew])
        qp4 = self.aa("qp4", 128, [6, nb], BF16)
        qabs = self.aa("qabss", 128, [nb, 6], BF16)
        for h in range(6):
            m1 = self.amark()
            qn = self.aa("qn", 64, [nb], BF16)
            self.mla_q_head(l, h, cqn, nb, cos, sin, 128, qn.ap, qn, qp4.ap[:, h, :], qp4)
            pq = self.bank("s")
            self.mm(pq.ap[:, 0:nb], P["wkT"].ap[:, h, :], qn.ap, [P["wkT"], qn], [pq])
            self.cp("act", qabs.ap[:, :, h], pq.ap[:, 0:nb], [pq], [qabs])
            self.arelease(m1)
        qpad = self.aa("qpad", 128, [nb, 4, 6], BF16)
        for i in range(4):
            self.ts("dve", qpad.ap[:, :, i, :], qp4.ap.rearrange("p h s -> p s h"), self.bmask.ap[:, i:i + 1], None, ALU.mult, None, [qp4, self.bmask], [qpad])
        rhsb = [self.aa("rhsb", 128, [390], BF16) for _ in range(2)]
        for b in rhsb:
            self.cp("dve", b.ap[:, 0:384], P["wk"].ap, [P["wk"]], [b])
        cl8 = I["cl"].rearrange("l n (c a) r -> (l n c) (a r)", c=8)
        ck8 = I["ck"].rearrange("l n (c a) r -> (l n c) (a r)", c=8)
        latcs = [self.aa("latc", 128, [16, 128], BF16) for _ in range(2)]
        kpecs = [self.aa("kpec", 128, [16, 32], BF16) for _ in range(2)]
        latT4s = [self.aa("latT4", 128, [512], BF16) for _ in range(2)]
        kpeT4s = [self.aa("kpeT4", 128, [128], BF16) for _ in range(2)]
        ksqbs = [self.aa("ksqb", 128, [4, 384], BF16) for _ in range(2)]
        ssns = [self.aa("ssn", 128, [4, 6]) for _ in range(2)]
        scs = [self.aa("scs", 128, [4, 6]) for _ in range(2)]
        pbf = [self.aa("pbf", 128, [4, 6], BF16) for _ in range(2)]
        ssps = [self.aa("ssp", 128, [16]) for _ in range(2)]
        kq = self.aa("kqs", 128, [16, 32])
        oacc = self.PB[0]
        po = self.PB[1]
        PK = [self.PB[2], self.PB[3], self.PB[4], self.PB[5]]
        state = {"g": 0, "first": True}

        def front(G):
            gi = state["g"]
            state["g"] += 1
            G["gi"] = gi
            s, nt = G["s"], G["nt"]
            if G.get("pre") is not None:
                G["pre"]()
            rb = rhsb[s % 2]
            for i in range(nt):
                pk = PK[i]
                lh, lbufs = G["lat_lhsT"](i)
                kp_ap, kp_buf = G["kpeT4"]()
                self.mm(pk.ap[:, 384:390], kp_ap, qpad.ap[:, s, G["kidx"](i), :], [kp_buf, qpad], [pk], start=True, stop=False, sgc=True)
                self.mm(pk.ap[:, 0:390], lh, rb.ap, lbufs + [rb], [pk], start=False, stop=True, sgc=True)
            kv = self.ps_t[:, 2 * 512:(2 + nt) * 512].rearrange("p (a b) -> p a b", a=nt)
            ksqb, sc = ksqbs[gi % 2], scs[gi % 2]
            self.act(ksqb.ap[:, 0:nt, :], kv[:, :, 0:384], AF.Square, PK[0:nt], [ksqb])
            self.cp("act", sc.ap[:, 0:nt, :], kv[:, :, 384:390], PK[0:nt], [sc])

        def tail(G):
            gi = G["gi"]
            s, nt = G["s"], G["nt"]
            ksqb, sc, ssn = ksqbs[gi % 2], scs[gi % 2], ssns[gi % 2]
            ssp_ap, ssp_buf = G["ssp"]
            self.red(ssn.ap[:, 0:nt, :].rearrange("p a h -> p (a h)"), ksqb.ap[:, 0:nt, :].rearrange("p a (h d) -> p (a h) d", h=6), ALU.add, [ksqb], [ssn])
            self.tt("dve", ssn.ap[:, 0:nt, :], ssn.ap[:, 0:nt, :], ssp_ap.unsqueeze(2).to_broadcast([128, nt, 6]), ALU.add, [ssn, ssp_buf], [ssn])
            self.rsqrt(ssn.ap[:, 0:nt, :], ssn.ap[:, 0:nt, :], 1.0 / 96.0, [ssn], [ssn])
            self.tt("dve", sc.ap[:, 0:nt, :], sc.ap[:, 0:nt, :], ssn.ap[:, 0:nt, :], ALU.mult, [sc, ssn], [sc])
            pb_ = pbf[gi % 2]
            self.act(pb_.ap[:, 0:nt, :], sc.ap[:, 0:nt, :], AF.Exp, [sc, P["negM"]], [pb_], scale=ATTN_SCALE, bias=P["negM"].ap[:, 0:1])
            onehot = G.get("onehot")
            if onehot is not None:
                self.ts("dve", pb_.ap[:, 0:nt, :], pb_.ap[:, 0:nt, :], onehot, None, ALU.mult, None, [pb_, self.identf], [pb_])
            for i in range(nt):
                rh, rbuf = G["pv_rhs"](i)
                self.mm(oacc.ap[0:6, 0:128], pb_.ap[:, i, :], rh, [pb_, rbuf], [oacc], start=state["first"], stop=False, sgc=True)
                state["first"] = False
                self.mm(oacc.ap[0:6, 128:130], pb_.ap[:, i, :], self.onesb.ap[:, 0:2], [pb_, self.onesb], [oacc], start=False,
                        stop=(onehot is not None and i == nt - 1), sgc=True)

        ci_ = 0
        for s in range(NS):
            rb = rhsb[s % 2]
            self.cp("dve", rb.ap[:, 384:390], qabs.ap[:, s, :], [qabs], [rb])
            state["first"] = True
            groups = []
            for c in range(8):
                latc, kpec, ssp = latcs[ci_ % 2], kpecs[ci_ % 2], ssps[ci_ % 2]
                sl = ci_ % 2
                ci_ += 1

                def load_chunk(latc=latc, kpec=kpec, ssp=ssp, s=s, c=c, sl=sl):
                    self.op("pool", lambda e: e.indirect_dma_start(
                        out=latc.ap.rearrange("p a r -> p (a r)"), out_offset=None, in_=cl8[:, :],
                        in_offset=bass.IndirectOffsetOnAxis(ap=self.idx8.ap[:, l, s, c:c + 1], axis=0)), [self.idx8], [latc], "latc%d" % sl)
                    self.op("pool", lambda e: e.indirect_dma_start(
                        out=kpec.ap.rearrange("p a r -> p (a r)"), out_offset=None, in_=ck8[:, :],
                        in_offset=bass.IndirectOffsetOnAxis(ap=self.idx8.ap[:, l, s, c:c + 1], axis=0)), [self.idx8], [kpec], "kpec%d" % sl)
                    self.tt("pool", kq.ap, kpec.ap, kpec.ap, ALU.mult, [kpec], [kq])
                    self.red(ssp.ap, kq.ap, ALU.add, [kq], [ssp])
                for g in range(4):
                    def pre(latc=latc, kpec=kpec, g=g, first_in_chunk=(g == 0), load=load_chunk, G_holder=[]):
                        if first_in_chunk:
                            load()
                    G = dict(s=s, nt=4, kidx=lambda i: i, onehot=None)
                    G["ssp"] = (ssp.ap[:, 4 * g:4 * g + 4], ssp)

                    def mk(latc=latc, kpec=kpec, g=g, G=G, first_in_chunk=(g == 0), load=load_chunk):
                        def pre():
                            if first_in_chunk:
                                load()
                            gi = state["g"]
                            latT4, kpeT4 = latT4s[gi % 2], kpeT4s[gi % 2]
                            tb = self.PB[6]
                            tv = tb.ap.bitcast(BF16)
                            for i in range(4):
                                self.tr(tv[:, i * 128:(i + 1) * 128], latc.ap[:, 4 * g + i, :], self.identb.ap, [latc, self.identb], [tb])
                            self.cp("act", latT4.ap, tv[:, 0:512], [tb], [latT4])
                            tb2 = self.PB[7]
                            tv2 = tb2.ap.bitcast(BF16)
                            self.tr(tv2[:, 0:128], kpec.ap[:, 4 * g:4 * g + 4, :].rearrange("p a r -> p (a r)"), self.identb.ap, [kpec, self.identb], [tb2])
                            self.cp("dve", kpeT4.ap, tv2[:, 0:128], [tb2], [kpeT4])
                            G["lat_lhsT"] = lambda i, latT4=latT4: (latT4.ap[:, i * 128:(i + 1) * 128], [latT4])
                            G["kpeT4"] = lambda kpeT4=kpeT4: (kpeT4.ap, kpeT4)
                        return pre
                    G["pre"] = mk()
                    G["pv_rhs"] = lambda i, latc=latc, g=g: (latc.ap[:, 4 * g + i, :], latc)
                    groups.append(G)
            Gn = dict(s=s, nt=1, kidx=lambda i: 0, onehot=self.identf.ap[:, s:s + 1], pre=None)
            Gn["lat_lhsT"] = lambda i: (latnewT.ap, [latnewT])
            Gn["kpeT4"] = lambda: (kpenewT4.ap, kpenewT4)
            Gn["ssp"] = (sspnew.ap[:, 0:1], sspnew)
            Gn["pv_rhs"] = lambda i: (latnew_tm.ap, latnew_tm)
            groups.append(Gn)
            front(groups[0])
            for gi_ in range(len(groups)):
                if gi_ + 1 < len(groups):
                    front(groups[gi_ + 1])
                tail(groups[gi_])
            m1 = self.amark()
            oa = self.aa("oa", 6, [129])
            self.cp("act", oa.ap, oacc.ap[0:6, 0:129], [oacc], [oa])
            rinv = self.aa("orinv", 6, [1])
            self.recip(rinv.ap, oa.ap[:, 128:129], [oa], [rinv])
            ol = self.aa("ol", 6, [128])
            self.ts("dve", ol.ap, oa.ap[:, 0:128], rinv.ap[:, 0:1], None, ALU.mult, None, [oa, rinv], [ol])
            tb = self.PB[6]
            self.tr(tb.ap[:, 0:6], ol.ap, self.identf.ap[0:6, 0:6], [ol, self.identf], [tb])
            olT = self.aa("olT", 128, [8], BF16)
            self.memset("pool", olT.ap, 0.0, [olT])
            self.cp("act", olT.ap[:, 0:6], tb.ap[:, 0:6], [tb], [olT])
            for h in range(6):
                self.mm(po.ap[0:64, h * 18 + s:h * 18 + s + 2], P["wv"].ap[:, h * 64:(h + 1) * 64], olT.ap[:, h:h + 2], [P["wv"], olT], [po], start=True, stop=True, sgc=True)
            self.arelease(m1)
        of = self.aa("of", 64, [6, nb])
        self.cp("dve", of.ap, po.ap[0:64, 0:108].rearrange("p (h s) -> p h s", h=6)[:, :, 0:16], [po], [of])
        sq = self.aa("osq", 64, [6, nb], BF16)
        self.act(sq.ap, of.ap, AF.Square, [of], [sq])
        ps = self.bank("s")
        for h in range(6):
            self.mm(ps.ap[0:64, 0:nb], self.onesb.ap[0:64, 0:64], sq.ap[:, h, :], [self.onesb, sq], [ps], start=(h == 0), stop=(h == 5))
        rs = self.aa("ors", 64, [nb])
        self.rsqrt(rs.ap, ps.ap[0:64, 0:nb], 1.0 / 384.0, [ps], [rs])
        self.tt("dve", of.ap, of.ap, rs.ap.unsqueeze(1).to_broadcast([64, 6, nb]), ALU.mult, [of, rs], [of])
        self.tt("dve", mixm.ap, of.ap, P["gmla_fm"].ap.unsqueeze(2).to_broadcast([64, 6, nb]), ALU.mult, [of, P["gmla_fm"]], [mixm])
        self.arelease(m0)

    def sample_block(self):
        I, O = self.I, self.O
        nb = NS
        m0 = self.amark()
        xT = self.aa("xTs", 128, [8, nb])
        hT = self.aa("hTs", 128, [8, nb], BF16)
        xin = self.aa("xins", 16, [D])
        self.dma("sp", xin.ap, I["xs"], (), [xin], "xins")
        for c in range(8):
            tb = self.bank("t")
            self.tr(tb.ap[:, 0:nb], xin.ap[:, c * 128:(c + 1) * 128], self.identf.ap[0:16, 0:16], [xin, self.identf], [tb])
            self.cp("act", xT.ap[:, c, :], tb.ap[:, 0:nb], [tb], [xT])
        cos, sin = self.rope_tables(PAST, 0, nb)
        for l in range(DEPTH):
            P = self.L[l]
            m1 = self.amark()
            mixs = self.aa("mixs", 64, [6, nb], BF16)
            mixm = self.aa("mixm", 64, [6, nb], BF16)
            mixp = self.aa("mixp", 128, [2, nb], BF16)
            self.norm_mod(xT, hT, P["A1"], P["mod"], 0, nb, True)
            self.ssd_sample(l, hT, mixs)
            self.mla_sample(l, hT, cos, sin, mixm)
            self.pool_sample(l, hT, mixp)
            self.out_proj(l, xT, mixs, mixm, mixp, nb, True)
            self.norm_mod(xT, hT, P["A2"], P["mod"], 24, nb, True)
            self.ffn(l, xT, hT, nb, True)
            self.arelease(m1)
        yo = self.aa("youts", 16, [D])
        for c in range(8):
            self.fm2tm(yo.ap[:, c * 128:(c + 1) * 128], yo, xT.ap[:, c, :], xT, 128, 16)
        self.store(O["y_s"], yo.ap, [yo], "st_ys")
        self.arelease(m0)

    def build(self, do_prompt=True, do_sample=True, nblk=NBLK, arena_bytes=78 * 1024):
        self.declare()
        self.arena_init(arena_bytes)
        try:
            self.setup()
            if do_prompt:
                for blk in range(nblk):
                    self.prompt_block(blk)
            if do_sample:
                self.sample_block()
        except Builder._Stop:
            pass
        self.S.ins["sp"].append(dict(fn=None, deps=list(self.out_events), signal=False, dma=None))
        self.S.finalize()
        nc = self.nc
        esems = {e: self.es.enter_context(nc.semaphore("sem_" + e)) for e in ENGS}
        dsems = {v[0]: self.es.enter_context(nc.semaphore("dsem_%d" % v[0])) for k, v in self.S.dma_slots.items()}
        S = self.S
        with nc.Block() as block:
            @block.sync
            def _(eng):
                S.run_engine("sp", eng, esems, dsems)

            @block.scalar
            def _(eng):
                S.run_engine("act", eng, esems, dsems)

            @block.vector
            def _(eng):
                S.run_engine("dve", eng, esems, dsems)

            @block.gpsimd
            def _(eng):
                S.run_engine("pool", eng, esems, dsems)

            @block.tensor
            def _(eng):
                S.run_engine("pe", eng, esems, dsems)
        self.es.close()
        return nc


def host_consts():
    kc = np.zeros((128, 8), np.float32)
    half = 16
    inv = (1.0 / (np.float32(10000.0) ** (np.arange(half, dtype=np.float32) / np.float32(half)))).astype(np.float32)
    p = np.arange(128)
    kc[:, 0] = inv[p % 16]
    kc[:, 1] = np.where((p % 32) < 16, -1.0, 1.0)
    kc[:, 2] = np.where(p < 64, 2.0, 4.0)
    kc[:, 3] = np.where(p < 64, 8.0, 16.0)
    kc[:, 4] = 1.0 / kc[:, 2]
    kc[:, 5] = 1.0 / kc[:, 3]
    return kc


WEIGHTS = ["w_ada", "b_ada", "g_norm1", "w_in", "conv_w", "conv_b", "dt_bias", "a_log", "d_skip", "g_ssd_norm", "g_q_lora", "w_q_up",
           "g_kv_lora", "w_k_up", "w_v_up", "g_qk_q", "g_qk_k", "g_mla_out", "w_pool", "pool_scale", "w_out", "g_norm2", "w_gate", "w_up", "w_down"]


def make_in_map(inp, c):
    f = lambda a: np.ascontiguousarray(np.asarray(a))
    m = {
        "xp": f(inp["x_prompt"][c]),
        "xs": f(inp["x_sample"][NS * c:NS * (c + 1), 0]),
        "cl": inp["cache_kv_latent"],
        "ck": inp["cache_k_rope"],
        "sssm": f(inp["state_ssm"][:, NS * c:NS * (c + 1)]),
        "sconv": f(inp["state_conv"][:, NS * c:NS * (c + 1)]),
        "spool": f(inp["state_pool"][:, NS * c:NS * (c + 1)]),
        "pt": f(inp["page_table"][NS * c:NS * (c + 1)]).astype(np.int32, copy=False),
        "cp": f(inp["c_prompt"][c:c + 1]),
        "cs": f(inp["c_sample"][NS * c:NS * (c + 1)]),
        "kconst": host_consts(),
    }
    for w in WEIGHTS:
        m[w] = inp[w]
    return m


_CACHE = {}


def kernel(**inp):
    inp = {k: np.asarray(v) for k, v in inp.items()}
    n_pool = inp["cache_kv_latent"].shape[1]
    ncores = inp["x_prompt"].shape[0]
    if n_pool not in _CACHE:
        _CACHE[n_pool] = Builder(n_pool).build()
    nc = _CACHE[n_pool]
    in_maps = [make_in_map(inp, c) for c in range(ncores)]
    res = run_bass_kernel_spmd(nc, in_maps, core_ids=list(range(ncores)))
    R = res.results
    cat = lambda k, ax: np.concatenate([np.asarray(r[k]) for r in R], axis=ax)
    y_p = np.stack([np.asarray(r["y_p"]) for r in R], 0)
    y_s = cat("y_s", 0)[:, None, :]
    p_lat = np.stack([np.asarray(r["p_lat"]) for r in R], 1)
    p_kpe = np.stack([np.asarray(r["p_kpe"]) for r in R], 1)
    p_ssm = np.stack([np.asarray(r["p_ssm"]) for r in R], 1)
    p_conv = np.stack([np.asarray(r["p_conv"]) for r in R], 1)
    p_pool = np.stack([np.asarray(r["p_pool"]) for r in R], 1)
    s_lat = cat("s_lat", 1)[:, :, None, :]
    s_kpe = cat("s_kpe", 1)[:, :, None, :]
    s_ssm = cat("s_ssm", 1)
    s_conv = cat("s_conv", 1)
    s_pool = cat("s_pool", 1)
    return (y_p, y_s, p_lat, p_kpe, p_ssm, p_conv, p_pool, s_lat, s_kpe, s_ssm, s_conv, s_pool)
```
